# Optimizing a Trainium2 kernel written in Bass

```python
import math
import jax, jax.numpy as jnp
from jax import lax
import numpy as np

D_MODEL = 1024
BATCH = 8
SEQ = 4096
DEPTH = 2

CHUNK = 64
D_MIX = D_MODEL
HEAD_DIM = 64
GROUP_W = D_MIX // 4
A_GROUPS = GROUP_W // HEAD_DIM
A_BLOCK = 128
B_HEADS = GROUP_W // HEAD_DIM
B_PREV_CHUNKS = 8
B_BAND = (B_PREV_CHUNKS + 1) * CHUNK
REL_CLIP = 128
C_HEADS = GROUP_W // HEAD_DIM
IDX_HEADS = 8
IDX_DIM = 64
TOPK_MAX = 256
Q_BLOCK = 128
T5_BUCKETS = 32
T5_MAX_DIST = 128
N_MEM = 256
M_HEADS = GROUP_W // HEAD_DIM
DEEPNORM_ALPHA = (2 * DEPTH) ** 0.25
DEEPNORM_BETA = (8 * DEPTH) ** -0.25
LN_EPS = 1e-5
SPLITS = (GROUP_W, GROUP_W, GROUP_W,
          GROUP_W, GROUP_W, GROUP_W, GROUP_W,
          GROUP_W, GROUP_W, GROUP_W, GROUP_W,
          IDX_HEADS * IDX_DIM, IDX_DIM, IDX_HEADS,
          GROUP_W, GROUP_W)
D_IN = sum(SPLITS)

kernel_name = "hybrid_streaming_gmlp_band_dsa_mem"


def layer_norm(x, g, b):
    xf = x.astype(jnp.float32)
    mu = jnp.mean(xf, axis=-1, keepdims=True)
    var = jnp.mean(jnp.square(xf - mu), axis=-1, keepdims=True)
    y = (xf - mu) * lax.rsqrt(var + LN_EPS)
    return (y * g.astype(jnp.float32) + b.astype(jnp.float32)).astype(x.dtype)


def softmax_f32(s, dtype):
    return jax.nn.softmax(s.astype(jnp.float32), axis=-1).astype(dtype)


def spatial_gating(u, v, ln_g, ln_b, w_s, b_s):
    bsz, seq, _ = v.shape
    nb = seq // A_BLOCK
    v = layer_norm(v, ln_g, ln_b)
    cpos = jnp.arange(A_BLOCK) // CHUNK
    mask = cpos[None, :] <= cpos[:, None]
    w = jnp.where(mask[None], w_s, 0.0).astype(v.dtype)
    vb = v.reshape(bsz, nb, A_BLOCK, A_GROUPS, HEAD_DIM)
    mixed = jnp.einsum('gij,bnjgc->bnigc', w, vb) + b_s.T.astype(v.dtype)[None, None, :, :, None]
    return u * mixed.reshape(bsz, seq, GROUP_W)


def chunk_band_attention(q, k, v, rel_bias):
    bsz, seq, h, dh = q.shape
    nc = seq // CHUNK
    qc = q.reshape(bsz, nc, CHUNK, h, dh)
    pad = ((0, 0), (B_PREV_CHUNKS, 0), (0, 0), (0, 0), (0, 0))
    kp = jnp.pad(k.reshape(bsz, nc, CHUNK, h, dh), pad)
    vp = jnp.pad(v.reshape(bsz, nc, CHUNK, h, dh), pad)
    kb = jnp.concatenate([kp[:, o:o + nc] for o in range(B_PREV_CHUNKS + 1)], axis=2)
    vb = jnp.concatenate([vp[:, o:o + nc] for o in range(B_PREV_CHUNKS + 1)], axis=2)
    s = jnp.einsum('bcqhd,bckhd->bchqk', qc, kb).astype(jnp.float32) * (dh ** -0.5)
    qi = jnp.arange(CHUNK)
    kk = jnp.arange(B_BAND)
    rel = qi[:, None] + B_PREV_CHUNKS * CHUNK - kk[None, :]
    rel_idx = jnp.clip(rel, -REL_CLIP, REL_CLIP) + REL_CLIP
    bias = rel_bias[:, rel_idx].astype(jnp.float32)
    key_chunk = jnp.arange(nc)[:, None] - B_PREV_CHUNKS + kk[None, :] // CHUNK
    valid = key_chunk >= 0
    s = jnp.where(valid[None, :, None, None, :], s + bias[None, None], -jnp.inf)
    p = softmax_f32(s, v.dtype)
    o = jnp.einsum('bchqk,bckhd->bcqhd', p, vb)
    return o.reshape(bsz, seq, h * dh)


def t5_bucket(rel):
    nb = T5_BUCKETS // 2
    max_exact = nb // 2
    ret = jnp.where(rel > 0, nb, 0)
    n = jnp.abs(rel)
    nf = jnp.maximum(n, 1).astype(jnp.float32)
    large = max_exact + (jnp.log(nf / max_exact) / math.log(T5_MAX_DIST / max_exact)
                         * (nb - max_exact)).astype(jnp.int32)
    large = jnp.minimum(large, nb - 1)
    return ret + jnp.where(n < max_exact, n, large)


def dsa_attention(q, k, v, iq, ik, iw, t5_table):
    bsz, seq, h, dh = q.shape
    k_sel = min(TOPK_MAX, seq // 4)
    nqb = seq // Q_BLOCK

    def to_blocks(a):
        return jnp.moveaxis(a.reshape((bsz, nqb, Q_BLOCK) + a.shape[2:]), 1, 0)

    key_chunk = jnp.arange(seq) // CHUNK
    ikf = ik.astype(jnp.float32)

    def block(args):
        qb, iqb, iwb, qpos = args
        qchunk = qpos // CHUNK
        logits = jnp.einsum('bqhd,bsd->bqhs', iqb.astype(jnp.float32), ikf) * (IDX_DIM ** -0.5)
        score = jnp.einsum('bqhs,bqh->bqs', jax.nn.relu(logits),
                           iwb.astype(jnp.float32) * (IDX_HEADS ** -0.5))
        admissible = key_chunk[None, :] <= qchunk[:, None]
        score = jnp.where(admissible[None], score, -jnp.inf)
        _, idx = lax.top_k(score, k_sel)
        kg = jax.vmap(lambda kk, ii: kk[ii])(k, idx)
        vg = jax.vmap(lambda vv, ii: vv[ii])(v, idx)
        s = jnp.einsum('bqhd,bqkhd->bqhk', qb, kg).astype(jnp.float32) * (dh ** -0.5)
        bias = t5_table[t5_bucket(idx - qpos[None, :, None])]
        s = s + jnp.moveaxis(bias, -1, 2).astype(jnp.float32)
        valid = (idx // CHUNK) <= qchunk[None, :, None]
        s = jnp.where(valid[:, :, None, :], s, -jnp.inf)
        p = softmax_f32(s, v.dtype)
        return jnp.einsum('bqhk,bqkhd->bqhd', p, vg)

    out = lax.map(block, (to_blocks(q), to_blocks(iq), to_blocks(iw),
                          jnp.arange(seq).reshape(nqb, Q_BLOCK)))
    return jnp.moveaxis(out, 0, 1).reshape(bsz, seq, h * dh)


def memory_attention(q, km, vm):
    bsz, seq, h, dh = q.shape
    s = jnp.einsum('bshd,bmhd->bhsm', q, km).astype(jnp.float32) * (dh ** -0.5)
    p = softmax_f32(s, vm.dtype)
    return jnp.einsum('bhsm,bmhd->bshd', p, vm).reshape(bsz, seq, h * dh)


def hybrid_layer(x, mem, w_in, b_in, a_ln_g, a_ln_b, a_ws, a_bs, b_rel, t5_table,
                 w_mem_kv, w_out, b_out, ln_g, ln_b):
    bsz, seq, _ = x.shape
    n_mem = mem.shape[1]
    proj = x @ w_in + b_in
    split_points = np.cumsum(SPLITS)[:-1].tolist()
    (a_u, a_v, a_g, bq, bk, bv, bg, cq, ck, cv, cg,
     iq, ik, iw, mq, mg) = jnp.split(proj, split_points, axis=-1)

    def heads(t):
        return t.reshape(t.shape[0], t.shape[1], -1, HEAD_DIM)

    ya = spatial_gating(jax.nn.gelu(a_u), jax.nn.gelu(a_v), a_ln_g, a_ln_b, a_ws, a_bs)
    yb = chunk_band_attention(heads(bq), heads(bk), heads(bv), b_rel)
    yc = dsa_attention(heads(cq), heads(ck), heads(cv),
                       iq.reshape(bsz, seq, IDX_HEADS, IDX_DIM), ik, iw, t5_table)
    km, vm = jnp.split(mem @ w_mem_kv, 2, axis=-1)
    ym = memory_attention(heads(mq), km.reshape(bsz, n_mem, M_HEADS, HEAD_DIM),
                          vm.reshape(bsz, n_mem, M_HEADS, HEAD_DIM))
    mixed = jnp.concatenate([ya * jax.nn.silu(a_g), yb * jax.nn.silu(bg),
                             yc * jax.nn.silu(cg), ym * jax.nn.silu(mg)], axis=-1)
    y = mixed @ w_out + b_out
    return layer_norm(DEEPNORM_ALPHA * x + y, ln_g, ln_b)


def setup_inputs(seed: int = 0) -> dict:
    key = jax.random.key(seed)
    ks = jax.random.split(key, 17)
    f32 = jnp.float32

    def nrm(k, shape, scale):
        return jax.random.normal(k, shape, f32) * scale

    return {
        "x": nrm(ks[0], (BATCH, SEQ, D_MODEL), 1.0),
        "mem": nrm(ks[1], (BATCH, N_MEM, D_MODEL), 1.0),
        "ln_in_g": 1.0 + nrm(ks[2], (D_MODEL,), 0.02),
        "ln_in_b": nrm(ks[3], (D_MODEL,), 0.02),
        "w_in": nrm(ks[4], (DEPTH, D_MODEL, D_IN), D_MODEL ** -0.5),
        "b_in": nrm(ks[5], (DEPTH, D_IN), 0.02),
        "a_ln_g": 1.0 + nrm(ks[6], (DEPTH, GROUP_W), 0.02),
        "a_ln_b": nrm(ks[7], (DEPTH, GROUP_W), 0.02),
        "a_ws": nrm(ks[8], (DEPTH, A_GROUPS, A_BLOCK, A_BLOCK), A_BLOCK ** -0.5),
        "a_bs": 1.0 + nrm(ks[9], (DEPTH, A_GROUPS, A_BLOCK), 0.02),
        "b_rel": nrm(ks[10], (DEPTH, B_HEADS, 2 * REL_CLIP + 1), 0.1),
        "t5_table": nrm(ks[11], (T5_BUCKETS, C_HEADS), 0.1),
        "w_mem_kv": nrm(ks[12], (DEPTH, D_MODEL, 2 * GROUP_W), D_MODEL ** -0.5),
        "w_out": nrm(ks[13], (DEPTH, D_MIX, D_MODEL), D_MIX ** -0.5 * DEEPNORM_BETA),
        "b_out": nrm(ks[14], (DEPTH, D_MODEL), 0.02),
        "ln_g": 1.0 + nrm(ks[15], (DEPTH, D_MODEL), 0.02),
        "ln_b": nrm(ks[16], (DEPTH, D_MODEL), 0.02),
    }


def reference(x, mem, ln_in_g, ln_in_b, w_in, b_in, a_ln_g, a_ln_b, a_ws, a_bs, b_rel,
              t5_table, w_mem_kv, w_out, b_out, ln_g, ln_b):
    h = layer_norm(x, ln_in_g, ln_in_b)
    for l in range(DEPTH):
        h = hybrid_layer(h, mem, w_in[l], b_in[l], a_ln_g[l], a_ln_b[l], a_ws[l], a_bs[l],
                         b_rel[l], t5_table, w_mem_kv[l], w_out[l], b_out[l], ln_g[l], ln_b[l])
    return h
```

```python
import numpy as np
import concourse.bass as bass
import concourse.mybir as mybir
from concourse.bass_utils import run_bass_kernel_spmd
from contextlib import ExitStack

F32 = mybir.dt.float32
BF16 = mybir.dt.bfloat16
U8 = mybir.dt.uint8
ALU = mybir.AluOpType
AF = mybir.ActivationFunctionType
AX = mybir.AxisListType

D_MODEL = 1024
BATCH = 8
SEQ = 4096
DEPTH = 2
NBLK = SEQ // 128
GW = 256
SPLITS = (GW, GW, GW, GW, GW, GW, GW, GW, GW, GW, GW, 512, 64, 8, GW, GW)
NAMES = ("a_u", "a_v", "a_g", "bq", "bk", "bv", "bg", "cq", "ck", "cv", "cg", "iq", "ik", "iw", "mq", "mg")
TM_ORDER = ("a_u", "a_v", "a_g", "bg", "cg", "mg", "bv", "cv", "iw")
FM_ORDER = ("bq", "bk", "cq", "ck", "mq", "iq", "ik", "ik")
NTM = 2056
NFM = 1920
ALPHA = (2 * DEPTH) ** 0.25
LN_EPS = 1e-5
NBIS = 16
DVE_FRAC = 0.35
W_IDX = 0.12
SMALL_ENG = "pool"
NEG = -30000.0

ENGS = ("pe", "act", "dve", "pool", "sp")
STRICT_WAR = False


class Buf:
    __slots__ = ("name", "writer", "readers", "psum")

    def __init__(self, name="", psum=False):
        self.name = name
        self.writer = None
        self.readers = {}
        self.psum = psum


class DmaSem:
    def __init__(self, sem):
        self.sem = sem
        self.count = 0


class Sched:
    EPOCH = 12000

    def __init__(self, nc, es):
        self.nc = nc
        self.es = es
        self.prog = {e: [] for e in ENGS}
        self.cnt = {e: 0 for e in ENGS}
        self.nsem = 0
        self.cursem = {e: self._newsem(e) for e in ENGS}
        self.waited = {e: {} for e in ENGS}
        self.semobj = {}

    def _newsem(self, name):
        self.nsem += 1
        return self.es.enter_context(self.nc.semaphore(f"s_{name}_{self.nsem}"))

    def dma_sem(self, name):
        return DmaSem(self._newsem("d" + name))

    def _deps(self, eng, reads, writes):
        need = {}

        def add(tok, kind):
            if tok is None:
                return
            sem, val, teng = tok
            if teng == eng:
                if eng == "pe" or (kind == "war" and not STRICT_WAR):
                    return
            k = id(sem)
            self.semobj[k] = sem
            if need.get(k, 0) < val:
                need[k] = val

        for b in reads:
            add(b.writer, "raw")
            if b.psum:
                for t in b.readers.values():
                    if t[2] != eng:
                        add(t, "rar")
        for b in writes:
            add(b.writer, "waw")
            for t in b.readers.values():
                add(t, "war")
        waits = []
        w = self.waited[eng]
        for k, val in need.items():
            if w.get(k, 0) < val:
                w[k] = val
                waits.append((self.semobj[k], val))
        return waits

    def _mark(self, tok, reads, writes):
        k = id(tok[0])
        for b in reads:
            b.readers[k] = tok
        for b in writes:
            b.writer = tok
            b.readers = {}

    def op(self, eng, fn, reads=(), writes=()):
        waits = self._deps(eng, reads, writes)
        if self.cnt[eng] >= self.EPOCH:
            self.cursem[eng] = self._newsem(eng)
            self.cnt[eng] = 0
        self.cnt[eng] += 1
        tok = (self.cursem[eng], self.cnt[eng], eng)
        self.prog[eng].append((waits, fn, (self.cursem[eng], 1)))
        self._mark(tok, reads, writes)
        return tok

    def dma(self, eng, ds, fn, reads=(), writes=()):
        waits = self._deps(eng, reads, writes)
        ds.count += 16
        tok = (ds.sem, ds.count, "dma")
        self.prog[eng].append((waits, fn, (ds.sem, 16)))
        self._mark(tok, reads, writes)
        return tok

    def wait_tokens(self, eng, toks):
        self.prog[eng].append(([(t[0], t[1]) for t in toks], None, None))

    def emit(self):
        nc = self.nc
        engobj = {"pe": nc.tensor, "act": nc.scalar, "dve": nc.vector, "pool": nc.gpsimd, "sp": nc.sync}

        def replay(e):
            eo = engobj[e]
            for waits, fn, inc in self.prog[e]:
                for sem, val in waits:
                    eo.wait_ge(sem, val)
                if fn is not None:
                    fn().then_inc(inc[0], inc[1])

        with nc.Block() as block:
            @block.tensor
            def _(x):
                replay("pe")

            @block.scalar
            def _(x):
                replay("act")

            @block.vector
            def _(x):
                replay("dve")

            @block.gpsimd
            def _(x):
                replay("pool")

            @block.sync
            def _(x):
                replay("sp")


def build_program(n_layers=DEPTH, nblk=NBLK, pipeline=True):
    nc = bass.Bass("TRN2", target_bir_lowering=False)
    es = ExitStack()
    S = Sched(nc, es)

    def din(name, shape):
        return nc.dram_tensor(name, list(shape), F32, kind="ExternalInput").ap()

    x_d = din("x", [SEQ, D_MODEL])
    mem_d = din("mem", [256, D_MODEL])
    lnin_d = din("lnin", [2, 128, 1024])
    wtm_d = din("wtm", [DEPTH, 128, 8 * NTM])
    wfm_d = din("wfm", [DEPTH, 128, 8 * NFM])
    wout_d = din("wout", [DEPTH, 128, 8 * 1024])
    wmk_d = din("wmk", [DEPTH, 128, 8 * 256])
    wmv_d = din("wmv", [DEPTH, 128, 8 * 256])
    brow_d = din("brow", [DEPTH, 128, 512])
    sel_d = din("sel", [128, 7 * 128])
    bfm_d = din("bfm", [DEPTH, 128, 15])
    lng_d = din("lng", [DEPTH, 2, 128, 1024])
    alng_d = din("alng", [DEPTH, 2, 128, 256])
    wsT_d = din("wsT", [DEPTH, 128, 512])
    absT_d = din("absT", [DEPTH, 128, 128])
    biasB_d = din("biasB", [DEPTH, 2, 128, 512])
    cB_d = din("cB", [DEPTH, 128, 512])
    biasC_d = din("biasC", [2, 128, 512])
    cC_d = din("cC", [128, 512])
    maskB_d = din("maskB", [2, 128, 512])
    E_d = din("Emat", [128, 256])
    ident4_d = din("ident4", [128, 512])
    pow2_d = din("pow2", [128, NBIS + 2])
    out_d = nc.dram_tensor("out", [SEQ, D_MODEL], F32, kind="ExternalOutput").ap()
    b_o = [Buf() for _ in range(NBLK)]

    def sb(name, shape, dt):
        return es.enter_context(nc.sbuf_tensor(name, list(shape), dt))

    def ps(name, shape, dt):
        return es.enter_context(nc.psum_tensor(name, list(shape), dt))

    wtm = sb("wtm_s", [128, 8 * NTM], BF16); b_wtm = Buf()
    wfm = sb("wfm_s", [128, 8 * NFM], BF16); b_wfm = Buf()
    wout = sb("wout_s", [128, 8 * 1024], BF16); b_wout = Buf()
    brow = sb("brow_s", [128, 512], BF16); b_brow = Buf()
    sel = sb("sel_s", [128, 7 * 128], BF16); b_sel = Buf()
    bfm = sb("bfm_s", [128, 15], F32); b_bfm = Buf()
    bfm8 = sb("bfm8_s", [128, 15], F32); b_bfm8 = Buf()
    lng = sb("lng_s", [128, 1024], F32); b_lng = Buf()
    lnb = sb("lnb_s", [128, 1024], F32); b_lnb = Buf()
    alng = sb("alng_s", [128, 256], F32); b_alng = Buf()
    alnb = sb("alnb_s", [128, 256], F32); b_alnb = Buf()
    wsT = sb("wsT_s", [128, 512], BF16); b_wsT = Buf()
    absT = sb("absT_s", [128, 128], BF16); b_absT = Buf()
    Emat = sb("E_s", [128, 256], BF16); b_E = Buf()
    ident4 = sb("ident4_s", [128, 512], BF16); b_id = Buf()
    pow2 = sb("pow2_s", [128, NBIS + 2], F32); b_pow2 = Buf()
    biasB0 = sb("biasB0_s", [128, 512], BF16); biasB1 = sb("biasB1_s", [128, 512], BF16)
    maskB4 = sb("maskB4_s", [128, 512], BF16)
    biasC0 = sb("biasC0_s", [128, 512], BF16); biasC1 = sb("biasC1_s", [128, 512], BF16)
    b_biasB0, b_biasB1, b_maskB4, b_biasC0, b_biasC1 = Buf(), Buf(), Buf(), Buf(), Buf()

    kCT = sb("kCT_s", [128, 2, SEQ], BF16); b_kC = [Buf() for _ in range(NBLK)]
    vC = sb("vC_s", [128, NBLK, 4, 65], BF16); b_vC = [Buf() for _ in range(NBLK)]
    ikT = sb("ikT_s", [128, SEQ], BF16); b_ik = [Buf() for _ in range(NBLK)]
    kBT = sb("kBT_s", [128, 2, 5 * 128], BF16); b_kB = [Buf() for _ in range(5)]
    vB = sb("vB_s", [128, 5, 4, 65], BF16); b_vB = [Buf() for _ in range(5)]
    kmT = sb("kmT_s", [128, 2, 256], BF16); b_kmT = Buf()
    vM = sb("vM_s", [128, 2, 4, 65], BF16); b_vM = Buf()

    score = sb("score_s", [128, SEQ], F32); b_sc = [Buf(), Buf()]
    mb = sb("mb_s", [128, SEQ], BF16); b_mb = Buf()
    wmb = mb[:, 0:2048]; b_wmb = b_mb
    gsc0 = sb("gscr_s", [128, 512], F32); b_gs = Buf()
    rt = sb("rt_s", [128, 1024], F32); b_rt = [Buf(), Buf()]
    rtmp = [rt[:, 0:512], rt[:, 512:1024]]; b_rtmp = b_rt
    junk = rt[:, :].bitcast(U8)
    memT = rt[:, :].bitcast(BF16).rearrange("p (a b) -> p a b", b=256)
    cntD = sb("cntD_s", [128, 2], F32); b_cntD = Buf()
    cntA = sb("cntA_s", [128, 2], F32); b_cntA = Buf()
    xin = [sb(f"xin{k}_s", [128, 1024], F32) for k in range(2)]; b_x = [Buf(), Buf()]
    mixed = [sb(f"mixed{k}_s", [128, 1024], BF16) for k in range(2)]; b_mixed = [Buf(), Buf()]
    xnT = sb("xnT_s", [128, 8, 128], BF16); b_xnT = Buf()
    mixedT = sb("mixedT_s", [128, 8, 128], BF16); b_mixedT = Buf()
    xg = sb("xg_s", [128, 512], F32); b_xg = Buf()
    tmpb = xg; b_tmpb = b_xg
    gates = sb("gates_s", [128, 768], BF16); b_gates = Buf()
    gatesC = [sb(f"gatesC{k}_s", [128, 256], BF16) for k in range(2)]; b_gatesC = [Buf(), Buf()]
    vln = sb("vln_s", [128, 256], BF16); b_vln = Buf()
    qblk = {"B": sb("qblkB_s", [128, 2, 256], BF16), "M": sb("qblkM_s", [128, 2, 256], BF16)}
    b_qblk = {"B": Buf(), "M": Buf()}
    qblkC = [sb(f"qblkC{k}_s", [128, 2, 256], BF16) for k in range(2)]; b_qblkC = [Buf(), Buf()]
    iqblk = sb("iqblk_s", [128, 8, 128], BF16); b_iqblk = Buf()
    iw = sb("iw_s", [128, 8], F32); b_iw = Buf()
    PTA = [sb("PTA0_s", [128, 512], BF16)]; b_PTA = [Buf()]
    PTC = [sb(f"PTC{k}_s", [128, 512], BF16) for k in range(2)]; b_PTC = [Buf(), Buf()]
    stA = sb("stA_s", [128, 32], F32); b_stA = Buf()
    stB = sb("stB_s", [128, 32], F32); b_stB = Buf()
    recs = {m: sb(f"rec{m}_s", [128, 4], F32) for m in "BMC"}; b_recs = {m: Buf() for m in "BMC"}
    cst = sb("cst_s", [128, 4], F32); b_cst = Buf()
    bis = sb("bis_s", [128, 32], F32); b_bis = Buf()
    rk = sb("rk_s", [128, NBIS + 2], F32); b_rk = Buf()

    NA, NS, NC = 2, 2, 2
    bigA = [ps(f"bigA{k}", [128, 512], F32) for k in range(NA)]; b_bigA = [Buf(psum=True) for _ in range(NA)]
    bigS = [ps(f"bigS{k}", [128, 512], F32) for k in range(NS)]; b_bigS = [Buf(psum=True) for _ in range(NS)]
    bigC = [ps(f"bigC{k}", [128, 512], F32) for k in range(NC)]; b_bigC = [Buf(psum=True) for _ in range(NC)]
    _accS = ps("accS", [128, 512], F32); _baccS = Buf(psum=True)
    _accC = ps("accC", [128, 512], F32); _baccC = Buf(psum=True)
    acc = {"B": _accS, "M": _accS, "C": _accC}; b_acc = {"B": _baccS, "M": _baccS, "C": _baccC}
    rr = {"A": 0, "S": 0, "C": 0, "PTA": 0, "PTC": 0, "rtmp": 0, "cvt": 0, "stage": 0}

    def nxt(key, n):
        k = rr[key]
        rr[key] = (k + 1) % n
        return k

    def bankA():
        k = nxt("A", NA)
        return bigA[k], b_bigA[k]

    def bankS():
        k = nxt("S", NS)
        return bigS[k], b_bigS[k]

    def bankC():
        k = nxt("C", NC)
        return bigC[k], b_bigC[k]

    d_stage = [S.dma_sem("st0"), S.dma_sem("st1")]
    d_x = [S.dma_sem("x0"), S.dma_sem("x1")]
    d_o = [S.dma_sem("o0"), S.dma_sem("o1")]
    d_misc = {}

    def dsem(name):
        if name not in d_misc:
            d_misc[name] = S.dma_sem(name)
        return d_misc[name]

    V, A, G, T = nc.vector, nc.scalar, nc.gpsimd, nc.tensor
    ENG = {"dve": V, "act": A, "pool": G}

    cur_q = [None]

    def free_elems(ap):
        n = 1
        for d in tuple(ap.shape)[1:]:
            n *= int(d)
        return n

    def issue(eng, fn, reads, writes, dur, ds=None, holder=None):
        if cur_q[0] is not None:
            cur_q[0].append((eng, fn, list(reads), list(writes), dur, ds, holder))
            return None
        if ds is not None:
            tok = S.dma(eng, ds, fn, reads=reads, writes=writes)
            if holder is not None:
                holder[0][holder[1]] = tok
            return tok
        return S.op(eng, fn, reads=reads, writes=writes)

    def op(eng, name, reads, writes, *args, **kw):
        f = getattr(ENG[eng], name)
        o = kw.get("out", args[0] if args else None)
        n = free_elems(o) if o is not None else 1
        if eng == "dve":
            dur = 0.12 + n / 960.0 + (0.08 if "accum_out" in kw else 0.0)
        elif eng == "act":
            dur = 0.22 + n / 1200.0 + (0.1 if "accum_out" in kw else 0.0)
        else:
            dur = 0.3 + n / 480.0
        issue(eng, lambda: f(*args, **kw), reads, writes, dur)

    def mm(out, lhsT, rhs, start, reads, writes):
        dur = 0.1 + free_elems(rhs) / 1500.0
        issue("pe", lambda: T.matmul(out, lhsT, rhs, start=start, stop=True, skip_group_check=True), reads, writes, dur)

    def transpose(out, in_, reads, writes):
        idn = ident4[:, 0:128]
        issue("pe", lambda: T.transpose(out, in_, idn), reads + [b_id], writes, 0.2)

    def cast_copy(k, out, in_, reads, writes, scale=None):
        e = k % 3
        if scale is not None:
            if e == 1:
                op("act", "mul", reads, writes, out=out, in_=in_, mul=scale)
            else:
                op("dve" if e == 0 else "pool", "tensor_scalar", reads, writes, out=out, in0=in_, scalar1=scale, scalar2=None, op0=ALU.mult)
            return
        if e == 0:
            op("dve", "tensor_copy", reads, writes, out=out, in_=in_)
        elif e == 1:
            op("act", "copy", reads, writes, out=out, in_=in_)
        else:
            op("pool", "tensor_copy", reads, writes, out=out, in_=in_)

    def dma(ds, out, in_, reads, writes, holder=None):
        return issue("sp", lambda: nc.sync.dma_start(out=out, in_=in_), reads, writes, 3.0, ds=ds, holder=holder)

    def load_direct(name, dst, bdst, src):
        dma(dsem(name), dst, src, [], [bdst])

    def load_cvt(dst, bdst, src, L, post=None, scale=None):
        c0 = 0
        while c0 < L:
            n = min(2048, L - c0)
            h = nxt("stage", 2)
            stg = score[:, h * 2048:h * 2048 + n]
            dma(d_stage[h], stg, src[:, c0:c0 + n], [], [b_sc[h]])
            if post is None:
                cast_copy(nxt("cvt", 3), dst[:, c0:c0 + n], stg, [b_sc[h]], [bdst], scale=scale)
            else:
                post(dst[:, c0:c0 + n], stg, c0, n, h)
            c0 += n

    load_cvt(ident4, b_id, ident4_d, 512)
    load_cvt(Emat, b_E, E_d, 256)
    load_cvt(sel, b_sel, sel_d, 7 * 128)
    load_direct("pow2", pow2[:], b_pow2, pow2_d)
    op("dve", "memset", [], [b_cst], cst[:, 0:1], -0.5)
    load_direct("tmpb", tmpb[:], b_tmpb, cC_d)
    for d, (dstt, bd) in enumerate(((biasC0, b_biasC0), (biasC1, b_biasC1))):
        def post(dst, stg, c0, n, h, bd=bd):
            op("dve", "tensor_tensor", [b_sc[h], b_tmpb], [bd], out=dst, in0=stg, in1=tmpb[:, c0:c0 + n], op=ALU.subtract)
        load_cvt(dstt, bd, biasC_d[d], 512, post=post)
    load_cvt(maskB4, b_maskB4, maskB_d[1], 512)
    for m in "BM":
        op("pool", "memset", [], [b_qblk[m]], qblk[m][:], 0.0)
    for k in range(2):
        op("pool", "memset", [], [b_qblkC[k]], qblkC[k][:], 0.0)
    op("pool", "memset", [], [b_iqblk], iqblk[:], 0.0)
    op("pool", "memset", [], b_vC, vC[:, :, :, 64:65], 1.0)
    op("pool", "memset", [], b_vB, vB[:, :, :, 64:65], 1.0)
    op("pool", "memset", [], [b_vM], vM[:, :, :, 64:65], 1.0)

    def layer_norm(X, bX, width, gt, bg_, bt, bb_, stt, bst, out=None, bout=None):
        nch = (width + 511) // 512
        cw = width // nch
        for c in range(nch):
            op("dve", "bn_stats", [bX], [bst], out=stt[:, 8 + 6 * c:14 + 6 * c], in_=X[:, c * cw:(c + 1) * cw])
        op("dve", "bn_aggr", [bst], [bst], out=stt[:, 0:2], in_=stt[:, 8:8 + 6 * nch])
        op("dve", "tensor_scalar", [bst], [bst], out=stt[:, 2:3], in0=stt[:, 1:2], scalar1=LN_EPS, scalar2=None, op0=ALU.add)
        op("pool", "tensor_tensor", [bst, b_cst], [bst], out=stt[:, 3:4], in0=stt[:, 2:3], in1=cst[:, 0:1], op=ALU.pow)
        op("dve", "tensor_scalar", [bst], [bst], out=stt[:, 4:5], in0=stt[:, 0:1], scalar1=stt[:, 3:4], scalar2=-1.0, op0=ALU.mult, op1=ALU.mult)
        op("act", "activation", [bX, bst], [bX], out=X, in_=X, func=AF.Identity, bias=stt[:, 4:5], scale=stt[:, 3:4])
        op("dve", "tensor_tensor", [bX, bg_], [bX], out=X, in0=X, in1=gt, op=ALU.mult)
        if out is None:
            op("pool", "tensor_tensor", [bX, bb_], [bX], out=X, in0=X, in1=bt, op=ALU.add)
        else:
            op("pool", "tensor_tensor", [bX, bb_], [bout], out=out, in0=X, in1=bt, op=ALU.add)

    def attention(mixer, q, bq, keys, mixed_t, bmixed, mixed_off, gate_ap, bgate, bank_fn, PTs, bPTs, ptkey):
        oacc = acc[mixer]
        bacc = b_acc[mixer]
        nk = len(keys)
        pend = []
        npt = len(PTs)

        def stage1(jj):
            (k0, k1, rk_, vfn, rv_, extra) = keys[jj]
            ST, bST = bank_fn()
            mm(ST[:, 0:256], k0, q[:, 0, :], True, rk_ + [bq], [bST])
            mm(ST[:, 256:512], k1, q[:, 1, :], False, rk_ + [bq], [bST])
            for (l_, r_, rd_) in extra:
                mm(ST[:, :], l_, r_, False, rd_, [bST])
            pk = nxt(ptkey, npt)
            op("act", "activation", [bST], [bPTs[pk]], out=PTs[pk][:], in_=ST[:, :], func=AF.Exp)
            pend.append((jj, pk))

        def stage2():
            jj, pk = pend.pop(0)
            (k0, k1, rk_, vfn, rv_, extra) = keys[jj]
            for h in range(4):
                mm(oacc[:, h * 65:(h + 1) * 65], PTs[pk][:, h * 128:(h + 1) * 128], vfn(h), (jj == 0 and h == 0), [bPTs[pk]] + rv_, [bacc])

        look = min(2, npt)
        for jj in range(nk):
            stage1(jj)
            if len(pend) >= look:
                stage2()
            yield
        while pend:
            stage2()
        ov = oacc[:, 0:260].rearrange("p (h d) -> p h d", d=65)
        rec = recs[mixer]
        op("dve", "reciprocal", [bacc], [b_recs[mixer]], out=rec[:, 0:4], in_=ov[:, :, 64])
        for h in range(4):
            op("dve", "scalar_tensor_tensor", [bacc, b_recs[mixer], bgate], [bmixed],
               out=mixed_t[:, mixed_off + h * 64:mixed_off + (h + 1) * 64], in0=oacc[:, h * 65:h * 65 + 64],
               scalar=rec[:, h:h + 1], in1=gate_ap[:, h * 64:(h + 1) * 64], op0=ALU.mult, op1=ALU.mult)
        yield

    load_direct("lng", lng[:], b_lng, lnin_d[0])
    load_direct("lnb", lnb[:], b_lnb, lnin_d[1])
    last_store = [None, None]
    for i in range(nblk):
        s_ = i % 2
        X = xin[s_]
        dma(d_x[s_], X[:], x_d[i * 128:(i + 1) * 128, :], [], [b_x[s_]])
        layer_norm(X[:], b_x[s_], 1024, lng[:], b_lng, lnb[:], b_lnb, stB, b_stB)
        last_store[s_] = dma(d_o[s_], out_d[i * 128:(i + 1) * 128, :], X[:], [b_x[s_]], [b_o[i]])

    def setup_layer(l):
        load_cvt(wtm, b_wtm, wtm_d[l], 8 * NTM)
        load_cvt(wfm, b_wfm, wfm_d[l], 8 * NFM)
        load_cvt(wout, b_wout, wout_d[l], 8 * 1024, scale=0.5)
        load_cvt(brow, b_brow, brow_d[l], 512)
        load_cvt(wsT, b_wsT, wsT_d[l], 512)
        op("pool", "memset", [], [b_wsT], wsT[64:128, :].rearrange("p (g i) -> p g i", g=4)[:, :, 0:64], 0.0)
        load_cvt(absT, b_absT, absT_d[l], 128)
        load_direct("bfm", bfm[:], b_bfm, bfm_d[l])
        op("dve", "tensor_scalar", [b_bfm], [b_bfm8], out=bfm8[:], in0=bfm[:], scalar1=0.125, scalar2=None, op0=ALU.mult)
        load_direct("lng", lng[:], b_lng, lng_d[l, 0])
        load_direct("lnb", lnb[:], b_lnb, lng_d[l, 1])
        load_direct("alng", alng[:], b_alng, alng_d[l, 0])
        load_direct("alnb", alnb[:], b_alnb, alng_d[l, 1])
        load_direct("tmpb", tmpb[:], b_tmpb, cB_d[l])
        for m_, (dstt, bd) in enumerate(((biasB0, b_biasB0), (biasB1, b_biasB1))):
            def post(dst, stg, c0, n, h, bd=bd):
                op("dve", "tensor_tensor", [b_sc[h], b_tmpb], [bd], out=dst, in0=stg, in1=tmpb[:, c0:c0 + n], op=ALU.subtract)
            load_cvt(dstt, bd, biasB_d[l, m_], 512, post=post)
        load_direct("tmpb", tmpb[:], b_tmpb, maskB_d[0])
        op("dve", "tensor_tensor", [b_biasB0, b_tmpb], [b_biasB0], out=biasB0[:], in0=biasB0[:], in1=tmpb[:], op=ALU.add)
        for mt in range(2):
            X = xin[mt]
            dma(d_x[mt], X[:], mem_d[mt * 128:(mt + 1) * 128, :], [], [b_x[mt]])
            op("act", "copy", [b_x[mt]], [b_mixed[0]], out=mixed[0][:], in_=X[:])
            tb, btb = bankA()
            tv = tb[:, :].bitcast(BF16)
            for kt in range(8):
                transpose(tv[:, kt * 128:(kt + 1) * 128], mixed[0][:, kt * 128:(kt + 1) * 128], [b_mixed[0]], [btb])
            op("dve", "tensor_copy", [btb], b_rt, out=memT[:, :, mt * 128:(mt + 1) * 128], in_=tv.rearrange("p (a b) -> p a b", b=128))
        load_cvt(wmb, b_wmb, wmk_d[l], 8 * 256)
        for t in range(2):
            bk, bbk = bankA()
            for kt in range(8):
                mm(bk[:, 0:256], wmb[:, kt * 256 + t * 128:kt * 256 + (t + 1) * 128], memT[:, kt, :], kt == 0, [b_wmb] + b_rt, [bbk])
            op("dve", "tensor_copy", [bbk], [b_kmT], out=kmT[:, t, :], in_=bk[:, 0:256])
        load_cvt(wmb, b_wmb, wmv_d[l], 8 * 256)
        for mt in range(2):
            bk, bbk = bankA()
            for kt in range(8):
                mm(bk[:, 0:256], memT[:, kt, mt * 128:(mt + 1) * 128], wmb[:, kt * 256:(kt + 1) * 256], kt == 0, [b_wmb] + b_rt, [bbk])
            op("dve", "tensor_copy", [bbk], [b_vM], out=vM[:, mt, :, 0:64], in_=bk[:, 0:256].rearrange("p (h d) -> p h d", d=64))

    lo, hi, full = slice(0, 64), slice(64, 128), slice(0, 128)

    def ev(out, in0, prt, ct, scale, reads, writes):
        if ct < 10:
            if scale is None:
                op("act", "activation", reads, writes, out=out, in_=in0, func=AF.Identity, bias=bfm[prt, ct:ct + 1], scale=1.0)
            else:
                op("act", "activation", reads + [b_bfm8], writes, out=out, in_=in0, func=AF.Identity, bias=bfm8[prt, ct:ct + 1], scale=scale)
            return
        if scale is None:
            op("dve", "tensor_scalar", reads, writes, out=out, in0=in0, scalar1=bfm[prt, ct:ct + 1], scalar2=None, op0=ALU.add)
        else:
            op("dve", "tensor_scalar", reads, writes, out=out, in0=in0, scalar1=bfm[prt, ct:ct + 1], scalar2=scale, op0=ALU.add, op1=ALU.mult)

    def fm_group(i, cts, bank_fn):
        p_ = i % 2
        sl = i % 5
        bk, bbk = bank_fn()
        first = True
        for ci, ct in enumerate(cts):
            for kt in range(8):
                mm(bk[:, ci * 128:(ci + 1) * 128], wfm[:, kt * NFM + ct * 128:kt * NFM + (ct + 1) * 128], xnT[:, kt, :], first, [b_wfm, b_xnT], [bbk])
                first = False
        for ci, ct in enumerate(cts):
            pst = bk[:, ci * 128:(ci + 1) * 128]
            rd = [bbk, b_bfm]
            if ct in (0, 1, 8, 9):
                m = "B" if ct < 2 else "M"
                t = ct % 2
                ev(qblk[m][lo, t, 0:128], pst[lo, :], lo, ct, 0.125, rd, [b_qblk[m]])
                ev(qblk[m][hi, t, 128:256], pst[hi, :], hi, ct, 0.125, rd, [b_qblk[m]])
            elif ct in (4, 5):
                t = ct % 2
                ev(qblkC[p_][lo, t, 0:128], pst[lo, :], lo, ct, 0.125, rd, [b_qblkC[p_]])
                ev(qblkC[p_][hi, t, 128:256], pst[hi, :], hi, ct, 0.125, rd, [b_qblkC[p_]])
            elif ct in (2, 3):
                ev(kBT[:, ct - 2, sl * 128:(sl + 1) * 128], pst[:, :], full, ct, None, rd, [b_kB[sl]])
            elif ct in (6, 7):
                ev(kCT[:, ct - 6, i * 128:(i + 1) * 128], pst[:, :], full, ct, None, rd, [b_kC[i]])
            elif ct in (10, 11, 12, 13):
                t = ct - 10
                ev(iqblk[lo, 2 * t, :], pst[lo, :], lo, ct, None, rd, [b_iqblk])
                ev(iqblk[hi, 2 * t + 1, :], pst[hi, :], hi, ct, None, rd, [b_iqblk])
            else:
                ev(ikT[:, i * 128:(i + 1) * 128], pst[:, :], full, ct, None, rd, [b_ik[i]])

    def tm_tile(r, c0, n, bank_fn):
        bk, bbk = bank_fn()
        for kt in range(8):
            mm(bk[:, 0:n], xnT[:, kt, :], wtm[:, kt * NTM + c0:kt * NTM + c0 + n], kt == 0, [b_xnT, b_wtm], [bbk])
        mm(bk[:, 0:n], sel[:, r * 128:(r + 1) * 128], brow[:, 0:n], False, [b_sel, b_brow], [bbk])
        return bk, bbk

    def phaseHead(i):
        s_ = i % 2
        p_ = i % 2
        X = xin[s_]
        N_i = (i + 1) * 128
        mx = mixed[p_]
        bmx = b_mixed[p_]
        dma(d_x[s_], X[:], out_d[i * 128:(i + 1) * 128, :], [b_o[i]], [b_x[s_]])
        op("act", "copy", [b_x[s_]], [bmx], out=mx[:], in_=X[:])
        tb, btb = bankA()
        tv = tb[:, :].bitcast(BF16)
        for kt in range(8):
            transpose(tv[:, kt * 128:(kt + 1) * 128], mx[:, kt * 128:(kt + 1) * 128], [bmx], [btb])
        op("dve", "tensor_copy", [btb], [b_xnT], out=xnT[:].rearrange("p a b -> p (a b)"), in_=tv)
        yield 0.5
        fm_group(i, [10, 11, 12, 13], bankA)
        yield 0.2
        fm_group(i, [14], bankA)
        bk, bbk = tm_tile(4, 2048, 8, bankA)
        op("dve", "tensor_copy", [bbk], [b_iw], out=iw[:], in_=bk[:, 0:8])
        yield 0.5

    def phaseTail(i):
        N_i = (i + 1) * 128
        ntile = (N_i + 511) // 512
        for c in range(ntile):
            c0 = c * 512
            n = min(512, N_i - c0)
            half = [b_sc[0]] if c0 + n <= 2048 else [b_sc[1]]
            rik = [b_ik[jj] for jj in range(c0 // 128, (c0 + n) // 128)]
            for h in range(8):
                bk, bbk = bankA()
                mm(bk[:, 0:n], iqblk[:, h, :], ikT[:, c0:c0 + n], True, [b_iqblk] + rik, [bbk])
                if h == 0:
                    op("dve", "tensor_scalar", [bbk, b_iw], half, out=score[:, c0:c0 + n], in0=bk[:, 0:n], scalar1=0.0, scalar2=iw[:, 0:1], op0=ALU.max, op1=ALU.mult)
                else:
                    r_ = nxt("rtmp", 2)
                    op("act", "activation", [bbk], [b_rtmp[r_]], out=rtmp[r_][:, 0:n], in_=bk[:, 0:n], func=AF.Relu)
                    op("dve", "scalar_tensor_tensor", [b_rtmp[r_], b_iw] + half, half, out=score[:, c0:c0 + n], in0=rtmp[r_][:, 0:n], scalar=iw[:, h:h + 1],
                       in1=score[:, c0:c0 + n], op0=ALU.mult, op1=ALU.add)
                yield W_IDX * n / 512.0
        SC = b_sc if N_i > 2048 else [b_sc[0]]
        lasthalf = [b_sc[1]] if N_i > 2048 else [b_sc[0]]
        op("pool", "memset", [], lasthalf, score[0:64, N_i - 64:N_i], -1e30)
        yield "IDX_DONE"
        if i >= 2:
            op("dve", "tensor_reduce", SC, [b_bis], out=bis[:, 0:1], in_=score[:, 0:N_i], axis=AX.X, op=ALU.max)
            op("dve", "tensor_reduce", SC, [b_bis], out=bis[:, 1:2], in_=score[:, 0:N_i - 64], axis=AX.X, op=ALU.min)
            op("dve", "tensor_tensor", [b_bis], [b_bis], out=bis[:, 2:3], in0=bis[:, 0:1], in1=bis[:, 1:2], op=ALU.subtract)
            op("dve", "tensor_scalar", [b_bis, b_pow2], [b_rk], out=rk[:, :], in0=pow2[:, :], scalar1=bis[:, 2:3], scalar2=None, op0=ALU.mult)
            op("dve", "tensor_tensor", [b_bis, b_rk], [b_bis], out=bis[:, 8:9], in0=bis[:, 1:2], in1=rk[:, 1:2], op=ALU.add)
            yield 1.0 + N_i / 500.0
            nd = max(64, int(round(N_i * DVE_FRAC / 64)) * 64, N_i - 2048)
            na = N_i - nd
            SCd = [b_sc[0]] if nd <= 2048 else b_sc
            SCa = [b_sc[1]] if nd >= 2048 else (b_sc if N_i > 2048 else [b_sc[0]])
            for k in range(1, NBIS + 1):
                mid = bis[:, 8 + (k - 1) % 2:9 + (k - 1) % 2]
                midn = bis[:, 8 + k % 2:9 + k % 2]
                op("dve", "tensor_scalar", SCd + [b_bis], [b_rt[0], b_cntD], out=junk[:, 0:nd], in0=score[:, 0:nd], scalar1=mid, scalar2=None,
                   op0=ALU.is_ge, op1=ALU.add, accum_out=cntD[:, 0:1])
                op("act", "activation", SCa + [b_bis], [b_rt[1], b_cntA], out=junk[:, 2048:2048 + na], in_=score[:, nd:N_i], func=AF.Sign, bias=mid, scale=-1.0,
                   accum_out=cntA[:, 0:1])
                op(SMALL_ENG, "tensor_scalar", [b_cntD, b_cntA], [b_bis], out=bis[:, 4:5], in0=cntD[:, 0:1], scalar1=2.0, scalar2=cntA[:, 0:1], op0=ALU.mult, op1=ALU.subtract)
                op(SMALL_ENG, "tensor_scalar", [b_bis, b_rk], [b_bis], out=bis[:, 5:6], in0=bis[:, 4:5], scalar1=float(512 - na), scalar2=rk[:, k:k + 1], op0=ALU.is_ge, op1=ALU.mult)
                kk = k + 1 if k < NBIS else k
                op(SMALL_ENG, "tensor_scalar", [b_bis, b_rk], [b_bis], out=midn, in0=bis[:, 5:6], scalar1=rk[:, kk:kk + 1], scalar2=mid, op0=ALU.subtract, op1=ALU.add)
                yield 1.2 + N_i / 1100.0
            thr = bis[:, 8 + NBIS % 2:9 + NBIS % 2]
        else:
            op("dve", "memset", [], [b_bis], bis[:, 12:13], -1e29)
            thr = bis[:, 12:13]
        yield "B_DONE"
        c0 = 0
        while c0 < N_i:
            n = min(2048, N_i - c0)
            op("dve", "tensor_scalar", [b_sc[c0 // 2048], b_bis], [b_mb], out=mb[:, c0:c0 + n], in0=score[:, c0:c0 + n], scalar1=thr, scalar2=NEG, op0=ALU.is_lt, op1=ALU.mult)
            c0 += n
        yield 0.0

    def main_weight(i):
        N_i = (i + 1) * 128
        w = 1.2 + W_IDX * 8 * N_i / 512.0
        if i >= 2:
            w += 1.0 + N_i / 500.0 + NBIS * (1.2 + N_i / 1100.0)
        return w

    def phaseSide(i):
        s_ = i % 2
        p_ = i % 2
        sl = i % 5
        mx = mixed[p_]
        bmx = b_mixed[p_]
        bk, bbk = tm_tile(0, 0, 512, bankS)
        op("act", "copy", [bbk], [b_xg], out=xg[:], in_=bk[:, :])
        op("pool", "tensor_tensor", [b_xg], [b_gs], out=gsc0[:], in0=xg[:], in1=xg[:], op=ALU.mult)
        op("dve", "scalar_tensor_tensor", [b_gs, b_xg], [b_gs], out=gsc0[:], in0=gsc0[:], scalar=0.044715, in1=xg[:], op0=ALU.mult, op1=ALU.mult)
        op("pool", "tensor_tensor", [b_gs, b_xg], [b_gs], out=gsc0[:], in0=gsc0[:], in1=xg[:], op=ALU.add)
        op("act", "activation", [b_gs], [b_gs], out=gsc0[:], in_=gsc0[:], func=AF.Tanh, scale=0.7978845608028654)
        op("dve", "scalar_tensor_tensor", [b_gs, b_xg], [b_xg], out=xg[:], in0=gsc0[:], scalar=1.0, in1=xg[:], op0=ALU.add, op1=ALU.mult)
        yield
        bk, bbk = tm_tile(1, 512, 512, bankS)
        op("act", "activation", [bbk], [b_gs], out=gsc0[:], in_=bk[:, :], func=AF.Tanh, scale=0.5)
        op("dve", "scalar_tensor_tensor", [b_gs, bbk], [b_gates], out=gates[:, 0:512], in0=gsc0[:], scalar=1.0, in1=bk[:, :], op0=ALU.add, op1=ALU.mult)
        yield
        bk, bbk = tm_tile(2, 1024, 512, bankS)
        op("act", "activation", [bbk], [b_gs], out=gsc0[:], in_=bk[:, :], func=AF.Tanh, scale=0.5)
        op("dve", "scalar_tensor_tensor", [b_gs, bbk], [b_gatesC[p_]], out=gatesC[p_][:, :], in0=gsc0[:, 0:256], scalar=1.0, in1=bk[:, 0:256], op0=ALU.add, op1=ALU.mult)
        op("dve", "scalar_tensor_tensor", [b_gs, bbk], [b_gates], out=gates[:, 512:768], in0=gsc0[:, 256:512], scalar=1.0, in1=bk[:, 256:512], op0=ALU.add, op1=ALU.mult)
        yield
        bk, bbk = tm_tile(3, 1536, 512, bankS)
        op("act", "copy", [bbk], [b_vB[sl]], out=vB[:, sl, :, 0:64], in_=bk[:, 0:256].rearrange("p (h d) -> p h d", d=64))
        op("act", "copy", [bbk], [b_vC[i]], out=vC[:, i, :, 0:64], in_=bk[:, 256:512].rearrange("p (h d) -> p h d", d=64))
        yield
        for cts in ([0, 1, 2, 3], [4, 5, 6, 7], [8, 9]):
            fm_group(i, cts, bankS)
            yield
        op("act", "mul", [b_xg], [b_xg], out=xg[:, 256:512], in_=xg[:, 256:512], mul=0.5)
        layer_norm(xg[:, 256:512], b_xg, 256, alng[:], b_alng, alnb[:], b_alnb, stA, b_stA, out=vln[:], bout=b_vln)
        bk, bbk = bankS()
        for g in range(4):
            mm(bk[:, g * 64:(g + 1) * 64], wsT[:, g * 128:(g + 1) * 128], vln[:, g * 64:(g + 1) * 64], g == 0, [b_wsT, b_vln], [bbk])
        mm(bk[:, 0:256], absT[:, :], Emat[:, :], False, [b_absT, b_E], [bbk])
        op("dve", "scalar_tensor_tensor", [bbk, b_xg], [b_xg], out=xg[:, 256:512], in0=bk[:, 0:256], scalar=0.5, in1=xg[:, 0:256], op0=ALU.mult, op1=ALU.mult)
        op("pool", "tensor_tensor", [b_xg, b_gates], [bmx], out=mx[:, 0:256], in0=xg[:, 256:512], in1=gates[:, 0:256], op=ALU.mult)
        yield
        keys = []
        for j in range(max(0, i - 4), i + 1):
            sj = j % 5
            m_ = i - j
            extra = []
            if m_ == 0:
                extra.append((ident4[:, 0:128], biasB0[:, :], [b_id, b_biasB0]))
            elif m_ == 1:
                extra.append((ident4[:, 0:128], biasB1[:, :], [b_id, b_biasB1]))
            elif m_ == 4:
                extra.append((ident4[:, 0:128], maskB4[:, :], [b_id, b_maskB4]))
            keys.append((kBT[:, 0, sj * 128:(sj + 1) * 128], kBT[:, 1, sj * 128:(sj + 1) * 128], [b_kB[sj]],
                         (lambda h, sj=sj: vB[:, sj, h, :]), [b_vB[sj]], extra))
        yield from attention("B", qblk["B"], b_qblk["B"], keys, mx, bmx, 256, gates[:, 256:512], b_gates, bankS, PTA, b_PTA, "PTA")
        keys = []
        for mt in range(2):
            keys.append((kmT[:, 0, mt * 128:(mt + 1) * 128], kmT[:, 1, mt * 128:(mt + 1) * 128], [b_kmT],
                         (lambda h, mt=mt: vM[:, mt, h, :]), [b_vM], []))
        yield from attention("M", qblk["M"], b_qblk["M"], keys, mx, bmx, 768, gates[:, 512:768], b_gates, bankS, PTA, b_PTA, "PTA")

    def phaseB(i, l):
        s_ = i % 2
        p_ = i % 2
        X = xin[s_]
        mx = mixed[p_]
        bmx = b_mixed[p_]
        keys = []
        for j in range(0, i + 1):
            extra = [(mb[:, j * 128:(j + 1) * 128], ident4[:, :], [b_mb, b_id])]
            d_ = i - j
            if d_ == 0:
                extra.append((ident4[:, 0:128], biasC0[:, :], [b_id, b_biasC0]))
            elif d_ == 1:
                extra.append((ident4[:, 0:128], biasC1[:, :], [b_id, b_biasC1]))
            keys.append((kCT[:, 0, j * 128:(j + 1) * 128], kCT[:, 1, j * 128:(j + 1) * 128], [b_kC[j]],
                         (lambda h, j=j: vC[:, j, h, :]), [b_vC[j]], extra))
        yield from attention("C", qblkC[p_], b_qblkC[p_], keys, mx, bmx, 512, gatesC[p_][:, :], b_gatesC[p_], bankC, PTC, b_PTC, "PTC")
        tb, btb = bankC()
        tv = tb[:, :].bitcast(BF16)
        for kt in range(8):
            transpose(tv[:, kt * 128:(kt + 1) * 128], mx[:, kt * 128:(kt + 1) * 128], [bmx], [btb])
        op("dve", "tensor_copy", [btb], [b_mixedT], out=mixedT[:].rearrange("p a b -> p (a b)"), in_=tv)
        yield
        for c in range(2):
            bk, bbk = bankC()
            for kt in range(8):
                mm(bk[:, :], mixedT[:, kt, :], wout[:, kt * 1024 + c * 512:kt * 1024 + (c + 1) * 512], kt == 0, [b_mixedT, b_wout], [bbk])
            mm(bk[:, :], sel[:, (5 + c) * 128:(6 + c) * 128], brow[:, :], False, [b_sel, b_brow], [bbk])
            op("dve", "scalar_tensor_tensor", [b_x[s_], bbk], [b_x[s_]], out=X[:, c * 512:(c + 1) * 512], in0=X[:, c * 512:(c + 1) * 512], scalar=ALPHA, in1=bk[:, :],
               op0=ALU.mult, op1=ALU.add)
            yield
        layer_norm(X[:], b_x[s_], 1024, lng[:], b_lng, lnb[:], b_lnb, stB, b_stB)
        dma(d_o[s_], out_d[i * 128:(i + 1) * 128, :], X[:], [b_x[s_]], [b_o[i]], holder=(last_store, s_))
        yield

    def drain(g):
        for _ in g:
            pass

    class Stream:
        def __init__(self, gen, prio):
            self.gen = gen
            self.q = []
            self.done = False
            self.prio = prio
            self.active = True

    eng_free = {e: 0.0 for e in ENGS}
    tok_done = {}

    def fill(st):
        while not st.q and not st.done:
            cur_q[0] = st.q
            try:
                y = next(st.gen)
                if isinstance(y, str):
                    st.q.append(y)
            except StopIteration:
                st.done = True
            finally:
                cur_q[0] = None

    def ready_time(eng, reads, writes):
        t = 0.0
        for bf in reads:
            if bf.writer is not None:
                t = max(t, tok_done.get((id(bf.writer[0]), bf.writer[1]), 0.0) + (0.0 if bf.writer[2] == eng else 0.12))
            if bf.psum:
                for tk in bf.readers.values():
                    if tk[2] != eng:
                        t = max(t, tok_done.get((id(tk[0]), tk[1]), 0.0) + 0.12)
        for bf in writes:
            if bf.writer is not None:
                t = max(t, tok_done.get((id(bf.writer[0]), bf.writer[1]), 0.0) + (0.0 if bf.writer[2] == eng else 0.12))
            for tk in bf.readers.values():
                if tk[2] != eng:
                    t = max(t, tok_done.get((id(tk[0]), tk[1]), 0.0) + 0.12)
        return t

    def commit(item, start):
        eng, fn, reads, writes, dur, ds, holder = item
        if ds is not None:
            tok = S.dma(eng, ds, fn, reads=reads, writes=writes)
            if holder is not None:
                holder[0][holder[1]] = tok
            eng_free[eng] = start + 0.1
        else:
            tok = S.op(eng, fn, reads=reads, writes=writes)
            eng_free[eng] = start + dur
        tok_done[(id(tok[0]), tok[1])] = start + dur

    def run_round(gmain, gside, gB, ghead):
        main = Stream(gmain, 0.0) if gmain is not None else None
        side = Stream(gside, 0.0) if gside is not None else None
        stB = Stream(gB, 0.0) if gB is not None else None
        head = Stream(ghead, 0.0) if ghead is not None else None
        idx_done = [main is None]
        if head is not None:
            head.active = False
        streams = [x for x in (main, stB, side, head) if x is not None]

        def finished(st):
            return st is None or (st.done and not st.q)

        while True:
            best = None
            alive = False
            if head is not None and not head.active and finished(stB) and finished(side) and idx_done[0]:
                head.active = True
            for st in streams:
                fill(st)
                if not st.q:
                    continue
                alive = True
                if not st.active:
                    continue
                item = st.q[0]
                if isinstance(item, str):
                    if item == "IDX_DONE":
                        st.q.pop(0)
                        idx_done[0] = True
                        best = "again"
                        break
                    if item == "B_DONE":
                        if finished(stB):
                            st.q.pop(0)
                            best = "again"
                            break
                        continue
                    st.q.pop(0)
                    best = "again"
                    break
                eng = item[0]
                start = max(eng_free[eng], ready_time(eng, item[2], item[3]))
                key = start - st.prio
                if best is None or key < best[0]:
                    best = (key, start, st)
            if best == "again":
                continue
            if best is None:
                if not alive:
                    break
                if head is not None and not head.active and finished(stB) and finished(side) and idx_done[0]:
                    continue
                raise RuntimeError("scheduler stuck")
            _, start, st = best
            commit(st.q.pop(0), start)

    for l in range(n_layers):
        setup_layer(l)
        if not pipeline:
            for i in range(nblk):
                drain(phaseHead(i))
                drain(phaseTail(i))
                drain(phaseSide(i))
                drain(phaseB(i, l))
        else:
            drain(phaseHead(0))
            run_round(phaseTail(0), phaseSide(0), None, phaseHead(1) if nblk > 1 else None)
            for i in range(nblk):
                run_round(phaseTail(i + 1) if i + 1 < nblk else None,
                          phaseSide(i + 1) if i + 1 < nblk else None,
                          phaseB(i, l),
                          phaseHead(i + 2) if i + 2 < nblk else None)

    S.wait_tokens("sp", [t for t in last_store if t is not None])
    S.emit()
    es.close()
    return nc


def _t5_bucket_idx(rel):
    nb = 16
    max_exact = 8
    ret = np.where(rel > 0, nb, 0)
    n = np.abs(rel)
    nf = np.maximum(n, 1).astype(np.float32)
    large = max_exact + (np.log(nf / np.float32(max_exact)) / np.float32(np.log(128 / max_exact)) * (nb - max_exact)).astype(np.int32)
    large = np.minimum(large, nb - 1)
    return ret + np.where(n < max_exact, n, large)


def _kt_layout(w):
    C = w.shape[1]
    return np.ascontiguousarray(w.reshape(8, 128, C).transpose(1, 0, 2).reshape(128, 8 * C))


def prepare(inputs):
    f = np.float32
    cum = np.cumsum((0,) + SPLITS)
    rng = {n: (int(cum[k]), int(cum[k + 1])) for k, n in enumerate(NAMES)}
    tm_cols = np.concatenate([np.arange(*rng[n]) for n in TM_ORDER])
    fm_cols = np.concatenate([np.arange(*rng[n]) for n in FM_ORDER])
    w_in = np.asarray(inputs["w_in"], f)
    b_in = np.asarray(inputs["b_in"], f)
    sh = {}
    sh["wtm"] = np.stack([_kt_layout(w_in[l][:, tm_cols]) for l in range(DEPTH)])
    sh["wfm"] = np.stack([_kt_layout(w_in[l][:, fm_cols]) for l in range(DEPTH)])
    sh["wout"] = np.stack([_kt_layout(np.asarray(inputs["w_out"], f)[l]) for l in range(DEPTH)])
    wm = np.asarray(inputs["w_mem_kv"], f)
    sh["wmk"] = np.stack([_kt_layout(wm[l][:, 0:256]) for l in range(DEPTH)])
    sh["wmv"] = np.stack([_kt_layout(wm[l][:, 256:512]) for l in range(DEPTH)])
    brow = np.zeros((DEPTH, 128, 512), f)
    btm = b_in[:, tm_cols]
    for r in range(4):
        brow[:, r, :] = btm[:, r * 512:(r + 1) * 512]
    brow[:, 4, 0:8] = btm[:, 2048:2056]
    b_out = np.asarray(inputs["b_out"], f)
    brow[:, 5, :] = b_out[:, 0:512]
    brow[:, 6, :] = b_out[:, 512:1024]
    sh["brow"] = brow
    sel = np.zeros((128, 7, 128), f)
    for r in range(7):
        sel[r, r, :] = 1.0
    sh["sel"] = sel.reshape(128, 7 * 128)
    sh["bfm"] = np.ascontiguousarray(b_in[:, fm_cols].reshape(DEPTH, 15, 128).transpose(0, 2, 1))
    sh["lnin"] = np.stack([np.broadcast_to(np.asarray(inputs["ln_in_g"], f), (128, 1024)),
                           np.broadcast_to(np.asarray(inputs["ln_in_b"], f), (128, 1024))]).copy()
    sh["lng"] = np.stack([np.stack([np.broadcast_to(np.asarray(inputs["ln_g"], f)[l], (128, 1024)),
                                    np.broadcast_to(np.asarray(inputs["ln_b"], f)[l], (128, 1024))]) for l in range(DEPTH)]).copy()
    sh["alng"] = np.stack([np.stack([np.broadcast_to(np.asarray(inputs["a_ln_g"], f)[l], (128, 256)),
                                     np.broadcast_to(np.asarray(inputs["a_ln_b"], f)[l], (128, 256))]) for l in range(DEPTH)]).copy()
    a_ws = np.asarray(inputs["a_ws"], f)
    sh["wsT"] = np.ascontiguousarray(a_ws.transpose(0, 3, 1, 2).reshape(DEPTH, 128, 512))
    ab = np.zeros((DEPTH, 128, 128), f)
    ab[:, 0:4, :] = np.asarray(inputs["a_bs"], f)
    sh["absT"] = ab
    b_rel = np.asarray(inputs["b_rel"], f)
    a = np.arange(128)[:, None]
    b = np.arange(128)[None, :]
    bB = np.zeros((DEPTH, 2, 128, 4, 128), f)
    for m in range(2):
        idx = np.clip(128 * m + b - a, -128, 128) + 128
        for l in range(DEPTH):
            bB[l, m] = b_rel[l][:, idx].transpose(1, 0, 2)
    sh["biasB"] = bB.reshape(DEPTH, 2, 128, 512)
    sh["cB"] = np.ascontiguousarray(np.broadcast_to(b_rel[:, None, :, 256, None], (DEPTH, 128, 4, 128)).reshape(DEPTH, 128, 512))
    t5 = np.asarray(inputs["t5_table"], f)
    bC = np.zeros((2, 128, 4, 128), f)
    for d in range(2):
        relkq = -128 * d + a - b
        bi = _t5_bucket_idx(relkq)
        bC[d] = t5[bi].transpose(0, 2, 1)
    sh["biasC"] = bC.reshape(2, 128, 512)
    sh["cC"] = np.ascontiguousarray(np.broadcast_to(t5[15][None, :, None], (128, 4, 128)).reshape(128, 512))
    mB = np.zeros((2, 128, 4, 128), f)
    mB[0][64:128, :, 0:64] = NEG
    mB[1][0:64, :, 64:128] = NEG
    sh["maskB"] = mB.reshape(2, 128, 512)
    E = np.zeros((128, 256), f)
    for g in range(4):
        E[g, g * 64:(g + 1) * 64] = 1.0
    sh["Emat"] = E
    sh["ident4"] = np.tile(np.eye(128, dtype=f), (1, 4))
    sh["pow2"] = np.broadcast_to((2.0 ** -np.arange(NBIS + 2)).astype(f)[None, :], (128, NBIS + 2)).copy()
    return sh


_NC_CACHE = {}


def kernel(**inputs):
    sh = prepare(inputs)
    x = np.asarray(inputs["x"], np.float32)
    mem = np.asarray(inputs["mem"], np.float32)
    if "nc" not in _NC_CACHE:
        _NC_CACHE["nc"] = build_program()
    nc = _NC_CACHE["nc"]
    in_maps = []
    for c in range(BATCH):
        m = dict(sh)
        m["x"] = np.ascontiguousarray(x[c])
        m["mem"] = np.ascontiguousarray(mem[c])
        in_maps.append(m)
    res = run_bass_kernel_spmd(nc, in_maps, core_ids=list(range(BATCH)))
    return np.stack([np.asarray(r["out"]) for r in res.results]).astype(np.float32)
```

```python
import numpy as np
import concourse.bass as bass
import concourse.mybir as mybir
from concourse.bass_utils import run_bass_kernel_spmd
from contextlib import ExitStack

F32 = mybir.dt.float32
BF16 = mybir.dt.bfloat16
U8 = mybir.dt.uint8
ALU = mybir.AluOpType
AF = mybir.ActivationFunctionType
AX = mybir.AxisListType

D_MODEL = 1024
BATCH = 8
SEQ = 4096
DEPTH = 2
NBLK = SEQ // 128
GW = 256
SPLITS = (GW, GW, GW, GW, GW, GW, GW, GW, GW, GW, GW, 512, 64, 8, GW, GW)
NAMES = ("a_u", "a_v", "a_g", "bq", "bk", "bv", "bg", "cq", "ck", "cv", "cg", "iq", "ik", "iw", "mq", "mg")
TM_ORDER = ("a_u", "a_v", "a_g", "bg", "cg", "mg", "bv", "cv", "iw")
FM_ORDER = ("bq", "bk", "cq", "ck", "mq", "iq", "ik", "ik")
NTM = 2056
NFM = 1920
ALPHA = (2 * DEPTH) ** 0.25
LN_EPS = 1e-5
NBIS = 16
DVE_FRAC = 0.35
W_IDX = 0.12
SMALL_ENG = "pool"
NEG = -30000.0

ENGS = ("pe", "act", "dve", "pool", "sp")
STRICT_WAR = False


class Buf:
    __slots__ = ("name", "writer", "readers", "psum")

    def __init__(self, name="", psum=False):
        self.name = name
        self.writer = None
        self.readers = {}
        self.psum = psum


class DmaSem:
    def __init__(self, sem):
        self.sem = sem
        self.count = 0


class Sched:
    EPOCH = 12000

    def __init__(self, nc, es):
        self.nc = nc
        self.es = es
        self.prog = {e: [] for e in ENGS}
        self.cnt = {e: 0 for e in ENGS}
        self.nsem = 0
        self.cursem = {e: self._newsem(e) for e in ENGS}
        self.waited = {e: {} for e in ENGS}
        self.semobj = {}

    def _newsem(self, name):
        self.nsem += 1
        return self.es.enter_context(self.nc.semaphore(f"s_{name}_{self.nsem}"))

    def dma_sem(self, name):
        return DmaSem(self._newsem("d" + name))

    def _deps(self, eng, reads, writes):
        need = {}

        def add(tok, kind):
            if tok is None:
                return
            sem, val, teng = tok
            if teng == eng:
                if eng == "pe" or (kind == "war" and not STRICT_WAR):
                    return
            k = id(sem)
            self.semobj[k] = sem
            if need.get(k, 0) < val:
                need[k] = val

        for b in reads:
            add(b.writer, "raw")
            if b.psum:
                for t in b.readers.values():
                    if t[2] != eng:
                        add(t, "rar")
        for b in writes:
            add(b.writer, "waw")
            for t in b.readers.values():
                add(t, "war")
        waits = []
        w = self.waited[eng]
        for k, val in need.items():
            if w.get(k, 0) < val:
                w[k] = val
                waits.append((self.semobj[k], val))
        return waits

    def _mark(self, tok, reads, writes):
        k = id(tok[0])
        for b in reads:
            b.readers[k] = tok
        for b in writes:
            b.writer = tok
            b.readers = {}

    def op(self, eng, fn, reads=(), writes=()):
        waits = self._deps(eng, reads, writes)
        if self.cnt[eng] >= self.EPOCH:
            self.cursem[eng] = self._newsem(eng)
            self.cnt[eng] = 0
        self.cnt[eng] += 1
        tok = (self.cursem[eng], self.cnt[eng], eng)
        self.prog[eng].append((waits, fn, (self.cursem[eng], 1)))
        self._mark(tok, reads, writes)
        return tok

    def dma(self, eng, ds, fn, reads=(), writes=()):
        waits = self._deps(eng, reads, writes)
        ds.count += 16
        tok = (ds.sem, ds.count, "dma")
        self.prog[eng].append((waits, fn, (ds.sem, 16)))
        self._mark(tok, reads, writes)
        return tok

    def wait_tokens(self, eng, toks):
        self.prog[eng].append(([(t[0], t[1]) for t in toks], None, None))

    def emit(self):
        nc = self.nc
        engobj = {"pe": nc.tensor, "act": nc.scalar, "dve": nc.vector, "pool": nc.gpsimd, "sp": nc.sync}

        def replay(e):
            eo = engobj[e]
            for waits, fn, inc in self.prog[e]:
                for sem, val in waits:
                    eo.wait_ge(sem, val)
                if fn is not None:
                    fn().then_inc(inc[0], inc[1])

        with nc.Block() as block:
            @block.tensor
            def _(x):
                replay("pe")

            @block.scalar
            def _(x):
                replay("act")

            @block.vector
            def _(x):
                replay("dve")

            @block.gpsimd
            def _(x):
                replay("pool")

            @block.sync
            def _(x):
                replay("sp")


def build_program(n_layers=DEPTH, nblk=NBLK, pipeline=True):
    nc = bass.Bass("TRN2", target_bir_lowering=False)
    es = ExitStack()
    S = Sched(nc, es)

    def din(name, shape):
        return nc.dram_tensor(name, list(shape), F32, kind="ExternalInput").ap()

    x_d = din("x", [SEQ, D_MODEL])
    mem_d = din("mem", [256, D_MODEL])
    lnin_d = din("lnin", [2, 128, 1024])
    wtm_d = din("wtm", [DEPTH, 128, 8 * NTM])
    wfm_d = din("wfm", [DEPTH, 128, 8 * NFM])
    wout_d = din("wout", [DEPTH, 128, 8 * 1024])
    wmk_d = din("wmk", [DEPTH, 128, 8 * 256])
    wmv_d = din("wmv", [DEPTH, 128, 8 * 256])
    brow_d = din("brow", [DEPTH, 128, 512])
    sel_d = din("sel", [128, 7 * 128])
    bfm_d = din("bfm", [DEPTH, 128, 15])
    lng_d = din("lng", [DEPTH, 2, 128, 1024])
    alng_d = din("alng", [DEPTH, 2, 128, 256])
    wsT_d = din("wsT", [DEPTH, 128, 512])
    absT_d = din("absT", [DEPTH, 128, 128])
    biasB_d = din("biasB", [DEPTH, 2, 128, 512])
    cB_d = din("cB", [DEPTH, 128, 512])
    biasC_d = din("biasC", [2, 128, 512])
    cC_d = din("cC", [128, 512])
    maskB_d = din("maskB", [2, 128, 512])
    E_d = din("Emat", [128, 256])
    ident4_d = din("ident4", [128, 512])
    pow2_d = din("pow2", [128, NBIS + 2])
    out_d = nc.dram_tensor("out", [SEQ, D_MODEL], F32, kind="ExternalOutput").ap()
    b_o = [Buf() for _ in range(NBLK)]

    def sb(name, shape, dt):
        return es.enter_context(nc.sbuf_tensor(name, list(shape), dt))

    def ps(name, shape, dt):
        return es.enter_context(nc.psum_tensor(name, list(shape), dt))

    wtm = sb("wtm_s", [128, 8 * NTM], BF16); b_wtm = Buf()
    wfm = sb("wfm_s", [128, 8 * NFM], BF16); b_wfm = Buf()
    wout = sb("wout_s", [128, 8 * 1024], BF16); b_wout = Buf()
    brow = sb("brow_s", [128, 512], BF16); b_brow = Buf()
    sel = sb("sel_s", [128, 7 * 128], BF16); b_sel = Buf()
    bfm = sb("bfm_s", [128, 15], F32); b_bfm = Buf()
    bfm8 = sb("bfm8_s", [128, 15], F32); b_bfm8 = Buf()
    lng = sb("lng_s", [128, 1024], F32); b_lng = Buf()
    lnb = sb("lnb_s", [128, 1024], F32); b_lnb = Buf()
    alng = sb("alng_s", [128, 256], F32); b_alng = Buf()
    alnb = sb("alnb_s", [128, 256], F32); b_alnb = Buf()
    wsT = sb("wsT_s", [128, 512], BF16); b_wsT = Buf()
    absT = sb("absT_s", [128, 128], BF16); b_absT = Buf()
    Emat = sb("E_s", [128, 256], BF16); b_E = Buf()
    ident4 = sb("ident4_s", [128, 512], BF16); b_id = Buf()
    pow2 = sb("pow2_s", [128, NBIS + 2], F32); b_pow2 = Buf()
    biasB0 = sb("biasB0_s", [128, 512], BF16); biasB1 = sb("biasB1_s", [128, 512], BF16)
    maskB4 = sb("maskB4_s", [128, 512], BF16)
    biasC0 = sb("biasC0_s", [128, 512], BF16); biasC1 = sb("biasC1_s", [128, 512], BF16)
    b_biasB0, b_biasB1, b_maskB4, b_biasC0, b_biasC1 = Buf(), Buf(), Buf(), Buf(), Buf()

    kCT = sb("kCT_s", [128, 2, SEQ], BF16); b_kC = [Buf() for _ in range(NBLK)]
    vC = sb("vC_s", [128, NBLK, 4, 65], BF16); b_vC = [Buf() for _ in range(NBLK)]
    ikT = sb("ikT_s", [128, SEQ], BF16); b_ik = [Buf() for _ in range(NBLK)]
    kBT = sb("kBT_s", [128, 2, 5 * 128], BF16); b_kB = [Buf() for _ in range(5)]
    vB = sb("vB_s", [128, 5, 4, 65], BF16); b_vB = [Buf() for _ in range(5)]
    kmT = sb("kmT_s", [128, 2, 256], BF16); b_kmT = Buf()
    vM = sb("vM_s", [128, 2, 4, 65], BF16); b_vM = Buf()

    score = sb("score_s", [128, SEQ], F32); b_sc = [Buf(), Buf()]
    mb = sb("mb_s", [128, SEQ], BF16); b_mb = Buf()
    wmb = mb[:, 0:2048]; b_wmb = b_mb
    gsc0 = sb("gscr_s", [128, 512], F32); b_gs = Buf()
    rt = sb("rt_s", [128, 1024], F32); b_rt = [Buf(), Buf()]
    rtmp = [rt[:, 0:512], rt[:, 512:1024]]; b_rtmp = b_rt
    junk = rt[:, :].bitcast(U8)
    memT = rt[:, :].bitcast(BF16).rearrange("p (a b) -> p a b", b=256)
    cntD = sb("cntD_s", [128, 2], F32); b_cntD = Buf()
    cntA = sb("cntA_s", [128, 2], F32); b_cntA = Buf()
    xin = [sb(f"xin{k}_s", [128, 1024], F32) for k in range(2)]; b_x = [Buf(), Buf()]
    mixed = [sb(f"mixed{k}_s", [128, 1024], BF16) for k in range(2)]; b_mixed = [Buf(), Buf()]
    xnT = sb("xnT_s", [128, 8, 128], BF16); b_xnT = Buf()
    mixedT = sb("mixedT_s", [128, 8, 128], BF16); b_mixedT = Buf()
    xg = sb("xg_s", [128, 512], F32); b_xg = Buf()
    tmpb = xg; b_tmpb = b_xg
    gates = sb("gates_s", [128, 768], BF16); b_gates = Buf()
    gatesC = [sb(f"gatesC{k}_s", [128, 256], BF16) for k in range(2)]; b_gatesC = [Buf(), Buf()]
    vln = sb("vln_s", [128, 256], BF16); b_vln = Buf()
    qblk = {"B": sb("qblkB_s", [128, 2, 256], BF16), "M": sb("qblkM_s", [128, 2, 256], BF16)}
    b_qblk = {"B": Buf(), "M": Buf()}
    qblkC = [sb(f"qblkC{k}_s", [128, 2, 256], BF16) for k in range(2)]; b_qblkC = [Buf(), Buf()]
    iqblk = sb("iqblk_s", [128, 8, 128], BF16); b_iqblk = Buf()
    iw = sb("iw_s", [128, 8], F32); b_iw = Buf()
    PTA = [sb("PTA0_s", [128, 512], BF16)]; b_PTA = [Buf()]
    PTC = [sb(f"PTC{k}_s", [128, 512], BF16) for k in range(2)]; b_PTC = [Buf(), Buf()]
    stA = sb("stA_s", [128, 32], F32); b_stA = Buf()
    stB = sb("stB_s", [128, 32], F32); b_stB = Buf()
    recs = {m: sb(f"rec{m}_s", [128, 4], F32) for m in "BMC"}; b_recs = {m: Buf() for m in "BMC"}
    cst = sb("cst_s", [128, 4], F32); b_cst = Buf()
    bis = sb("bis_s", [128, 32], F32); b_bis = Buf()
    rk = sb("rk_s", [128, NBIS + 2], F32); b_rk = Buf()

    NA, NS, NC = 2, 3, 1
    bigA = [ps(f"bigA{k}", [128, 512], F32) for k in range(NA)]; b_bigA = [Buf(psum=True) for _ in range(NA)]
    bigS = [ps(f"bigS{k}", [128, 512], F32) for k in range(NS)]; b_bigS = [Buf(psum=True) for _ in range(NS)]
    bigC = [ps(f"bigC{k}", [128, 512], F32) for k in range(NC)]; b_bigC = [Buf(psum=True) for _ in range(NC)]
    _accS = ps("accS", [128, 512], F32); _baccS = Buf(psum=True)
    _accC = ps("accC", [128, 512], F32); _baccC = Buf(psum=True)
    acc = {"B": _accS, "M": _accS, "C": _accC}; b_acc = {"B": _baccS, "M": _baccS, "C": _baccC}
    rr = {"A": 0, "S": 0, "C": 0, "PTA": 0, "PTC": 0, "rtmp": 0, "cvt": 0, "stage": 0}

    def nxt(key, n):
        k = rr[key]
        rr[key] = (k + 1) % n
        return k

    def bankA():
        k = nxt("A", NA)
        return bigA[k], b_bigA[k]

    def bankS():
        k = nxt("S", NS)
        return bigS[k], b_bigS[k]

    def bankC():
        k = nxt("C", NC)
        return bigC[k], b_bigC[k]

    d_stage = [S.dma_sem("st0"), S.dma_sem("st1")]
    d_x = [S.dma_sem("x0"), S.dma_sem("x1")]
    d_o = [S.dma_sem("o0"), S.dma_sem("o1")]
    d_misc = {}

    def dsem(name):
        if name not in d_misc:
            d_misc[name] = S.dma_sem(name)
        return d_misc[name]

    V, A, G, T = nc.vector, nc.scalar, nc.gpsimd, nc.tensor
    ENG = {"dve": V, "act": A, "pool": G}

    cur_q = [None]

    def free_elems(ap):
        n = 1
        for d in tuple(ap.shape)[1:]:
            n *= int(d)
        return n

    def issue(eng, fn, reads, writes, dur, ds=None, holder=None):
        if cur_q[0] is not None:
            cur_q[0].append((eng, fn, list(reads), list(writes), dur, ds, holder))
            return None
        if ds is not None:
            tok = S.dma(eng, ds, fn, reads=reads, writes=writes)
            if holder is not None:
                holder[0][holder[1]] = tok
            return tok
        return S.op(eng, fn, reads=reads, writes=writes)

    def op(eng, name, reads, writes, *args, **kw):
        f = getattr(ENG[eng], name)
        o = kw.get("out", args[0] if args else None)
        n = free_elems(o) if o is not None else 1
        if eng == "dve":
            dur = 0.12 + n / 960.0 + (0.08 if "accum_out" in kw else 0.0)
        elif eng == "act":
            dur = 0.22 + n / 1200.0 + (0.1 if "accum_out" in kw else 0.0)
        else:
            dur = 0.3 + n / 480.0
        issue(eng, lambda: f(*args, **kw), reads, writes, dur)

    def mm(out, lhsT, rhs, start, reads, writes):
        dur = 0.1 + free_elems(rhs) / 1500.0
        issue("pe", lambda: T.matmul(out, lhsT, rhs, start=start, stop=True, skip_group_check=True), reads, writes, dur)

    def transpose(out, in_, reads, writes):
        idn = ident4[:, 0:128]
        issue("pe", lambda: T.transpose(out, in_, idn), reads + [b_id], writes, 0.2)

    def cast_copy(k, out, in_, reads, writes, scale=None):
        e = k % 3
        if scale is not None:
            if e == 1:
                op("act", "mul", reads, writes, out=out, in_=in_, mul=scale)
            else:
                op("dve" if e == 0 else "pool", "tensor_scalar", reads, writes, out=out, in0=in_, scalar1=scale, scalar2=None, op0=ALU.mult)
            return
        if e == 0:
            op("dve", "tensor_copy", reads, writes, out=out, in_=in_)
        elif e == 1:
            op("act", "copy", reads, writes, out=out, in_=in_)
        else:
            op("pool", "tensor_copy", reads, writes, out=out, in_=in_)

    def dma(ds, out, in_, reads, writes, holder=None):
        return issue("sp", lambda: nc.sync.dma_start(out=out, in_=in_), reads, writes, 3.0, ds=ds, holder=holder)

    def load_direct(name, dst, bdst, src):
        dma(dsem(name), dst, src, [], [bdst])

    def load_cvt(dst, bdst, src, L, post=None, scale=None):
        c0 = 0
        while c0 < L:
            n = min(2048, L - c0)
            h = nxt("stage", 2)
            stg = score[:, h * 2048:h * 2048 + n]
            dma(d_stage[h], stg, src[:, c0:c0 + n], [], [b_sc[h]])
            if post is None:
                cast_copy(nxt("cvt", 3), dst[:, c0:c0 + n], stg, [b_sc[h]], [bdst], scale=scale)
            else:
                post(dst[:, c0:c0 + n], stg, c0, n, h)
            c0 += n

    load_cvt(ident4, b_id, ident4_d, 512)
    load_cvt(Emat, b_E, E_d, 256)
    load_cvt(sel, b_sel, sel_d, 7 * 128)
    load_direct("pow2", pow2[:], b_pow2, pow2_d)
    op("dve", "memset", [], [b_cst], cst[:, 0:1], -0.5)
    load_direct("tmpb", tmpb[:], b_tmpb, cC_d)
    for d, (dstt, bd) in enumerate(((biasC0, b_biasC0), (biasC1, b_biasC1))):
        def post(dst, stg, c0, n, h, bd=bd):
            op("dve", "tensor_tensor", [b_sc[h], b_tmpb], [bd], out=dst, in0=stg, in1=tmpb[:, c0:c0 + n], op=ALU.subtract)
        load_cvt(dstt, bd, biasC_d[d], 512, post=post)
    load_cvt(maskB4, b_maskB4, maskB_d[1], 512)
    for m in "BM":
        op("pool", "memset", [], [b_qblk[m]], qblk[m][:], 0.0)
    for k in range(2):
        op("pool", "memset", [], [b_qblkC[k]], qblkC[k][:], 0.0)
    op("pool", "memset", [], [b_iqblk], iqblk[:], 0.0)
    op("pool", "memset", [], b_vC, vC[:, :, :, 64:65], 1.0)
    op("pool", "memset", [], b_vB, vB[:, :, :, 64:65], 1.0)
    op("pool", "memset", [], [b_vM], vM[:, :, :, 64:65], 1.0)

    def layer_norm(X, bX, width, gt, bg_, bt, bb_, stt, bst, out=None, bout=None):
        nch = (width + 511) // 512
        cw = width // nch
        for c in range(nch):
            op("dve", "bn_stats", [bX], [bst], out=stt[:, 8 + 6 * c:14 + 6 * c], in_=X[:, c * cw:(c + 1) * cw])
        op("dve", "bn_aggr", [bst], [bst], out=stt[:, 0:2], in_=stt[:, 8:8 + 6 * nch])
        op("dve", "tensor_scalar", [bst], [bst], out=stt[:, 2:3], in0=stt[:, 1:2], scalar1=LN_EPS, scalar2=None, op0=ALU.add)
        op("pool", "tensor_tensor", [bst, b_cst], [bst], out=stt[:, 3:4], in0=stt[:, 2:3], in1=cst[:, 0:1], op=ALU.pow)
        op("dve", "tensor_scalar", [bst], [bst], out=stt[:, 4:5], in0=stt[:, 0:1], scalar1=stt[:, 3:4], scalar2=-1.0, op0=ALU.mult, op1=ALU.mult)
        op("act", "activation", [bX, bst], [bX], out=X, in_=X, func=AF.Identity, bias=stt[:, 4:5], scale=stt[:, 3:4])
        op("dve", "tensor_tensor", [bX, bg_], [bX], out=X, in0=X, in1=gt, op=ALU.mult)
        if out is None:
            op("pool", "tensor_tensor", [bX, bb_], [bX], out=X, in0=X, in1=bt, op=ALU.add)
        else:
            op("pool", "tensor_tensor", [bX, bb_], [bout], out=out, in0=X, in1=bt, op=ALU.add)

    def attention(mixer, q, bq, keys, mixed_t, bmixed, mixed_off, gate_ap, bgate, bank_fn, PTs, bPTs, ptkey):
        oacc = acc[mixer]
        bacc = b_acc[mixer]
        nk = len(keys)
        pend = []
        npt = len(PTs)

        def stage1(jj):
            (k0, k1, rk_, vfn, rv_, extra) = keys[jj]
            ST, bST = bank_fn()
            mm(ST[:, 0:256], k0, q[:, 0, :], True, rk_ + [bq], [bST])
            mm(ST[:, 256:512], k1, q[:, 1, :], False, rk_ + [bq], [bST])
            for (l_, r_, rd_) in extra:
                mm(ST[:, :], l_, r_, False, rd_, [bST])
            pk = nxt(ptkey, npt)
            op("act", "activation", [bST], [bPTs[pk]], out=PTs[pk][:], in_=ST[:, :], func=AF.Exp)
            pend.append((jj, pk))

        def stage2():
            jj, pk = pend.pop(0)
            (k0, k1, rk_, vfn, rv_, extra) = keys[jj]
            for h in range(4):
                mm(oacc[:, h * 65:(h + 1) * 65], PTs[pk][:, h * 128:(h + 1) * 128], vfn(h), (jj == 0 and h == 0), [bPTs[pk]] + rv_, [bacc])

        look = min(2, npt)
        for jj in range(nk):
            stage1(jj)
            if len(pend) >= look:
                stage2()
            yield
        while pend:
            stage2()
        ov = oacc[:, 0:260].rearrange("p (h d) -> p h d", d=65)
        rec = recs[mixer]
        op("dve", "reciprocal", [bacc], [b_recs[mixer]], out=rec[:, 0:4], in_=ov[:, :, 64])
        for h in range(4):
            op("dve", "scalar_tensor_tensor", [bacc, b_recs[mixer], bgate], [bmixed],
               out=mixed_t[:, mixed_off + h * 64:mixed_off + (h + 1) * 64], in0=oacc[:, h * 65:h * 65 + 64],
               scalar=rec[:, h:h + 1], in1=gate_ap[:, h * 64:(h + 1) * 64], op0=ALU.mult, op1=ALU.mult)
        yield

    load_direct("lng", lng[:], b_lng, lnin_d[0])
    load_direct("lnb", lnb[:], b_lnb, lnin_d[1])
    last_store = [None, None]
    for i in range(nblk):
        s_ = i % 2
        X = xin[s_]
        dma(d_x[s_], X[:], x_d[i * 128:(i + 1) * 128, :], [], [b_x[s_]])
        layer_norm(X[:], b_x[s_], 1024, lng[:], b_lng, lnb[:], b_lnb, stB, b_stB)
        last_store[s_] = dma(d_o[s_], out_d[i * 128:(i + 1) * 128, :], X[:], [b_x[s_]], [b_o[i]])

    def setup_layer(l):
        load_cvt(wtm, b_wtm, wtm_d[l], 8 * NTM)
        load_cvt(wfm, b_wfm, wfm_d[l], 8 * NFM)
        load_cvt(wout, b_wout, wout_d[l], 8 * 1024, scale=0.5)
        load_cvt(brow, b_brow, brow_d[l], 512)
        load_cvt(wsT, b_wsT, wsT_d[l], 512)
        op("pool", "memset", [], [b_wsT], wsT[64:128, :].rearrange("p (g i) -> p g i", g=4)[:, :, 0:64], 0.0)
        load_cvt(absT, b_absT, absT_d[l], 128)
        load_direct("bfm", bfm[:], b_bfm, bfm_d[l])
        op("dve", "tensor_scalar", [b_bfm], [b_bfm8], out=bfm8[:], in0=bfm[:], scalar1=0.125, scalar2=None, op0=ALU.mult)
        load_direct("lng", lng[:], b_lng, lng_d[l, 0])
        load_direct("lnb", lnb[:], b_lnb, lng_d[l, 1])
        load_direct("alng", alng[:], b_alng, alng_d[l, 0])
        load_direct("alnb", alnb[:], b_alnb, alng_d[l, 1])
        load_direct("tmpb", tmpb[:], b_tmpb, cB_d[l])
        for m_, (dstt, bd) in enumerate(((biasB0, b_biasB0), (biasB1, b_biasB1))):
            def post(dst, stg, c0, n, h, bd=bd):
                op("dve", "tensor_tensor", [b_sc[h], b_tmpb], [bd], out=dst, in0=stg, in1=tmpb[:, c0:c0 + n], op=ALU.subtract)
            load_cvt(dstt, bd, biasB_d[l, m_], 512, post=post)
        load_direct("tmpb", tmpb[:], b_tmpb, maskB_d[0])
        op("dve", "tensor_tensor", [b_biasB0, b_tmpb], [b_biasB0], out=biasB0[:], in0=biasB0[:], in1=tmpb[:], op=ALU.add)
        for mt in range(2):
            X = xin[mt]
            dma(d_x[mt], X[:], mem_d[mt * 128:(mt + 1) * 128, :], [], [b_x[mt]])
            op("act", "copy", [b_x[mt]], [b_mixed[0]], out=mixed[0][:], in_=X[:])
            tb, btb = bankA()
            tv = tb[:, :].bitcast(BF16)
            for kt in range(8):
                transpose(tv[:, kt * 128:(kt + 1) * 128], mixed[0][:, kt * 128:(kt + 1) * 128], [b_mixed[0]], [btb])
            op("dve", "tensor_copy", [btb], b_rt, out=memT[:, :, mt * 128:(mt + 1) * 128], in_=tv.rearrange("p (a b) -> p a b", b=128))
        load_cvt(wmb, b_wmb, wmk_d[l], 8 * 256)
        for t in range(2):
            bk, bbk = bankA()
            for kt in range(8):
                mm(bk[:, 0:256], wmb[:, kt * 256 + t * 128:kt * 256 + (t + 1) * 128], memT[:, kt, :], kt == 0, [b_wmb] + b_rt, [bbk])
            op("dve", "tensor_copy", [bbk], [b_kmT], out=kmT[:, t, :], in_=bk[:, 0:256])
        load_cvt(wmb, b_wmb, wmv_d[l], 8 * 256)
        for mt in range(2):
            bk, bbk = bankA()
            for kt in range(8):
                mm(bk[:, 0:256], memT[:, kt, mt * 128:(mt + 1) * 128], wmb[:, kt * 256:(kt + 1) * 256], kt == 0, [b_wmb] + b_rt, [bbk])
            op("dve", "tensor_copy", [bbk], [b_vM], out=vM[:, mt, :, 0:64], in_=bk[:, 0:256].rearrange("p (h d) -> p h d", d=64))

    lo, hi, full = slice(0, 64), slice(64, 128), slice(0, 128)

    def ev(out, in0, prt, ct, scale, reads, writes):
        if ct < 10:
            if scale is None:
                op("act", "activation", reads, writes, out=out, in_=in0, func=AF.Identity, bias=bfm[prt, ct:ct + 1], scale=1.0)
            else:
                op("act", "activation", reads + [b_bfm8], writes, out=out, in_=in0, func=AF.Identity, bias=bfm8[prt, ct:ct + 1], scale=scale)
            return
        if scale is None:
            op("dve", "tensor_scalar", reads, writes, out=out, in0=in0, scalar1=bfm[prt, ct:ct + 1], scalar2=None, op0=ALU.add)
        else:
            op("dve", "tensor_scalar", reads, writes, out=out, in0=in0, scalar1=bfm[prt, ct:ct + 1], scalar2=scale, op0=ALU.add, op1=ALU.mult)

    def fm_group(i, cts, bank_fn):
        p_ = i % 2
        sl = i % 5
        bk, bbk = bank_fn()
        first = True
        for ci, ct in enumerate(cts):
            for kt in range(8):
                mm(bk[:, ci * 128:(ci + 1) * 128], wfm[:, kt * NFM + ct * 128:kt * NFM + (ct + 1) * 128], xnT[:, kt, :], first, [b_wfm, b_xnT], [bbk])
                first = False
        for ci, ct in enumerate(cts):
            pst = bk[:, ci * 128:(ci + 1) * 128]
            rd = [bbk, b_bfm]
            if ct in (0, 1, 8, 9):
                m = "B" if ct < 2 else "M"
                t = ct % 2
                ev(qblk[m][lo, t, 0:128], pst[lo, :], lo, ct, 0.125, rd, [b_qblk[m]])
                ev(qblk[m][hi, t, 128:256], pst[hi, :], hi, ct, 0.125, rd, [b_qblk[m]])
            elif ct in (4, 5):
                t = ct % 2
                ev(qblkC[p_][lo, t, 0:128], pst[lo, :], lo, ct, 0.125, rd, [b_qblkC[p_]])
                ev(qblkC[p_][hi, t, 128:256], pst[hi, :], hi, ct, 0.125, rd, [b_qblkC[p_]])
            elif ct in (2, 3):
                ev(kBT[:, ct - 2, sl * 128:(sl + 1) * 128], pst[:, :], full, ct, None, rd, [b_kB[sl]])
            elif ct in (6, 7):
                ev(kCT[:, ct - 6, i * 128:(i + 1) * 128], pst[:, :], full, ct, None, rd, [b_kC[i]])
            elif ct in (10, 11, 12, 13):
                t = ct - 10
                ev(iqblk[lo, 2 * t, :], pst[lo, :], lo, ct, None, rd, [b_iqblk])
                ev(iqblk[hi, 2 * t + 1, :], pst[hi, :], hi, ct, None, rd, [b_iqblk])
            else:
                ev(ikT[:, i * 128:(i + 1) * 128], pst[:, :], full, ct, None, rd, [b_ik[i]])

    def tm_tile(r, c0, n, bank_fn):
        bk, bbk = bank_fn()
        for kt in range(8):
            mm(bk[:, 0:n], xnT[:, kt, :], wtm[:, kt * NTM + c0:kt * NTM + c0 + n], kt == 0, [b_xnT, b_wtm], [bbk])
        mm(bk[:, 0:n], sel[:, r * 128:(r + 1) * 128], brow[:, 0:n], False, [b_sel, b_brow], [bbk])
        return bk, bbk

    def phaseMain(i):
        s_ = i % 2
        p_ = i % 2
        X = xin[s_]
        N_i = (i + 1) * 128
        mx = mixed[p_]
        bmx = b_mixed[p_]
        dma(d_x[s_], X[:], out_d[i * 128:(i + 1) * 128, :], [b_o[i]], [b_x[s_]])
        op("act", "copy", [b_x[s_]], [bmx], out=mx[:], in_=X[:])
        tb, btb = bankA()
        tv = tb[:, :].bitcast(BF16)
        for kt in range(8):
            transpose(tv[:, kt * 128:(kt + 1) * 128], mx[:, kt * 128:(kt + 1) * 128], [bmx], [btb])
        op("dve", "tensor_copy", [btb], [b_xnT], out=xnT[:].rearrange("p a b -> p (a b)"), in_=tv)
        yield "XNT"
        fm_group(i, [10, 11, 12, 13], bankA)
        yield 0.2
        fm_group(i, [14], bankA)
        bk, bbk = tm_tile(4, 2048, 8, bankA)
        op("dve", "tensor_copy", [bbk], [b_iw], out=iw[:], in_=bk[:, 0:8])
        yield 0.5
        ntile = (N_i + 511) // 512
        for c in range(ntile):
            c0 = c * 512
            n = min(512, N_i - c0)
            half = [b_sc[0]] if c0 + n <= 2048 else [b_sc[1]]
            rik = [b_ik[jj] for jj in range(c0 // 128, (c0 + n) // 128)]
            for h in range(8):
                bk, bbk = bankA()
                mm(bk[:, 0:n], iqblk[:, h, :], ikT[:, c0:c0 + n], True, [b_iqblk] + rik, [bbk])
                if h == 0:
                    op("dve", "tensor_scalar", [bbk, b_iw], half, out=score[:, c0:c0 + n], in0=bk[:, 0:n], scalar1=0.0, scalar2=iw[:, 0:1], op0=ALU.max, op1=ALU.mult)
                else:
                    r_ = nxt("rtmp", 2)
                    op("act", "activation", [bbk], [b_rtmp[r_]], out=rtmp[r_][:, 0:n], in_=bk[:, 0:n], func=AF.Relu)
                    op("dve", "scalar_tensor_tensor", [b_rtmp[r_], b_iw] + half, half, out=score[:, c0:c0 + n], in0=rtmp[r_][:, 0:n], scalar=iw[:, h:h + 1],
                       in1=score[:, c0:c0 + n], op0=ALU.mult, op1=ALU.add)
                yield W_IDX * n / 512.0
        SC = b_sc if N_i > 2048 else [b_sc[0]]
        lasthalf = [b_sc[1]] if N_i > 2048 else [b_sc[0]]
        op("pool", "memset", [], lasthalf, score[0:64, N_i - 64:N_i], -1e30)
        if i >= 2:
            op("dve", "tensor_reduce", SC, [b_bis], out=bis[:, 0:1], in_=score[:, 0:N_i], axis=AX.X, op=ALU.max)
            op("dve", "tensor_reduce", SC, [b_bis], out=bis[:, 1:2], in_=score[:, 0:N_i - 64], axis=AX.X, op=ALU.min)
            op("dve", "tensor_tensor", [b_bis], [b_bis], out=bis[:, 2:3], in0=bis[:, 0:1], in1=bis[:, 1:2], op=ALU.subtract)
            op("dve", "tensor_scalar", [b_bis, b_pow2], [b_rk], out=rk[:, :], in0=pow2[:, :], scalar1=bis[:, 2:3], scalar2=None, op0=ALU.mult)
            op("dve", "tensor_tensor", [b_bis, b_rk], [b_bis], out=bis[:, 8:9], in0=bis[:, 1:2], in1=rk[:, 1:2], op=ALU.add)
            yield 1.0 + N_i / 500.0
            nd = max(64, int(round(N_i * DVE_FRAC / 64)) * 64, N_i - 2048)
            na = N_i - nd
            SCd = [b_sc[0]] if nd <= 2048 else b_sc
            SCa = [b_sc[1]] if nd >= 2048 else (b_sc if N_i > 2048 else [b_sc[0]])
            for k in range(1, NBIS + 1):
                mid = bis[:, 8 + (k - 1) % 2:9 + (k - 1) % 2]
                midn = bis[:, 8 + k % 2:9 + k % 2]
                op("dve", "tensor_scalar", SCd + [b_bis], [b_rt[0], b_cntD], out=junk[:, 0:nd], in0=score[:, 0:nd], scalar1=mid, scalar2=None,
                   op0=ALU.is_ge, op1=ALU.add, accum_out=cntD[:, 0:1])
                op("act", "activation", SCa + [b_bis], [b_rt[1], b_cntA], out=junk[:, 2048:2048 + na], in_=score[:, nd:N_i], func=AF.Sign, bias=mid, scale=-1.0,
                   accum_out=cntA[:, 0:1])
                op(SMALL_ENG, "tensor_scalar", [b_cntD, b_cntA], [b_bis], out=bis[:, 4:5], in0=cntD[:, 0:1], scalar1=2.0, scalar2=cntA[:, 0:1], op0=ALU.mult, op1=ALU.subtract)
                op(SMALL_ENG, "tensor_scalar", [b_bis, b_rk], [b_bis], out=bis[:, 5:6], in0=bis[:, 4:5], scalar1=float(512 - na), scalar2=rk[:, k:k + 1], op0=ALU.is_ge, op1=ALU.mult)
                kk = k + 1 if k < NBIS else k
                op(SMALL_ENG, "tensor_scalar", [b_bis, b_rk], [b_bis], out=midn, in0=bis[:, 5:6], scalar1=rk[:, kk:kk + 1], scalar2=mid, op0=ALU.subtract, op1=ALU.add)
                yield 1.2 + N_i / 1100.0
            thr = bis[:, 8 + NBIS % 2:9 + NBIS % 2]
        else:
            op("dve", "memset", [], [b_bis], bis[:, 12:13], -1e29)
            thr = bis[:, 12:13]
        yield "B_DONE"
        c0 = 0
        while c0 < N_i:
            n = min(2048, N_i - c0)
            op("dve", "tensor_scalar", [b_sc[c0 // 2048], b_bis], [b_mb], out=mb[:, c0:c0 + n], in0=score[:, c0:c0 + n], scalar1=thr, scalar2=NEG, op0=ALU.is_lt, op1=ALU.mult)
            c0 += n
        yield 0.0

    def main_weight(i):
        N_i = (i + 1) * 128
        w = 1.2 + W_IDX * 8 * N_i / 512.0
        if i >= 2:
            w += 1.0 + N_i / 500.0 + NBIS * (1.2 + N_i / 1100.0)
        return w

    def phaseSide(i):
        s_ = i % 2
        p_ = i % 2
        sl = i % 5
        mx = mixed[p_]
        bmx = b_mixed[p_]
        bk, bbk = tm_tile(0, 0, 512, bankS)
        op("act", "copy", [bbk], [b_xg], out=xg[:], in_=bk[:, :])
        op("pool", "tensor_tensor", [b_xg], [b_gs], out=gsc0[:], in0=xg[:], in1=xg[:], op=ALU.mult)
        op("dve", "scalar_tensor_tensor", [b_gs, b_xg], [b_gs], out=gsc0[:], in0=gsc0[:], scalar=0.044715, in1=xg[:], op0=ALU.mult, op1=ALU.mult)
        op("pool", "tensor_tensor", [b_gs, b_xg], [b_gs], out=gsc0[:], in0=gsc0[:], in1=xg[:], op=ALU.add)
        op("act", "activation", [b_gs], [b_gs], out=gsc0[:], in_=gsc0[:], func=AF.Tanh, scale=0.7978845608028654)
        op("dve", "scalar_tensor_tensor", [b_gs, b_xg], [b_xg], out=xg[:], in0=gsc0[:], scalar=1.0, in1=xg[:], op0=ALU.add, op1=ALU.mult)
        yield
        bk, bbk = tm_tile(1, 512, 512, bankS)
        op("act", "activation", [bbk], [b_gs], out=gsc0[:], in_=bk[:, :], func=AF.Tanh, scale=0.5)
        op("dve", "scalar_tensor_tensor", [b_gs, bbk], [b_gates], out=gates[:, 0:512], in0=gsc0[:], scalar=1.0, in1=bk[:, :], op0=ALU.add, op1=ALU.mult)
        yield
        bk, bbk = tm_tile(2, 1024, 512, bankS)
        op("act", "activation", [bbk], [b_gs], out=gsc0[:], in_=bk[:, :], func=AF.Tanh, scale=0.5)
        op("dve", "scalar_tensor_tensor", [b_gs, bbk], [b_gatesC[p_]], out=gatesC[p_][:, :], in0=gsc0[:, 0:256], scalar=1.0, in1=bk[:, 0:256], op0=ALU.add, op1=ALU.mult)
        op("dve", "scalar_tensor_tensor", [b_gs, bbk], [b_gates], out=gates[:, 512:768], in0=gsc0[:, 256:512], scalar=1.0, in1=bk[:, 256:512], op0=ALU.add, op1=ALU.mult)
        yield
        bk, bbk = tm_tile(3, 1536, 512, bankS)
        op("act", "copy", [bbk], [b_vB[sl]], out=vB[:, sl, :, 0:64], in_=bk[:, 0:256].rearrange("p (h d) -> p h d", d=64))
        op("act", "copy", [bbk], [b_vC[i]], out=vC[:, i, :, 0:64], in_=bk[:, 256:512].rearrange("p (h d) -> p h d", d=64))
        yield
        for cts in ([0, 1, 2, 3], [4, 5, 6, 7], [8, 9]):
            fm_group(i, cts, bankS)
            yield
        op("act", "mul", [b_xg], [b_xg], out=xg[:, 256:512], in_=xg[:, 256:512], mul=0.5)
        layer_norm(xg[:, 256:512], b_xg, 256, alng[:], b_alng, alnb[:], b_alnb, stA, b_stA, out=vln[:], bout=b_vln)
        bk, bbk = bankS()
        for g in range(4):
            mm(bk[:, g * 64:(g + 1) * 64], wsT[:, g * 128:(g + 1) * 128], vln[:, g * 64:(g + 1) * 64], g == 0, [b_wsT, b_vln], [bbk])
        mm(bk[:, 0:256], absT[:, :], Emat[:, :], False, [b_absT, b_E], [bbk])
        op("dve", "scalar_tensor_tensor", [bbk, b_xg], [b_xg], out=xg[:, 256:512], in0=bk[:, 0:256], scalar=0.5, in1=xg[:, 0:256], op0=ALU.mult, op1=ALU.mult)
        op("pool", "tensor_tensor", [b_xg, b_gates], [bmx], out=mx[:, 0:256], in0=xg[:, 256:512], in1=gates[:, 0:256], op=ALU.mult)
        yield
        keys = []
        for j in range(max(0, i - 4), i + 1):
            sj = j % 5
            m_ = i - j
            extra = []
            if m_ == 0:
                extra.append((ident4[:, 0:128], biasB0[:, :], [b_id, b_biasB0]))
            elif m_ == 1:
                extra.append((ident4[:, 0:128], biasB1[:, :], [b_id, b_biasB1]))
            elif m_ == 4:
                extra.append((ident4[:, 0:128], maskB4[:, :], [b_id, b_maskB4]))
            keys.append((kBT[:, 0, sj * 128:(sj + 1) * 128], kBT[:, 1, sj * 128:(sj + 1) * 128], [b_kB[sj]],
                         (lambda h, sj=sj: vB[:, sj, h, :]), [b_vB[sj]], extra))
        yield from attention("B", qblk["B"], b_qblk["B"], keys, mx, bmx, 256, gates[:, 256:512], b_gates, bankS, PTA, b_PTA, "PTA")
        keys = []
        for mt in range(2):
            keys.append((kmT[:, 0, mt * 128:(mt + 1) * 128], kmT[:, 1, mt * 128:(mt + 1) * 128], [b_kmT],
                         (lambda h, mt=mt: vM[:, mt, h, :]), [b_vM], []))
        yield from attention("M", qblk["M"], b_qblk["M"], keys, mx, bmx, 768, gates[:, 512:768], b_gates, bankS, PTA, b_PTA, "PTA")

    def phaseB(i, l):
        s_ = i % 2
        p_ = i % 2
        X = xin[s_]
        mx = mixed[p_]
        bmx = b_mixed[p_]
        keys = []
        for j in range(0, i + 1):
            extra = [(mb[:, j * 128:(j + 1) * 128], ident4[:, :], [b_mb, b_id])]
            d_ = i - j
            if d_ == 0:
                extra.append((ident4[:, 0:128], biasC0[:, :], [b_id, b_biasC0]))
            elif d_ == 1:
                extra.append((ident4[:, 0:128], biasC1[:, :], [b_id, b_biasC1]))
            keys.append((kCT[:, 0, j * 128:(j + 1) * 128], kCT[:, 1, j * 128:(j + 1) * 128], [b_kC[j]],
                         (lambda h, j=j: vC[:, j, h, :]), [b_vC[j]], extra))
        yield from attention("C", qblkC[p_], b_qblkC[p_], keys, mx, bmx, 512, gatesC[p_][:, :], b_gatesC[p_], bankC, PTC, b_PTC, "PTC")
        tb, btb = bankC()
        tv = tb[:, :].bitcast(BF16)
        for kt in range(8):
            transpose(tv[:, kt * 128:(kt + 1) * 128], mx[:, kt * 128:(kt + 1) * 128], [bmx], [btb])
        op("dve", "tensor_copy", [btb], [b_mixedT], out=mixedT[:].rearrange("p a b -> p (a b)"), in_=tv)
        yield
        for c in range(2):
            bk, bbk = bankC()
            for kt in range(8):
                mm(bk[:, :], mixedT[:, kt, :], wout[:, kt * 1024 + c * 512:kt * 1024 + (c + 1) * 512], kt == 0, [b_mixedT, b_wout], [bbk])
            mm(bk[:, :], sel[:, (5 + c) * 128:(6 + c) * 128], brow[:, :], False, [b_sel, b_brow], [bbk])
            op("dve", "scalar_tensor_tensor", [b_x[s_], bbk], [b_x[s_]], out=X[:, c * 512:(c + 1) * 512], in0=X[:, c * 512:(c + 1) * 512], scalar=ALPHA, in1=bk[:, :],
               op0=ALU.mult, op1=ALU.add)
            yield
        layer_norm(X[:], b_x[s_], 1024, lng[:], b_lng, lnb[:], b_lnb, stB, b_stB)
        dma(d_o[s_], out_d[i * 128:(i + 1) * 128, :], X[:], [b_x[s_]], [b_o[i]], holder=(last_store, s_))
        yield

    def drain(g):
        for _ in g:
            pass

    class Stream:
        def __init__(self, gen, prio):
            self.gen = gen
            self.q = []
            self.done = False
            self.prio = prio
            self.active = True

    eng_free = {e: 0.0 for e in ENGS}
    tok_done = {}

    def fill(st):
        while not st.q and not st.done:
            cur_q[0] = st.q
            try:
                y = next(st.gen)
                if isinstance(y, str):
                    st.q.append(y)
            except StopIteration:
                st.done = True
            finally:
                cur_q[0] = None

    def ready_time(eng, reads, writes):
        t = 0.0
        for bf in reads:
            if bf.writer is not None:
                t = max(t, tok_done.get((id(bf.writer[0]), bf.writer[1]), 0.0) + (0.0 if bf.writer[2] == eng else 0.12))
            if bf.psum:
                for tk in bf.readers.values():
                    if tk[2] != eng:
                        t = max(t, tok_done.get((id(tk[0]), tk[1]), 0.0) + 0.12)
        for bf in writes:
            if bf.writer is not None:
                t = max(t, tok_done.get((id(bf.writer[0]), bf.writer[1]), 0.0) + (0.0 if bf.writer[2] == eng else 0.12))
            for tk in bf.readers.values():
                if tk[2] != eng:
                    t = max(t, tok_done.get((id(tk[0]), tk[1]), 0.0) + 0.12)
        return t

    def commit(item, start):
        eng, fn, reads, writes, dur, ds, holder = item
        if ds is not None:
            tok = S.dma(eng, ds, fn, reads=reads, writes=writes)
            if holder is not None:
                holder[0][holder[1]] = tok
            eng_free[eng] = start + 0.1
        else:
            tok = S.op(eng, fn, reads=reads, writes=writes)
            eng_free[eng] = start + dur
        tok_done[(id(tok[0]), tok[1])] = start + dur

    def run_round(gmain, gside, gB):
        main = Stream(gmain, 0.0)
        side = Stream(gside, 0.0) if gside is not None else None
        stB = Stream(gB, 0.0) if gB is not None else None
        if side is not None:
            side.active = False
        streams = [x for x in (main, stB, side) if x is not None]
        while True:
            best = None
            alive = False
            for st in streams:
                fill(st)
                if not st.q:
                    continue
                alive = True
                if not st.active:
                    continue
                item = st.q[0]
                if isinstance(item, str):
                    if item == "XNT":
                        st.q.pop(0)
                        if side is not None:
                            side.active = True
                        best = "again"
                        break
                    if item == "B_DONE":
                        if stB is None or (stB.done and not stB.q):
                            st.q.pop(0)
                            best = "again"
                            break
                        continue
                    st.q.pop(0)
                    best = "again"
                    break
                eng = item[0]
                start = max(eng_free[eng], ready_time(eng, item[2], item[3]))
                key = start - st.prio
                if best is None or key < best[0]:
                    best = (key, start, st)
            if best == "again":
                continue
            if best is None:
                if not alive:
                    break
                raise RuntimeError("scheduler stuck")
            _, start, st = best
            commit(st.q.pop(0), start)

    for l in range(n_layers):
        setup_layer(l)
        if not pipeline:
            for i in range(nblk):
                drain(phaseMain(i))
                drain(phaseSide(i))
                drain(phaseB(i, l))
        else:
            run_round(phaseMain(0), phaseSide(0), None)
            for i in range(nblk):
                if i + 1 < nblk:
                    run_round(phaseMain(i + 1), phaseSide(i + 1), phaseB(i, l))
                else:
                    drain(phaseB(i, l))

    S.wait_tokens("sp", [t for t in last_store if t is not None])
    S.emit()
    es.close()
    return nc


def _t5_bucket_idx(rel):
    nb = 16
    max_exact = 8
    ret = np.where(rel > 0, nb, 0)
    n = np.abs(rel)
    nf = np.maximum(n, 1).astype(np.float32)
    large = max_exact + (np.log(nf / np.float32(max_exact)) / np.float32(np.log(128 / max_exact)) * (nb - max_exact)).astype(np.int32)
    large = np.minimum(large, nb - 1)
    return ret + np.where(n < max_exact, n, large)


def _kt_layout(w):
    C = w.shape[1]
    return np.ascontiguousarray(w.reshape(8, 128, C).transpose(1, 0, 2).reshape(128, 8 * C))


def prepare(inputs):
    f = np.float32
    cum = np.cumsum((0,) + SPLITS)
    rng = {n: (int(cum[k]), int(cum[k + 1])) for k, n in enumerate(NAMES)}
    tm_cols = np.concatenate([np.arange(*rng[n]) for n in TM_ORDER])
    fm_cols = np.concatenate([np.arange(*rng[n]) for n in FM_ORDER])
    w_in = np.asarray(inputs["w_in"], f)
    b_in = np.asarray(inputs["b_in"], f)
    sh = {}
    sh["wtm"] = np.stack([_kt_layout(w_in[l][:, tm_cols]) for l in range(DEPTH)])
    sh["wfm"] = np.stack([_kt_layout(w_in[l][:, fm_cols]) for l in range(DEPTH)])
    sh["wout"] = np.stack([_kt_layout(np.asarray(inputs["w_out"], f)[l]) for l in range(DEPTH)])
    wm = np.asarray(inputs["w_mem_kv"], f)
    sh["wmk"] = np.stack([_kt_layout(wm[l][:, 0:256]) for l in range(DEPTH)])
    sh["wmv"] = np.stack([_kt_layout(wm[l][:, 256:512]) for l in range(DEPTH)])
    brow = np.zeros((DEPTH, 128, 512), f)
    btm = b_in[:, tm_cols]
    for r in range(4):
        brow[:, r, :] = btm[:, r * 512:(r + 1) * 512]
    brow[:, 4, 0:8] = btm[:, 2048:2056]
    b_out = np.asarray(inputs["b_out"], f)
    brow[:, 5, :] = b_out[:, 0:512]
    brow[:, 6, :] = b_out[:, 512:1024]
    sh["brow"] = brow
    sel = np.zeros((128, 7, 128), f)
    for r in range(7):
        sel[r, r, :] = 1.0
    sh["sel"] = sel.reshape(128, 7 * 128)
    sh["bfm"] = np.ascontiguousarray(b_in[:, fm_cols].reshape(DEPTH, 15, 128).transpose(0, 2, 1))
    sh["lnin"] = np.stack([np.broadcast_to(np.asarray(inputs["ln_in_g"], f), (128, 1024)),
                           np.broadcast_to(np.asarray(inputs["ln_in_b"], f), (128, 1024))]).copy()
    sh["lng"] = np.stack([np.stack([np.broadcast_to(np.asarray(inputs["ln_g"], f)[l], (128, 1024)),
                                    np.broadcast_to(np.asarray(inputs["ln_b"], f)[l], (128, 1024))]) for l in range(DEPTH)]).copy()
    sh["alng"] = np.stack([np.stack([np.broadcast_to(np.asarray(inputs["a_ln_g"], f)[l], (128, 256)),
                                     np.broadcast_to(np.asarray(inputs["a_ln_b"], f)[l], (128, 256))]) for l in range(DEPTH)]).copy()
    a_ws = np.asarray(inputs["a_ws"], f)
    sh["wsT"] = np.ascontiguousarray(a_ws.transpose(0, 3, 1, 2).reshape(DEPTH, 128, 512))
    ab = np.zeros((DEPTH, 128, 128), f)
    ab[:, 0:4, :] = np.asarray(inputs["a_bs"], f)
    sh["absT"] = ab
    b_rel = np.asarray(inputs["b_rel"], f)
    a = np.arange(128)[:, None]
    b = np.arange(128)[None, :]
    bB = np.zeros((DEPTH, 2, 128, 4, 128), f)
    for m in range(2):
        idx = np.clip(128 * m + b - a, -128, 128) + 128
        for l in range(DEPTH):
            bB[l, m] = b_rel[l][:, idx].transpose(1, 0, 2)
    sh["biasB"] = bB.reshape(DEPTH, 2, 128, 512)
    sh["cB"] = np.ascontiguousarray(np.broadcast_to(b_rel[:, None, :, 256, None], (DEPTH, 128, 4, 128)).reshape(DEPTH, 128, 512))
    t5 = np.asarray(inputs["t5_table"], f)
    bC = np.zeros((2, 128, 4, 128), f)
    for d in range(2):
        relkq = -128 * d + a - b
        bi = _t5_bucket_idx(relkq)
        bC[d] = t5[bi].transpose(0, 2, 1)
    sh["biasC"] = bC.reshape(2, 128, 512)
    sh["cC"] = np.ascontiguousarray(np.broadcast_to(t5[15][None, :, None], (128, 4, 128)).reshape(128, 512))
    mB = np.zeros((2, 128, 4, 128), f)
    mB[0][64:128, :, 0:64] = NEG
    mB[1][0:64, :, 64:128] = NEG
    sh["maskB"] = mB.reshape(2, 128, 512)
    E = np.zeros((128, 256), f)
    for g in range(4):
        E[g, g * 64:(g + 1) * 64] = 1.0
    sh["Emat"] = E
    sh["ident4"] = np.tile(np.eye(128, dtype=f), (1, 4))
    sh["pow2"] = np.broadcast_to((2.0 ** -np.arange(NBIS + 2)).astype(f)[None, :], (128, NBIS + 2)).copy()
    return sh


_NC_CACHE = {}


def kernel(**inputs):
    sh = prepare(inputs)
    x = np.asarray(inputs["x"], np.float32)
    mem = np.asarray(inputs["mem"], np.float32)
    if "nc" not in _NC_CACHE:
        _NC_CACHE["nc"] = build_program()
    nc = _NC_CACHE["nc"]
    in_maps = []
    for c in range(BATCH):
        m = dict(sh)
        m["x"] = np.ascontiguousarray(x[c])
        m["mem"] = np.ascontiguousarray(mem[c])
        in_maps.append(m)
    res = run_bass_kernel_spmd(nc, in_maps, core_ids=list(range(BATCH)))
    return np.stack([np.asarray(r["out"]) for r in res.results]).astype(np.float32)
```

```python
import numpy as np
import concourse.bass as bass
import concourse.mybir as mybir
from concourse.bass_utils import run_bass_kernel_spmd
from contextlib import ExitStack

F32 = mybir.dt.float32
BF16 = mybir.dt.bfloat16
U8 = mybir.dt.uint8
ALU = mybir.AluOpType
AF = mybir.ActivationFunctionType
AX = mybir.AxisListType

D_MODEL = 1024
BATCH = 8
SEQ = 4096
DEPTH = 2
NBLK = SEQ // 128
GW = 256
SPLITS = (GW, GW, GW, GW, GW, GW, GW, GW, GW, GW, GW, 512, 64, 8, GW, GW)
NAMES = ("a_u", "a_v", "a_g", "bq", "bk", "bv", "bg", "cq", "ck", "cv", "cg", "iq", "ik", "iw", "mq", "mg")
TM_ORDER = ("a_u", "a_v", "a_g", "bg", "cg", "mg", "bv", "cv", "iw")
FM_ORDER = ("bq", "bk", "cq", "ck", "mq", "iq", "ik", "ik")
NTM = 2056
NFM = 1920
ALPHA = (2 * DEPTH) ** 0.25
LN_EPS = 1e-5
NBIS = 16
DVE_FRAC = 0.35
W_IDX = 0.12
SMALL_ENG = "pool"
NEG = -30000.0

ENGS = ("pe", "act", "dve", "pool", "sp")
STRICT_WAR = False


class Buf:
    __slots__ = ("name", "writer", "readers", "psum")

    def __init__(self, name="", psum=False):
        self.name = name
        self.writer = None
        self.readers = {}
        self.psum = psum


class DmaSem:
    def __init__(self, sem):
        self.sem = sem
        self.count = 0


class Sched:
    EPOCH = 12000

    def __init__(self, nc, es):
        self.nc = nc
        self.es = es
        self.prog = {e: [] for e in ENGS}
        self.cnt = {e: 0 for e in ENGS}
        self.nsem = 0
        self.cursem = {e: self._newsem(e) for e in ENGS}
        self.waited = {e: {} for e in ENGS}
        self.semobj = {}

    def _newsem(self, name):
        self.nsem += 1
        return self.es.enter_context(self.nc.semaphore(f"s_{name}_{self.nsem}"))

    def dma_sem(self, name):
        return DmaSem(self._newsem("d" + name))

    def _deps(self, eng, reads, writes):
        need = {}

        def add(tok, kind):
            if tok is None:
                return
            sem, val, teng = tok
            if teng == eng:
                if eng == "pe" or (kind == "war" and not STRICT_WAR):
                    return
            k = id(sem)
            self.semobj[k] = sem
            if need.get(k, 0) < val:
                need[k] = val

        for b in reads:
            add(b.writer, "raw")
            if b.psum:
                for t in b.readers.values():
                    if t[2] != eng:
                        add(t, "rar")
        for b in writes:
            add(b.writer, "waw")
            for t in b.readers.values():
                add(t, "war")
        waits = []
        w = self.waited[eng]
        for k, val in need.items():
            if w.get(k, 0) < val:
                w[k] = val
                waits.append((self.semobj[k], val))
        return waits

    def _mark(self, tok, reads, writes):
        k = id(tok[0])
        for b in reads:
            b.readers[k] = tok
        for b in writes:
            b.writer = tok
            b.readers = {}

    def op(self, eng, fn, reads=(), writes=()):
        waits = self._deps(eng, reads, writes)
        if self.cnt[eng] >= self.EPOCH:
            self.cursem[eng] = self._newsem(eng)
            self.cnt[eng] = 0
        self.cnt[eng] += 1
        tok = (self.cursem[eng], self.cnt[eng], eng)
        self.prog[eng].append((waits, fn, (self.cursem[eng], 1)))
        self._mark(tok, reads, writes)
        return tok

    def dma(self, eng, ds, fn, reads=(), writes=()):
        waits = self._deps(eng, reads, writes)
        ds.count += 16
        tok = (ds.sem, ds.count, "dma")
        self.prog[eng].append((waits, fn, (ds.sem, 16)))
        self._mark(tok, reads, writes)
        return tok

    def wait_tokens(self, eng, toks):
        self.prog[eng].append(([(t[0], t[1]) for t in toks], None, None))

    def emit(self):
        nc = self.nc
        engobj = {"pe": nc.tensor, "act": nc.scalar, "dve": nc.vector, "pool": nc.gpsimd, "sp": nc.sync}

        def replay(e):
            eo = engobj[e]
            for waits, fn, inc in self.prog[e]:
                for sem, val in waits:
                    eo.wait_ge(sem, val)
                if fn is not None:
                    fn().then_inc(inc[0], inc[1])

        with nc.Block() as block:
            @block.tensor
            def _(x):
                replay("pe")

            @block.scalar
            def _(x):
                replay("act")

            @block.vector
            def _(x):
                replay("dve")

            @block.gpsimd
            def _(x):
                replay("pool")

            @block.sync
            def _(x):
                replay("sp")


def build_program(n_layers=DEPTH, nblk=NBLK, pipeline=True):
    nc = bass.Bass("TRN2", target_bir_lowering=False)
    es = ExitStack()
    S = Sched(nc, es)

    def din(name, shape):
        return nc.dram_tensor(name, list(shape), F32, kind="ExternalInput").ap()

    x_d = din("x", [SEQ, D_MODEL])
    mem_d = din("mem", [256, D_MODEL])
    lnin_d = din("lnin", [2, 128, 1024])
    wtm_d = din("wtm", [DEPTH, 128, 8 * NTM])
    wfm_d = din("wfm", [DEPTH, 128, 8 * NFM])
    wout_d = din("wout", [DEPTH, 128, 8 * 1024])
    wmk_d = din("wmk", [DEPTH, 128, 8 * 256])
    wmv_d = din("wmv", [DEPTH, 128, 8 * 256])
    brow_d = din("brow", [DEPTH, 128, 512])
    sel_d = din("sel", [128, 7 * 128])
    bfm_d = din("bfm", [DEPTH, 128, 15])
    lng_d = din("lng", [DEPTH, 2, 128, 1024])
    alng_d = din("alng", [DEPTH, 2, 128, 256])
    wsT_d = din("wsT", [DEPTH, 128, 512])
    absT_d = din("absT", [DEPTH, 128, 128])
    biasB_d = din("biasB", [DEPTH, 2, 128, 512])
    cB_d = din("cB", [DEPTH, 128, 512])
    biasC_d = din("biasC", [2, 128, 512])
    cC_d = din("cC", [128, 512])
    maskB_d = din("maskB", [2, 128, 512])
    E_d = din("Emat", [128, 256])
    ident4_d = din("ident4", [128, 512])
    pow2_d = din("pow2", [128, NBIS + 2])
    out_d = nc.dram_tensor("out", [SEQ, D_MODEL], F32, kind="ExternalOutput").ap()
    b_o = [Buf() for _ in range(NBLK)]

    def sb(name, shape, dt):
        return es.enter_context(nc.sbuf_tensor(name, list(shape), dt))

    def ps(name, shape, dt):
        return es.enter_context(nc.psum_tensor(name, list(shape), dt))

    wtm = sb("wtm_s", [128, 8 * NTM], BF16); b_wtm = Buf()
    wfm = sb("wfm_s", [128, 8 * NFM], BF16); b_wfm = Buf()
    wout = sb("wout_s", [128, 8 * 1024], BF16); b_wout = Buf()
    brow = sb("brow_s", [128, 512], BF16); b_brow = Buf()
    sel = sb("sel_s", [128, 7 * 128], BF16); b_sel = Buf()
    bfm = sb("bfm_s", [128, 15], F32); b_bfm = Buf()
    bfm8 = sb("bfm8_s", [128, 15], F32); b_bfm8 = Buf()
    lng = sb("lng_s", [128, 1024], F32); b_lng = Buf()
    lnb = sb("lnb_s", [128, 1024], F32); b_lnb = Buf()
    alng = sb("alng_s", [128, 256], F32); b_alng = Buf()
    alnb = sb("alnb_s", [128, 256], F32); b_alnb = Buf()
    wsT = sb("wsT_s", [128, 512], BF16); b_wsT = Buf()
    absT = sb("absT_s", [128, 128], BF16); b_absT = Buf()
    Emat = sb("E_s", [128, 256], BF16); b_E = Buf()
    ident4 = sb("ident4_s", [128, 512], BF16); b_id = Buf()
    pow2 = sb("pow2_s", [128, NBIS + 2], F32); b_pow2 = Buf()
    biasB0 = sb("biasB0_s", [128, 512], BF16); biasB1 = sb("biasB1_s", [128, 512], BF16)
    maskB4 = sb("maskB4_s", [128, 512], BF16)
    biasC0 = sb("biasC0_s", [128, 512], BF16); biasC1 = sb("biasC1_s", [128, 512], BF16)
    b_biasB0, b_biasB1, b_maskB4, b_biasC0, b_biasC1 = Buf(), Buf(), Buf(), Buf(), Buf()

    kCT = sb("kCT_s", [128, 2, SEQ], BF16); b_kC = [Buf() for _ in range(NBLK)]
    vC = sb("vC_s", [128, NBLK, 4, 65], BF16); b_vC = [Buf() for _ in range(NBLK)]
    ikT = sb("ikT_s", [128, SEQ], BF16); b_ik = [Buf() for _ in range(NBLK)]
    kBT = sb("kBT_s", [128, 2, 5 * 128], BF16); b_kB = [Buf() for _ in range(5)]
    vB = sb("vB_s", [128, 5, 4, 65], BF16); b_vB = [Buf() for _ in range(5)]
    kmT = sb("kmT_s", [128, 2, 256], BF16); b_kmT = Buf()
    vM = sb("vM_s", [128, 2, 4, 65], BF16); b_vM = Buf()

    score = sb("score_s", [128, SEQ], F32); b_sc = [Buf(), Buf()]
    mb = sb("mb_s", [128, SEQ], BF16); b_mb = Buf()
    wmb = mb[:, 0:2048]; b_wmb = b_mb
    gsc0 = sb("gscr_s", [128, 512], F32); b_gs = Buf()
    rt = sb("rt_s", [128, 1024], F32); b_rt = [Buf(), Buf()]
    rtmp = [rt[:, 0:512], rt[:, 512:1024]]; b_rtmp = b_rt
    junk = rt[:, :].bitcast(U8)
    memT = rt[:, :].bitcast(BF16).rearrange("p (a b) -> p a b", b=256)
    cntD = sb("cntD_s", [128, 2], F32); b_cntD = Buf()
    cntA = sb("cntA_s", [128, 2], F32); b_cntA = Buf()
    xin = [sb(f"xin{k}_s", [128, 1024], F32) for k in range(2)]; b_x = [Buf(), Buf()]
    mixed = [sb(f"mixed{k}_s", [128, 1024], BF16) for k in range(2)]; b_mixed = [Buf(), Buf()]
    xnT = sb("xnT_s", [128, 8, 128], BF16); b_xnT = Buf()
    mixedT = sb("mixedT_s", [128, 8, 128], BF16); b_mixedT = Buf()
    xg = sb("xg_s", [128, 512], F32); b_xg = Buf()
    tmpb = xg; b_tmpb = b_xg
    gates = sb("gates_s", [128, 768], BF16); b_gates = Buf()
    gatesC = [sb(f"gatesC{k}_s", [128, 256], BF16) for k in range(2)]; b_gatesC = [Buf(), Buf()]
    vln = sb("vln_s", [128, 256], BF16); b_vln = Buf()
    qblk = {"B": sb("qblkB_s", [128, 2, 256], BF16), "M": sb("qblkM_s", [128, 2, 256], BF16)}
    b_qblk = {"B": Buf(), "M": Buf()}
    qblkC = [sb(f"qblkC{k}_s", [128, 2, 256], BF16) for k in range(2)]; b_qblkC = [Buf(), Buf()]
    iqblk = sb("iqblk_s", [128, 8, 128], BF16); b_iqblk = Buf()
    iw = sb("iw_s", [128, 8], F32); b_iw = Buf()
    PTA = [sb("PTA0_s", [128, 512], BF16)]; b_PTA = [Buf()]
    PTC = [sb(f"PTC{k}_s", [128, 512], BF16) for k in range(2)]; b_PTC = [Buf(), Buf()]
    stA = sb("stA_s", [128, 32], F32); b_stA = Buf()
    stB = sb("stB_s", [128, 32], F32); b_stB = Buf()
    recs = {m: sb(f"rec{m}_s", [128, 4], F32) for m in "BMC"}; b_recs = {m: Buf() for m in "BMC"}
    cst = sb("cst_s", [128, 4], F32); b_cst = Buf()
    bis = sb("bis_s", [128, 32], F32); b_bis = Buf()
    rk = sb("rk_s", [128, NBIS + 2], F32); b_rk = Buf()

    NA, NS, NC = 2, 2, 2
    bigA = [ps(f"bigA{k}", [128, 512], F32) for k in range(NA)]; b_bigA = [Buf(psum=True) for _ in range(NA)]
    bigS = [ps(f"bigS{k}", [128, 512], F32) for k in range(NS)]; b_bigS = [Buf(psum=True) for _ in range(NS)]
    bigC = [ps(f"bigC{k}", [128, 512], F32) for k in range(NC)]; b_bigC = [Buf(psum=True) for _ in range(NC)]
    _accS = ps("accS", [128, 512], F32); _baccS = Buf(psum=True)
    _accC = ps("accC", [128, 512], F32); _baccC = Buf(psum=True)
    acc = {"B": _accS, "M": _accS, "C": _accC}; b_acc = {"B": _baccS, "M": _baccS, "C": _baccC}
    rr = {"A": 0, "S": 0, "C": 0, "PTA": 0, "PTC": 0, "rtmp": 0, "cvt": 0, "stage": 0}

    def nxt(key, n):
        k = rr[key]
        rr[key] = (k + 1) % n
        return k

    def bankA():
        k = nxt("A", NA)
        return bigA[k], b_bigA[k]

    def bankS():
        k = nxt("S", NS)
        return bigS[k], b_bigS[k]

    def bankC():
        k = nxt("C", NC)
        return bigC[k], b_bigC[k]

    d_stage = [S.dma_sem("st0"), S.dma_sem("st1")]
    d_x = [S.dma_sem("x0"), S.dma_sem("x1")]
    d_o = [S.dma_sem("o0"), S.dma_sem("o1")]
    d_misc = {}

    def dsem(name):
        if name not in d_misc:
            d_misc[name] = S.dma_sem(name)
        return d_misc[name]

    V, A, G, T = nc.vector, nc.scalar, nc.gpsimd, nc.tensor
    ENG = {"dve": V, "act": A, "pool": G}

    cur_q = [None]

    def free_elems(ap):
        n = 1
        for d in tuple(ap.shape)[1:]:
            n *= int(d)
        return n

    def issue(eng, fn, reads, writes, dur, ds=None, holder=None):
        if cur_q[0] is not None:
            cur_q[0].append((eng, fn, list(reads), list(writes), dur, ds, holder))
            return None
        if ds is not None:
            tok = S.dma(eng, ds, fn, reads=reads, writes=writes)
            if holder is not None:
                holder[0][holder[1]] = tok
            return tok
        return S.op(eng, fn, reads=reads, writes=writes)

    def op(eng, name, reads, writes, *args, **kw):
        f = getattr(ENG[eng], name)
        o = kw.get("out", args[0] if args else None)
        n = free_elems(o) if o is not None else 1
        if eng == "dve":
            dur = 0.12 + n / 960.0 + (0.08 if "accum_out" in kw else 0.0)
        elif eng == "act":
            dur = 0.22 + n / 1200.0 + (0.1 if "accum_out" in kw else 0.0)
        else:
            dur = 0.3 + n / 480.0
        issue(eng, lambda: f(*args, **kw), reads, writes, dur)

    def mm(out, lhsT, rhs, start, reads, writes):
        dur = 0.1 + free_elems(rhs) / 1500.0
        issue("pe", lambda: T.matmul(out, lhsT, rhs, start=start, stop=True, skip_group_check=True), reads, writes, dur)

    def transpose(out, in_, reads, writes):
        idn = ident4[:, 0:128]
        issue("pe", lambda: T.transpose(out, in_, idn), reads + [b_id], writes, 0.2)

    def cast_copy(k, out, in_, reads, writes, scale=None):
        e = k % 3
        if scale is not None:
            if e == 1:
                op("act", "mul", reads, writes, out=out, in_=in_, mul=scale)
            else:
                op("dve" if e == 0 else "pool", "tensor_scalar", reads, writes, out=out, in0=in_, scalar1=scale, scalar2=None, op0=ALU.mult)
            return
        if e == 0:
            op("dve", "tensor_copy", reads, writes, out=out, in_=in_)
        elif e == 1:
            op("act", "copy", reads, writes, out=out, in_=in_)
        else:
            op("pool", "tensor_copy", reads, writes, out=out, in_=in_)

    def dma(ds, out, in_, reads, writes, holder=None):
        return issue("sp", lambda: nc.sync.dma_start(out=out, in_=in_), reads, writes, 3.0, ds=ds, holder=holder)

    def load_direct(name, dst, bdst, src):
        dma(dsem(name), dst, src, [], [bdst])

    def load_cvt(dst, bdst, src, L, post=None, scale=None):
        c0 = 0
        while c0 < L:
            n = min(2048, L - c0)
            h = nxt("stage", 2)
            stg = score[:, h * 2048:h * 2048 + n]
            dma(d_stage[h], stg, src[:, c0:c0 + n], [], [b_sc[h]])
            if post is None:
                cast_copy(nxt("cvt", 3), dst[:, c0:c0 + n], stg, [b_sc[h]], [bdst], scale=scale)
            else:
                post(dst[:, c0:c0 + n], stg, c0, n, h)
            c0 += n

    load_cvt(ident4, b_id, ident4_d, 512)
    load_cvt(Emat, b_E, E_d, 256)
    load_cvt(sel, b_sel, sel_d, 7 * 128)
    load_direct("pow2", pow2[:], b_pow2, pow2_d)
    op("dve", "memset", [], [b_cst], cst[:, 0:1], -0.5)
    load_direct("tmpb", tmpb[:], b_tmpb, cC_d)
    for d, (dstt, bd) in enumerate(((biasC0, b_biasC0), (biasC1, b_biasC1))):
        def post(dst, stg, c0, n, h, bd=bd):
            op("dve", "tensor_tensor", [b_sc[h], b_tmpb], [bd], out=dst, in0=stg, in1=tmpb[:, c0:c0 + n], op=ALU.subtract)
        load_cvt(dstt, bd, biasC_d[d], 512, post=post)
    load_cvt(maskB4, b_maskB4, maskB_d[1], 512)
    for m in "BM":
        op("pool", "memset", [], [b_qblk[m]], qblk[m][:], 0.0)
    for k in range(2):
        op("pool", "memset", [], [b_qblkC[k]], qblkC[k][:], 0.0)
    op("pool", "memset", [], [b_iqblk], iqblk[:], 0.0)
    op("pool", "memset", [], b_vC, vC[:, :, :, 64:65], 1.0)
    op("pool", "memset", [], b_vB, vB[:, :, :, 64:65], 1.0)
    op("pool", "memset", [], [b_vM], vM[:, :, :, 64:65], 1.0)

    def layer_norm(X, bX, width, gt, bg_, bt, bb_, stt, bst, out=None, bout=None):
        nch = (width + 511) // 512
        cw = width // nch
        for c in range(nch):
            op("dve", "bn_stats", [bX], [bst], out=stt[:, 8 + 6 * c:14 + 6 * c], in_=X[:, c * cw:(c + 1) * cw])
        op("dve", "bn_aggr", [bst], [bst], out=stt[:, 0:2], in_=stt[:, 8:8 + 6 * nch])
        op("dve", "tensor_scalar", [bst], [bst], out=stt[:, 2:3], in0=stt[:, 1:2], scalar1=LN_EPS, scalar2=None, op0=ALU.add)
        op("pool", "tensor_tensor", [bst, b_cst], [bst], out=stt[:, 3:4], in0=stt[:, 2:3], in1=cst[:, 0:1], op=ALU.pow)
        op("dve", "tensor_scalar", [bst], [bst], out=stt[:, 4:5], in0=stt[:, 0:1], scalar1=stt[:, 3:4], scalar2=-1.0, op0=ALU.mult, op1=ALU.mult)
        op("act", "activation", [bX, bst], [bX], out=X, in_=X, func=AF.Identity, bias=stt[:, 4:5], scale=stt[:, 3:4])
        op("dve", "tensor_tensor", [bX, bg_], [bX], out=X, in0=X, in1=gt, op=ALU.mult)
        if out is None:
            op("pool", "tensor_tensor", [bX, bb_], [bX], out=X, in0=X, in1=bt, op=ALU.add)
        else:
            op("pool", "tensor_tensor", [bX, bb_], [bout], out=out, in0=X, in1=bt, op=ALU.add)

    def attention(mixer, q, bq, keys, mixed_t, bmixed, mixed_off, gate_ap, bgate, bank_fn, PTs, bPTs, ptkey):
        oacc = acc[mixer]
        bacc = b_acc[mixer]
        nk = len(keys)
        pend = []
        npt = len(PTs)

        def stage1(jj):
            (k0, k1, rk_, vfn, rv_, extra) = keys[jj]
            ST, bST = bank_fn()
            mm(ST[:, 0:256], k0, q[:, 0, :], True, rk_ + [bq], [bST])
            mm(ST[:, 256:512], k1, q[:, 1, :], False, rk_ + [bq], [bST])
            for (l_, r_, rd_) in extra:
                mm(ST[:, :], l_, r_, False, rd_, [bST])
            pk = nxt(ptkey, npt)
            op("act", "activation", [bST], [bPTs[pk]], out=PTs[pk][:], in_=ST[:, :], func=AF.Exp)
            pend.append((jj, pk))

        def stage2():
            jj, pk = pend.pop(0)
            (k0, k1, rk_, vfn, rv_, extra) = keys[jj]
            for h in range(4):
                mm(oacc[:, h * 65:(h + 1) * 65], PTs[pk][:, h * 128:(h + 1) * 128], vfn(h), (jj == 0 and h == 0), [bPTs[pk]] + rv_, [bacc])

        look = min(2, npt)
        for jj in range(nk):
            stage1(jj)
            if len(pend) >= look:
                stage2()
            yield
        while pend:
            stage2()
        ov = oacc[:, 0:260].rearrange("p (h d) -> p h d", d=65)
        rec = recs[mixer]
        op("dve", "reciprocal", [bacc], [b_recs[mixer]], out=rec[:, 0:4], in_=ov[:, :, 64])
        for h in range(4):
            op("dve", "scalar_tensor_tensor", [bacc, b_recs[mixer], bgate], [bmixed],
               out=mixed_t[:, mixed_off + h * 64:mixed_off + (h + 1) * 64], in0=oacc[:, h * 65:h * 65 + 64],
               scalar=rec[:, h:h + 1], in1=gate_ap[:, h * 64:(h + 1) * 64], op0=ALU.mult, op1=ALU.mult)
        yield

    load_direct("lng", lng[:], b_lng, lnin_d[0])
    load_direct("lnb", lnb[:], b_lnb, lnin_d[1])
    last_store = [None, None]
    for i in range(nblk):
        s_ = i % 2
        X = xin[s_]
        dma(d_x[s_], X[:], x_d[i * 128:(i + 1) * 128, :], [], [b_x[s_]])
        layer_norm(X[:], b_x[s_], 1024, lng[:], b_lng, lnb[:], b_lnb, stB, b_stB)
        last_store[s_] = dma(d_o[s_], out_d[i * 128:(i + 1) * 128, :], X[:], [b_x[s_]], [b_o[i]])

    def setup_layer(l):
        load_cvt(wtm, b_wtm, wtm_d[l], 8 * NTM)
        load_cvt(wfm, b_wfm, wfm_d[l], 8 * NFM)
        load_cvt(wout, b_wout, wout_d[l], 8 * 1024, scale=0.5)
        load_cvt(brow, b_brow, brow_d[l], 512)
        load_cvt(wsT, b_wsT, wsT_d[l], 512)
        op("pool", "memset", [], [b_wsT], wsT[64:128, :].rearrange("p (g i) -> p g i", g=4)[:, :, 0:64], 0.0)
        load_cvt(absT, b_absT, absT_d[l], 128)
        load_direct("bfm", bfm[:], b_bfm, bfm_d[l])
        op("dve", "tensor_scalar", [b_bfm], [b_bfm8], out=bfm8[:], in0=bfm[:], scalar1=0.125, scalar2=None, op0=ALU.mult)
        load_direct("lng", lng[:], b_lng, lng_d[l, 0])
        load_direct("lnb", lnb[:], b_lnb, lng_d[l, 1])
        load_direct("alng", alng[:], b_alng, alng_d[l, 0])
        load_direct("alnb", alnb[:], b_alnb, alng_d[l, 1])
        load_direct("tmpb", tmpb[:], b_tmpb, cB_d[l])
        for m_, (dstt, bd) in enumerate(((biasB0, b_biasB0), (biasB1, b_biasB1))):
            def post(dst, stg, c0, n, h, bd=bd):
                op("dve", "tensor_tensor", [b_sc[h], b_tmpb], [bd], out=dst, in0=stg, in1=tmpb[:, c0:c0 + n], op=ALU.subtract)
            load_cvt(dstt, bd, biasB_d[l, m_], 512, post=post)
        load_direct("tmpb", tmpb[:], b_tmpb, maskB_d[0])
        op("dve", "tensor_tensor", [b_biasB0, b_tmpb], [b_biasB0], out=biasB0[:], in0=biasB0[:], in1=tmpb[:], op=ALU.add)
        for mt in range(2):
            X = xin[mt]
            dma(d_x[mt], X[:], mem_d[mt * 128:(mt + 1) * 128, :], [], [b_x[mt]])
            op("act", "copy", [b_x[mt]], [b_mixed[0]], out=mixed[0][:], in_=X[:])
            tb, btb = bankA()
            tv = tb[:, :].bitcast(BF16)
            for kt in range(8):
                transpose(tv[:, kt * 128:(kt + 1) * 128], mixed[0][:, kt * 128:(kt + 1) * 128], [b_mixed[0]], [btb])
            op("dve", "tensor_copy", [btb], b_rt, out=memT[:, :, mt * 128:(mt + 1) * 128], in_=tv.rearrange("p (a b) -> p a b", b=128))
        load_cvt(wmb, b_wmb, wmk_d[l], 8 * 256)
        for t in range(2):
            bk, bbk = bankA()
            for kt in range(8):
                mm(bk[:, 0:256], wmb[:, kt * 256 + t * 128:kt * 256 + (t + 1) * 128], memT[:, kt, :], kt == 0, [b_wmb] + b_rt, [bbk])
            op("dve", "tensor_copy", [bbk], [b_kmT], out=kmT[:, t, :], in_=bk[:, 0:256])
        load_cvt(wmb, b_wmb, wmv_d[l], 8 * 256)
        for mt in range(2):
            bk, bbk = bankA()
            for kt in range(8):
                mm(bk[:, 0:256], memT[:, kt, mt * 128:(mt + 1) * 128], wmb[:, kt * 256:(kt + 1) * 256], kt == 0, [b_wmb] + b_rt, [bbk])
            op("dve", "tensor_copy", [bbk], [b_vM], out=vM[:, mt, :, 0:64], in_=bk[:, 0:256].rearrange("p (h d) -> p h d", d=64))

    lo, hi, full = slice(0, 64), slice(64, 128), slice(0, 128)

    def ev(out, in0, prt, ct, scale, reads, writes):
        if ct < 10:
            if scale is None:
                op("act", "activation", reads, writes, out=out, in_=in0, func=AF.Identity, bias=bfm[prt, ct:ct + 1], scale=1.0)
            else:
                op("act", "activation", reads + [b_bfm8], writes, out=out, in_=in0, func=AF.Identity, bias=bfm8[prt, ct:ct + 1], scale=scale)
            return
        if scale is None:
            op("dve", "tensor_scalar", reads, writes, out=out, in0=in0, scalar1=bfm[prt, ct:ct + 1], scalar2=None, op0=ALU.add)
        else:
            op("dve", "tensor_scalar", reads, writes, out=out, in0=in0, scalar1=bfm[prt, ct:ct + 1], scalar2=scale, op0=ALU.add, op1=ALU.mult)

    def fm_group(i, cts, bank_fn):
        p_ = i % 2
        sl = i % 5
        bk, bbk = bank_fn()
        first = True
        for ci, ct in enumerate(cts):
            for kt in range(8):
                mm(bk[:, ci * 128:(ci + 1) * 128], wfm[:, kt * NFM + ct * 128:kt * NFM + (ct + 1) * 128], xnT[:, kt, :], first, [b_wfm, b_xnT], [bbk])
                first = False
        for ci, ct in enumerate(cts):
            pst = bk[:, ci * 128:(ci + 1) * 128]
            rd = [bbk, b_bfm]
            if ct in (0, 1, 8, 9):
                m = "B" if ct < 2 else "M"
                t = ct % 2
                ev(qblk[m][lo, t, 0:128], pst[lo, :], lo, ct, 0.125, rd, [b_qblk[m]])
                ev(qblk[m][hi, t, 128:256], pst[hi, :], hi, ct, 0.125, rd, [b_qblk[m]])
            elif ct in (4, 5):
                t = ct % 2
                ev(qblkC[p_][lo, t, 0:128], pst[lo, :], lo, ct, 0.125, rd, [b_qblkC[p_]])
                ev(qblkC[p_][hi, t, 128:256], pst[hi, :], hi, ct, 0.125, rd, [b_qblkC[p_]])
            elif ct in (2, 3):
                ev(kBT[:, ct - 2, sl * 128:(sl + 1) * 128], pst[:, :], full, ct, None, rd, [b_kB[sl]])
            elif ct in (6, 7):
                ev(kCT[:, ct - 6, i * 128:(i + 1) * 128], pst[:, :], full, ct, None, rd, [b_kC[i]])
            elif ct in (10, 11, 12, 13):
                t = ct - 10
                ev(iqblk[lo, 2 * t, :], pst[lo, :], lo, ct, None, rd, [b_iqblk])
                ev(iqblk[hi, 2 * t + 1, :], pst[hi, :], hi, ct, None, rd, [b_iqblk])
            else:
                ev(ikT[:, i * 128:(i + 1) * 128], pst[:, :], full, ct, None, rd, [b_ik[i]])

    def tm_tile(r, c0, n, bank_fn):
        bk, bbk = bank_fn()
        for kt in range(8):
            mm(bk[:, 0:n], xnT[:, kt, :], wtm[:, kt * NTM + c0:kt * NTM + c0 + n], kt == 0, [b_xnT, b_wtm], [bbk])
        mm(bk[:, 0:n], sel[:, r * 128:(r + 1) * 128], brow[:, 0:n], False, [b_sel, b_brow], [bbk])
        return bk, bbk

    def phaseMain(i):
        s_ = i % 2
        p_ = i % 2
        X = xin[s_]
        N_i = (i + 1) * 128
        mx = mixed[p_]
        bmx = b_mixed[p_]
        dma(d_x[s_], X[:], out_d[i * 128:(i + 1) * 128, :], [b_o[i]], [b_x[s_]])
        op("act", "copy", [b_x[s_]], [bmx], out=mx[:], in_=X[:])
        tb, btb = bankA()
        tv = tb[:, :].bitcast(BF16)
        for kt in range(8):
            transpose(tv[:, kt * 128:(kt + 1) * 128], mx[:, kt * 128:(kt + 1) * 128], [bmx], [btb])
        op("dve", "tensor_copy", [btb], [b_xnT], out=xnT[:].rearrange("p a b -> p (a b)"), in_=tv)
        yield "XNT"
        fm_group(i, [10, 11, 12, 13], bankA)
        yield 0.2
        fm_group(i, [14], bankA)
        bk, bbk = tm_tile(4, 2048, 8, bankA)
        op("dve", "tensor_copy", [bbk], [b_iw], out=iw[:], in_=bk[:, 0:8])
        yield 0.5
        ntile = (N_i + 511) // 512
        for c in range(ntile):
            c0 = c * 512
            n = min(512, N_i - c0)
            half = [b_sc[0]] if c0 + n <= 2048 else [b_sc[1]]
            rik = [b_ik[jj] for jj in range(c0 // 128, (c0 + n) // 128)]
            for h in range(8):
                bk, bbk = bankA()
                mm(bk[:, 0:n], iqblk[:, h, :], ikT[:, c0:c0 + n], True, [b_iqblk] + rik, [bbk])
                if h == 0:
                    op("dve", "tensor_scalar", [bbk, b_iw], half, out=score[:, c0:c0 + n], in0=bk[:, 0:n], scalar1=0.0, scalar2=iw[:, 0:1], op0=ALU.max, op1=ALU.mult)
                else:
                    r_ = nxt("rtmp", 2)
                    op("act", "activation", [bbk], [b_rtmp[r_]], out=rtmp[r_][:, 0:n], in_=bk[:, 0:n], func=AF.Relu)
                    op("dve", "scalar_tensor_tensor", [b_rtmp[r_], b_iw] + half, half, out=score[:, c0:c0 + n], in0=rtmp[r_][:, 0:n], scalar=iw[:, h:h + 1],
                       in1=score[:, c0:c0 + n], op0=ALU.mult, op1=ALU.add)
                yield W_IDX * n / 512.0
        SC = b_sc if N_i > 2048 else [b_sc[0]]
        lasthalf = [b_sc[1]] if N_i > 2048 else [b_sc[0]]
        op("pool", "memset", [], lasthalf, score[0:64, N_i - 64:N_i], -1e30)
        if i >= 2:
            op("dve", "tensor_reduce", SC, [b_bis], out=bis[:, 0:1], in_=score[:, 0:N_i], axis=AX.X, op=ALU.max)
            op("dve", "tensor_reduce", SC, [b_bis], out=bis[:, 1:2], in_=score[:, 0:N_i - 64], axis=AX.X, op=ALU.min)
            op("dve", "tensor_tensor", [b_bis], [b_bis], out=bis[:, 2:3], in0=bis[:, 0:1], in1=bis[:, 1:2], op=ALU.subtract)
            op("dve", "tensor_scalar", [b_bis, b_pow2], [b_rk], out=rk[:, :], in0=pow2[:, :], scalar1=bis[:, 2:3], scalar2=None, op0=ALU.mult)
            op("dve", "tensor_tensor", [b_bis, b_rk], [b_bis], out=bis[:, 8:9], in0=bis[:, 1:2], in1=rk[:, 1:2], op=ALU.add)
            yield 1.0 + N_i / 500.0
            nd = max(64, int(round(N_i * DVE_FRAC / 64)) * 64, N_i - 2048)
            na = N_i - nd
            SCd = [b_sc[0]] if nd <= 2048 else b_sc
            SCa = [b_sc[1]] if nd >= 2048 else (b_sc if N_i > 2048 else [b_sc[0]])
            for k in range(1, NBIS + 1):
                mid = bis[:, 8 + (k - 1) % 2:9 + (k - 1) % 2]
                midn = bis[:, 8 + k % 2:9 + k % 2]
                op("dve", "tensor_scalar", SCd + [b_bis], [b_rt[0], b_cntD], out=junk[:, 0:nd], in0=score[:, 0:nd], scalar1=mid, scalar2=None,
                   op0=ALU.is_ge, op1=ALU.add, accum_out=cntD[:, 0:1])
                op("act", "activation", SCa + [b_bis], [b_rt[1], b_cntA], out=junk[:, 2048:2048 + na], in_=score[:, nd:N_i], func=AF.Sign, bias=mid, scale=-1.0,
                   accum_out=cntA[:, 0:1])
                op(SMALL_ENG, "tensor_scalar", [b_cntD, b_cntA], [b_bis], out=bis[:, 4:5], in0=cntD[:, 0:1], scalar1=2.0, scalar2=cntA[:, 0:1], op0=ALU.mult, op1=ALU.subtract)
                op(SMALL_ENG, "tensor_scalar", [b_bis, b_rk], [b_bis], out=bis[:, 5:6], in0=bis[:, 4:5], scalar1=float(512 - na), scalar2=rk[:, k:k + 1], op0=ALU.is_ge, op1=ALU.mult)
                kk = k + 1 if k < NBIS else k
                op(SMALL_ENG, "tensor_scalar", [b_bis, b_rk], [b_bis], out=midn, in0=bis[:, 5:6], scalar1=rk[:, kk:kk + 1], scalar2=mid, op0=ALU.subtract, op1=ALU.add)
                yield 1.2 + N_i / 1100.0
            thr = bis[:, 8 + NBIS % 2:9 + NBIS % 2]
        else:
            op("dve", "memset", [], [b_bis], bis[:, 12:13], -1e29)
            thr = bis[:, 12:13]
        yield "B_DONE"
        c0 = 0
        while c0 < N_i:
            n = min(2048, N_i - c0)
            op("dve", "tensor_scalar", [b_sc[c0 // 2048], b_bis], [b_mb], out=mb[:, c0:c0 + n], in0=score[:, c0:c0 + n], scalar1=thr, scalar2=NEG, op0=ALU.is_lt, op1=ALU.mult)
            c0 += n
        yield 0.0

    def main_weight(i):
        N_i = (i + 1) * 128
        w = 1.2 + W_IDX * 8 * N_i / 512.0
        if i >= 2:
            w += 1.0 + N_i / 500.0 + NBIS * (1.2 + N_i / 1100.0)
        return w

    def phaseSide(i):
        s_ = i % 2
        p_ = i % 2
        sl = i % 5
        mx = mixed[p_]
        bmx = b_mixed[p_]
        bk, bbk = tm_tile(0, 0, 512, bankS)
        op("act", "copy", [bbk], [b_xg], out=xg[:], in_=bk[:, :])
        op("pool", "tensor_tensor", [b_xg], [b_gs], out=gsc0[:], in0=xg[:], in1=xg[:], op=ALU.mult)
        op("dve", "scalar_tensor_tensor", [b_gs, b_xg], [b_gs], out=gsc0[:], in0=gsc0[:], scalar=0.044715, in1=xg[:], op0=ALU.mult, op1=ALU.mult)
        op("pool", "tensor_tensor", [b_gs, b_xg], [b_gs], out=gsc0[:], in0=gsc0[:], in1=xg[:], op=ALU.add)
        op("act", "activation", [b_gs], [b_gs], out=gsc0[:], in_=gsc0[:], func=AF.Tanh, scale=0.7978845608028654)
        op("dve", "scalar_tensor_tensor", [b_gs, b_xg], [b_xg], out=xg[:], in0=gsc0[:], scalar=1.0, in1=xg[:], op0=ALU.add, op1=ALU.mult)
        yield
        bk, bbk = tm_tile(1, 512, 512, bankS)
        op("act", "activation", [bbk], [b_gs], out=gsc0[:], in_=bk[:, :], func=AF.Tanh, scale=0.5)
        op("dve", "scalar_tensor_tensor", [b_gs, bbk], [b_gates], out=gates[:, 0:512], in0=gsc0[:], scalar=1.0, in1=bk[:, :], op0=ALU.add, op1=ALU.mult)
        yield
        bk, bbk = tm_tile(2, 1024, 512, bankS)
        op("act", "activation", [bbk], [b_gs], out=gsc0[:], in_=bk[:, :], func=AF.Tanh, scale=0.5)
        op("dve", "scalar_tensor_tensor", [b_gs, bbk], [b_gatesC[p_]], out=gatesC[p_][:, :], in0=gsc0[:, 0:256], scalar=1.0, in1=bk[:, 0:256], op0=ALU.add, op1=ALU.mult)
        op("dve", "scalar_tensor_tensor", [b_gs, bbk], [b_gates], out=gates[:, 512:768], in0=gsc0[:, 256:512], scalar=1.0, in1=bk[:, 256:512], op0=ALU.add, op1=ALU.mult)
        yield
        bk, bbk = tm_tile(3, 1536, 512, bankS)
        op("act", "copy", [bbk], [b_vB[sl]], out=vB[:, sl, :, 0:64], in_=bk[:, 0:256].rearrange("p (h d) -> p h d", d=64))
        op("act", "copy", [bbk], [b_vC[i]], out=vC[:, i, :, 0:64], in_=bk[:, 256:512].rearrange("p (h d) -> p h d", d=64))
        yield
        for cts in ([0, 1, 2, 3], [4, 5, 6, 7], [8, 9]):
            fm_group(i, cts, bankS)
            yield
        op("act", "mul", [b_xg], [b_xg], out=xg[:, 256:512], in_=xg[:, 256:512], mul=0.5)
        layer_norm(xg[:, 256:512], b_xg, 256, alng[:], b_alng, alnb[:], b_alnb, stA, b_stA, out=vln[:], bout=b_vln)
        bk, bbk = bankS()
        for g in range(4):
            mm(bk[:, g * 64:(g + 1) * 64], wsT[:, g * 128:(g + 1) * 128], vln[:, g * 64:(g + 1) * 64], g == 0, [b_wsT, b_vln], [bbk])
        mm(bk[:, 0:256], absT[:, :], Emat[:, :], False, [b_absT, b_E], [bbk])
        op("dve", "scalar_tensor_tensor", [bbk, b_xg], [b_xg], out=xg[:, 256:512], in0=bk[:, 0:256], scalar=0.5, in1=xg[:, 0:256], op0=ALU.mult, op1=ALU.mult)
        op("pool", "tensor_tensor", [b_xg, b_gates], [bmx], out=mx[:, 0:256], in0=xg[:, 256:512], in1=gates[:, 0:256], op=ALU.mult)
        yield
        keys = []
        for j in range(max(0, i - 4), i + 1):
            sj = j % 5
            m_ = i - j
            extra = []
            if m_ == 0:
                extra.append((ident4[:, 0:128], biasB0[:, :], [b_id, b_biasB0]))
            elif m_ == 1:
                extra.append((ident4[:, 0:128], biasB1[:, :], [b_id, b_biasB1]))
            elif m_ == 4:
                extra.append((ident4[:, 0:128], maskB4[:, :], [b_id, b_maskB4]))
            keys.append((kBT[:, 0, sj * 128:(sj + 1) * 128], kBT[:, 1, sj * 128:(sj + 1) * 128], [b_kB[sj]],
                         (lambda h, sj=sj: vB[:, sj, h, :]), [b_vB[sj]], extra))
        yield from attention("B", qblk["B"], b_qblk["B"], keys, mx, bmx, 256, gates[:, 256:512], b_gates, bankS, PTA, b_PTA, "PTA")
        keys = []
        for mt in range(2):
            keys.append((kmT[:, 0, mt * 128:(mt + 1) * 128], kmT[:, 1, mt * 128:(mt + 1) * 128], [b_kmT],
                         (lambda h, mt=mt: vM[:, mt, h, :]), [b_vM], []))
        yield from attention("M", qblk["M"], b_qblk["M"], keys, mx, bmx, 768, gates[:, 512:768], b_gates, bankS, PTA, b_PTA, "PTA")

    def phaseB(i, l):
        s_ = i % 2
        p_ = i % 2
        X = xin[s_]
        mx = mixed[p_]
        bmx = b_mixed[p_]
        keys = []
        for j in range(0, i + 1):
            extra = [(mb[:, j * 128:(j + 1) * 128], ident4[:, :], [b_mb, b_id])]
            d_ = i - j
            if d_ == 0:
                extra.append((ident4[:, 0:128], biasC0[:, :], [b_id, b_biasC0]))
            elif d_ == 1:
                extra.append((ident4[:, 0:128], biasC1[:, :], [b_id, b_biasC1]))
            keys.append((kCT[:, 0, j * 128:(j + 1) * 128], kCT[:, 1, j * 128:(j + 1) * 128], [b_kC[j]],
                         (lambda h, j=j: vC[:, j, h, :]), [b_vC[j]], extra))
        yield from attention("C", qblkC[p_], b_qblkC[p_], keys, mx, bmx, 512, gatesC[p_][:, :], b_gatesC[p_], bankC, PTC, b_PTC, "PTC")
        tb, btb = bankC()
        tv = tb[:, :].bitcast(BF16)
        for kt in range(8):
            transpose(tv[:, kt * 128:(kt + 1) * 128], mx[:, kt * 128:(kt + 1) * 128], [bmx], [btb])
        op("dve", "tensor_copy", [btb], [b_mixedT], out=mixedT[:].rearrange("p a b -> p (a b)"), in_=tv)
        yield
        for c in range(2):
            bk, bbk = bankC()
            for kt in range(8):
                mm(bk[:, :], mixedT[:, kt, :], wout[:, kt * 1024 + c * 512:kt * 1024 + (c + 1) * 512], kt == 0, [b_mixedT, b_wout], [bbk])
            mm(bk[:, :], sel[:, (5 + c) * 128:(6 + c) * 128], brow[:, :], False, [b_sel, b_brow], [bbk])
            op("dve", "scalar_tensor_tensor", [b_x[s_], bbk], [b_x[s_]], out=X[:, c * 512:(c + 1) * 512], in0=X[:, c * 512:(c + 1) * 512], scalar=ALPHA, in1=bk[:, :],
               op0=ALU.mult, op1=ALU.add)
            yield
        layer_norm(X[:], b_x[s_], 1024, lng[:], b_lng, lnb[:], b_lnb, stB, b_stB)
        dma(d_o[s_], out_d[i * 128:(i + 1) * 128, :], X[:], [b_x[s_]], [b_o[i]], holder=(last_store, s_))
        yield

    def drain(g):
        for _ in g:
            pass

    class Stream:
        def __init__(self, gen, prio):
            self.gen = gen
            self.q = []
            self.done = False
            self.prio = prio
            self.active = True

    eng_free = {e: 0.0 for e in ENGS}
    tok_done = {}

    def fill(st):
        while not st.q and not st.done:
            cur_q[0] = st.q
            try:
                y = next(st.gen)
                if isinstance(y, str):
                    st.q.append(y)
            except StopIteration:
                st.done = True
            finally:
                cur_q[0] = None

    def ready_time(eng, reads, writes):
        t = 0.0
        for bf in reads:
            if bf.writer is not None:
                t = max(t, tok_done.get((id(bf.writer[0]), bf.writer[1]), 0.0) + (0.0 if bf.writer[2] == eng else 0.25))
            if bf.psum:
                for tk in bf.readers.values():
                    if tk[2] != eng:
                        t = max(t, tok_done.get((id(tk[0]), tk[1]), 0.0) + 0.25)
        for bf in writes:
            if bf.writer is not None:
                t = max(t, tok_done.get((id(bf.writer[0]), bf.writer[1]), 0.0) + (0.0 if bf.writer[2] == eng else 0.25))
            for tk in bf.readers.values():
                if tk[2] != eng:
                    t = max(t, tok_done.get((id(tk[0]), tk[1]), 0.0) + 0.25)
        return t

    def commit(item, start):
        eng, fn, reads, writes, dur, ds, holder = item
        if ds is not None:
            tok = S.dma(eng, ds, fn, reads=reads, writes=writes)
            if holder is not None:
                holder[0][holder[1]] = tok
            eng_free[eng] = start + 0.1
        else:
            tok = S.op(eng, fn, reads=reads, writes=writes)
            eng_free[eng] = start + dur
        tok_done[(id(tok[0]), tok[1])] = start + dur

    def run_round(gmain, gside, gB):
        main = Stream(gmain, 0.0)
        side = Stream(gside, 0.0) if gside is not None else None
        stB = Stream(gB, 0.0) if gB is not None else None
        if side is not None:
            side.active = False
        streams = [x for x in (main, stB, side) if x is not None]
        while True:
            best = None
            alive = False
            for st in streams:
                fill(st)
                if not st.q:
                    continue
                alive = True
                if not st.active:
                    continue
                item = st.q[0]
                if isinstance(item, str):
                    if item == "XNT":
                        st.q.pop(0)
                        if side is not None:
                            side.active = True
                        best = "again"
                        break
                    if item == "B_DONE":
                        if stB is None or (stB.done and not stB.q):
                            st.q.pop(0)
                            best = "again"
                            break
                        continue
                    st.q.pop(0)
                    best = "again"
                    break
                eng = item[0]
                start = max(eng_free[eng], ready_time(eng, item[2], item[3]))
                key = start - st.prio
                if best is None or key < best[0]:
                    best = (key, start, st)
            if best == "again":
                continue
            if best is None:
                if not alive:
                    break
                raise RuntimeError("scheduler stuck")
            _, start, st = best
            commit(st.q.pop(0), start)

    for l in range(n_layers):
        setup_layer(l)
        if not pipeline:
            for i in range(nblk):
                drain(phaseMain(i))
                drain(phaseSide(i))
                drain(phaseB(i, l))
        else:
            run_round(phaseMain(0), phaseSide(0), None)
            for i in range(nblk):
                if i + 1 < nblk:
                    run_round(phaseMain(i + 1), phaseSide(i + 1), phaseB(i, l))
                else:
                    drain(phaseB(i, l))

    S.wait_tokens("sp", [t for t in last_store if t is not None])
    S.emit()
    es.close()
    return nc


def _t5_bucket_idx(rel):
    nb = 16
    max_exact = 8
    ret = np.where(rel > 0, nb, 0)
    n = np.abs(rel)
    nf = np.maximum(n, 1).astype(np.float32)
    large = max_exact + (np.log(nf / np.float32(max_exact)) / np.float32(np.log(128 / max_exact)) * (nb - max_exact)).astype(np.int32)
    large = np.minimum(large, nb - 1)
    return ret + np.where(n < max_exact, n, large)


def _kt_layout(w):
    C = w.shape[1]
    return np.ascontiguousarray(w.reshape(8, 128, C).transpose(1, 0, 2).reshape(128, 8 * C))


def prepare(inputs):
    f = np.float32
    cum = np.cumsum((0,) + SPLITS)
    rng = {n: (int(cum[k]), int(cum[k + 1])) for k, n in enumerate(NAMES)}
    tm_cols = np.concatenate([np.arange(*rng[n]) for n in TM_ORDER])
    fm_cols = np.concatenate([np.arange(*rng[n]) for n in FM_ORDER])
    w_in = np.asarray(inputs["w_in"], f)
    b_in = np.asarray(inputs["b_in"], f)
    sh = {}
    sh["wtm"] = np.stack([_kt_layout(w_in[l][:, tm_cols]) for l in range(DEPTH)])
    sh["wfm"] = np.stack([_kt_layout(w_in[l][:, fm_cols]) for l in range(DEPTH)])
    sh["wout"] = np.stack([_kt_layout(np.asarray(inputs["w_out"], f)[l]) for l in range(DEPTH)])
    wm = np.asarray(inputs["w_mem_kv"], f)
    sh["wmk"] = np.stack([_kt_layout(wm[l][:, 0:256]) for l in range(DEPTH)])
    sh["wmv"] = np.stack([_kt_layout(wm[l][:, 256:512]) for l in range(DEPTH)])
    brow = np.zeros((DEPTH, 128, 512), f)
    btm = b_in[:, tm_cols]
    for r in range(4):
        brow[:, r, :] = btm[:, r * 512:(r + 1) * 512]
    brow[:, 4, 0:8] = btm[:, 2048:2056]
    b_out = np.asarray(inputs["b_out"], f)
    brow[:, 5, :] = b_out[:, 0:512]
    brow[:, 6, :] = b_out[:, 512:1024]
    sh["brow"] = brow
    sel = np.zeros((128, 7, 128), f)
    for r in range(7):
        sel[r, r, :] = 1.0
    sh["sel"] = sel.reshape(128, 7 * 128)
    sh["bfm"] = np.ascontiguousarray(b_in[:, fm_cols].reshape(DEPTH, 15, 128).transpose(0, 2, 1))
    sh["lnin"] = np.stack([np.broadcast_to(np.asarray(inputs["ln_in_g"], f), (128, 1024)),
                           np.broadcast_to(np.asarray(inputs["ln_in_b"], f), (128, 1024))]).copy()
    sh["lng"] = np.stack([np.stack([np.broadcast_to(np.asarray(inputs["ln_g"], f)[l], (128, 1024)),
                                    np.broadcast_to(np.asarray(inputs["ln_b"], f)[l], (128, 1024))]) for l in range(DEPTH)]).copy()
    sh["alng"] = np.stack([np.stack([np.broadcast_to(np.asarray(inputs["a_ln_g"], f)[l], (128, 256)),
                                     np.broadcast_to(np.asarray(inputs["a_ln_b"], f)[l], (128, 256))]) for l in range(DEPTH)]).copy()
    a_ws = np.asarray(inputs["a_ws"], f)
    sh["wsT"] = np.ascontiguousarray(a_ws.transpose(0, 3, 1, 2).reshape(DEPTH, 128, 512))
    ab = np.zeros((DEPTH, 128, 128), f)
    ab[:, 0:4, :] = np.asarray(inputs["a_bs"], f)
    sh["absT"] = ab
    b_rel = np.asarray(inputs["b_rel"], f)
    a = np.arange(128)[:, None]
    b = np.arange(128)[None, :]
    bB = np.zeros((DEPTH, 2, 128, 4, 128), f)
    for m in range(2):
        idx = np.clip(128 * m + b - a, -128, 128) + 128
        for l in range(DEPTH):
            bB[l, m] = b_rel[l][:, idx].transpose(1, 0, 2)
    sh["biasB"] = bB.reshape(DEPTH, 2, 128, 512)
    sh["cB"] = np.ascontiguousarray(np.broadcast_to(b_rel[:, None, :, 256, None], (DEPTH, 128, 4, 128)).reshape(DEPTH, 128, 512))
    t5 = np.asarray(inputs["t5_table"], f)
    bC = np.zeros((2, 128, 4, 128), f)
    for d in range(2):
        relkq = -128 * d + a - b
        bi = _t5_bucket_idx(relkq)
        bC[d] = t5[bi].transpose(0, 2, 1)
    sh["biasC"] = bC.reshape(2, 128, 512)
    sh["cC"] = np.ascontiguousarray(np.broadcast_to(t5[15][None, :, None], (128, 4, 128)).reshape(128, 512))
    mB = np.zeros((2, 128, 4, 128), f)
    mB[0][64:128, :, 0:64] = NEG
    mB[1][0:64, :, 64:128] = NEG
    sh["maskB"] = mB.reshape(2, 128, 512)
    E = np.zeros((128, 256), f)
    for g in range(4):
        E[g, g * 64:(g + 1) * 64] = 1.0
    sh["Emat"] = E
    sh["ident4"] = np.tile(np.eye(128, dtype=f), (1, 4))
    sh["pow2"] = np.broadcast_to((2.0 ** -np.arange(NBIS + 2)).astype(f)[None, :], (128, NBIS + 2)).copy()
    return sh


_NC_CACHE = {}


def kernel(**inputs):
    sh = prepare(inputs)
    x = np.asarray(inputs["x"], np.float32)
    mem = np.asarray(inputs["mem"], np.float32)
    if "nc" not in _NC_CACHE:
        _NC_CACHE["nc"] = build_program()
    nc = _NC_CACHE["nc"]
    in_maps = []
    for c in range(BATCH):
        m = dict(sh)
        m["x"] = np.ascontiguousarray(x[c])
        m["mem"] = np.ascontiguousarray(mem[c])
        in_maps.append(m)
    res = run_bass_kernel_spmd(nc, in_maps, core_ids=list(range(BATCH)))
    return np.stack([np.asarray(r["out"]) for r in res.results]).astype(np.float32)
```

```python
import numpy as np
import concourse.bass as bass
import concourse.mybir as mybir
from concourse.bass_utils import run_bass_kernel_spmd
from contextlib import ExitStack

F32 = mybir.dt.float32
BF16 = mybir.dt.bfloat16
U8 = mybir.dt.uint8
ALU = mybir.AluOpType
AF = mybir.ActivationFunctionType
AX = mybir.AxisListType

D_MODEL = 1024
BATCH = 8
SEQ = 4096
DEPTH = 2
NBLK = SEQ // 128
GW = 256
SPLITS = (GW, GW, GW, GW, GW, GW, GW, GW, GW, GW, GW, 512, 64, 8, GW, GW)
NAMES = ("a_u", "a_v", "a_g", "bq", "bk", "bv", "bg", "cq", "ck", "cv", "cg", "iq", "ik", "iw", "mq", "mg")
TM_ORDER = ("a_u", "a_v", "a_g", "bg", "cg", "mg", "bv", "cv", "iw")
FM_ORDER = ("bq", "bk", "cq", "ck", "mq", "iq", "ik", "ik")
NTM = 2056
NFM = 1920
ALPHA = (2 * DEPTH) ** 0.25
LN_EPS = 1e-5
NBIS = 12
DVE_FRAC = 0.35
W_IDX = 0.12
SMALL_ENG = "pool"
NEG = -30000.0

ENGS = ("pe", "act", "dve", "pool", "sp")
STRICT_WAR = False


class Buf:
    __slots__ = ("name", "writer", "readers", "psum")

    def __init__(self, name="", psum=False):
        self.name = name
        self.writer = None
        self.readers = {}
        self.psum = psum


class DmaSem:
    def __init__(self, sem):
        self.sem = sem
        self.count = 0


class Sched:
    EPOCH = 12000

    def __init__(self, nc, es):
        self.nc = nc
        self.es = es
        self.prog = {e: [] for e in ENGS}
        self.cnt = {e: 0 for e in ENGS}
        self.nsem = 0
        self.cursem = {e: self._newsem(e) for e in ENGS}
        self.waited = {e: {} for e in ENGS}
        self.semobj = {}

    def _newsem(self, name):
        self.nsem += 1
        return self.es.enter_context(self.nc.semaphore(f"s_{name}_{self.nsem}"))

    def dma_sem(self, name):
        return DmaSem(self._newsem("d" + name))

    def _deps(self, eng, reads, writes):
        need = {}

        def add(tok, kind):
            if tok is None:
                return
            sem, val, teng = tok
            if teng == eng:
                if eng == "pe" or (kind == "war" and not STRICT_WAR):
                    return
            k = id(sem)
            self.semobj[k] = sem
            if need.get(k, 0) < val:
                need[k] = val

        for b in reads:
            add(b.writer, "raw")
            if b.psum:
                for t in b.readers.values():
                    if t[2] != eng:
                        add(t, "rar")
        for b in writes:
            add(b.writer, "waw")
            for t in b.readers.values():
                add(t, "war")
        waits = []
        w = self.waited[eng]
        for k, val in need.items():
            if w.get(k, 0) < val:
                w[k] = val
                waits.append((self.semobj[k], val))
        return waits

    def _mark(self, tok, reads, writes):
        k = id(tok[0])
        for b in reads:
            b.readers[k] = tok
        for b in writes:
            b.writer = tok
            b.readers = {}

    def op(self, eng, fn, reads=(), writes=()):
        waits = self._deps(eng, reads, writes)
        if self.cnt[eng] >= self.EPOCH:
            self.cursem[eng] = self._newsem(eng)
            self.cnt[eng] = 0
        self.cnt[eng] += 1
        tok = (self.cursem[eng], self.cnt[eng], eng)
        self.prog[eng].append((waits, fn, (self.cursem[eng], 1)))
        self._mark(tok, reads, writes)
        return tok

    def dma(self, eng, ds, fn, reads=(), writes=()):
        waits = self._deps(eng, reads, writes)
        ds.count += 16
        tok = (ds.sem, ds.count, "dma")
        self.prog[eng].append((waits, fn, (ds.sem, 16)))
        self._mark(tok, reads, writes)
        return tok

    def wait_tokens(self, eng, toks):
        self.prog[eng].append(([(t[0], t[1]) for t in toks], None, None))

    def emit(self):
        nc = self.nc
        engobj = {"pe": nc.tensor, "act": nc.scalar, "dve": nc.vector, "pool": nc.gpsimd, "sp": nc.sync}

        def replay(e):
            eo = engobj[e]
            for waits, fn, inc in self.prog[e]:
                for sem, val in waits:
                    eo.wait_ge(sem, val)
                if fn is not None:
                    fn().then_inc(inc[0], inc[1])

        with nc.Block() as block:
            @block.tensor
            def _(x):
                replay("pe")

            @block.scalar
            def _(x):
                replay("act")

            @block.vector
            def _(x):
                replay("dve")

            @block.gpsimd
            def _(x):
                replay("pool")

            @block.sync
            def _(x):
                replay("sp")


def build_program(n_layers=DEPTH, nblk=NBLK, pipeline=True):
    nc = bass.Bass("TRN2", target_bir_lowering=False)
    es = ExitStack()
    S = Sched(nc, es)

    def din(name, shape):
        return nc.dram_tensor(name, list(shape), F32, kind="ExternalInput").ap()

    x_d = din("x", [SEQ, D_MODEL])
    mem_d = din("mem", [256, D_MODEL])
    lnin_d = din("lnin", [2, 128, 1024])
    wtm_d = din("wtm", [DEPTH, 128, 8 * NTM])
    wfm_d = din("wfm", [DEPTH, 128, 8 * NFM])
    wout_d = din("wout", [DEPTH, 128, 8 * 1024])
    wmk_d = din("wmk", [DEPTH, 128, 8 * 256])
    wmv_d = din("wmv", [DEPTH, 128, 8 * 256])
    brow_d = din("brow", [DEPTH, 128, 512])
    sel_d = din("sel", [128, 7 * 128])
    bfm_d = din("bfm", [DEPTH, 128, 15])
    lng_d = din("lng", [DEPTH, 2, 128, 1024])
    alng_d = din("alng", [DEPTH, 2, 128, 256])
    wsT_d = din("wsT", [DEPTH, 128, 512])
    absT_d = din("absT", [DEPTH, 128, 128])
    biasB_d = din("biasB", [DEPTH, 2, 128, 512])
    cB_d = din("cB", [DEPTH, 128, 512])
    biasC_d = din("biasC", [2, 128, 512])
    cC_d = din("cC", [128, 512])
    maskB_d = din("maskB", [2, 128, 512])
    E_d = din("Emat", [128, 256])
    ident4_d = din("ident4", [128, 512])
    pow2_d = din("pow2", [128, NBIS + 2])
    out_d = nc.dram_tensor("out", [SEQ, D_MODEL], F32, kind="ExternalOutput").ap()
    b_o = [Buf() for _ in range(NBLK)]

    def sb(name, shape, dt):
        return es.enter_context(nc.sbuf_tensor(name, list(shape), dt))

    def ps(name, shape, dt):
        return es.enter_context(nc.psum_tensor(name, list(shape), dt))

    wtm = sb("wtm_s", [128, 8 * NTM], BF16); b_wtm = Buf()
    wfm = sb("wfm_s", [128, 8 * NFM], BF16); b_wfm = Buf()
    wout = sb("wout_s", [128, 8 * 1024], BF16); b_wout = Buf()
    brow = sb("brow_s", [128, 512], BF16); b_brow = Buf()
    sel = sb("sel_s", [128, 7 * 128], BF16); b_sel = Buf()
    bfm = sb("bfm_s", [128, 15], F32); b_bfm = Buf()
    bfm8 = sb("bfm8_s", [128, 15], F32); b_bfm8 = Buf()
    lng = sb("lng_s", [128, 1024], F32); b_lng = Buf()
    lnb = sb("lnb_s", [128, 1024], F32); b_lnb = Buf()
    alng = sb("alng_s", [128, 256], F32); b_alng = Buf()
    alnb = sb("alnb_s", [128, 256], F32); b_alnb = Buf()
    wsT = sb("wsT_s", [128, 512], BF16); b_wsT = Buf()
    absT = sb("absT_s", [128, 128], BF16); b_absT = Buf()
    Emat = sb("E_s", [128, 256], BF16); b_E = Buf()
    ident4 = sb("ident4_s", [128, 512], BF16); b_id = Buf()
    pow2 = sb("pow2_s", [128, NBIS + 2], F32); b_pow2 = Buf()
    biasB0 = sb("biasB0_s", [128, 512], BF16); biasB1 = sb("biasB1_s", [128, 512], BF16)
    maskB4 = sb("maskB4_s", [128, 512], BF16)
    biasC0 = sb("biasC0_s", [128, 512], BF16); biasC1 = sb("biasC1_s", [128, 512], BF16)
    b_biasB0, b_biasB1, b_maskB4, b_biasC0, b_biasC1 = Buf(), Buf(), Buf(), Buf(), Buf()

    kCT = sb("kCT_s", [128, 2, SEQ], BF16); b_kC = [Buf() for _ in range(NBLK)]
    vC = sb("vC_s", [128, NBLK, 4, 65], BF16); b_vC = [Buf() for _ in range(NBLK)]
    ikT = sb("ikT_s", [128, SEQ], BF16); b_ik = [Buf() for _ in range(NBLK)]
    kBT = sb("kBT_s", [128, 2, 5 * 128], BF16); b_kB = [Buf() for _ in range(5)]
    vB = sb("vB_s", [128, 5, 4, 65], BF16); b_vB = [Buf() for _ in range(5)]
    kmT = sb("kmT_s", [128, 2, 256], BF16); b_kmT = Buf()
    vM = sb("vM_s", [128, 2, 4, 65], BF16); b_vM = Buf()

    score = sb("score_s", [128, SEQ], F32); b_sc = [Buf(), Buf()]
    mb = sb("mb_s", [128, SEQ], BF16); b_mb = Buf()
    wmb = mb[:, 0:2048]; b_wmb = b_mb
    gsc0 = sb("gscr_s", [128, 512], F32); b_gs = Buf()
    rt = sb("rt_s", [128, 1024], F32); b_rt = [Buf(), Buf()]
    rtmp = [rt[:, 0:512], rt[:, 512:1024]]; b_rtmp = b_rt
    junk = rt[:, :].bitcast(U8)
    memT = rt[:, :].bitcast(BF16).rearrange("p (a b) -> p a b", b=256)
    cntD = sb("cntD_s", [128, 2], F32); b_cntD = Buf()
    cntA = sb("cntA_s", [128, 2], F32); b_cntA = Buf()
    xin = [sb(f"xin{k}_s", [128, 1024], F32) for k in range(2)]; b_x = [Buf(), Buf()]
    mixed = [sb(f"mixed{k}_s", [128, 1024], BF16) for k in range(2)]; b_mixed = [Buf(), Buf()]
    xnT = sb("xnT_s", [128, 8, 128], BF16); b_xnT = Buf()
    mixedT = sb("mixedT_s", [128, 8, 128], BF16); b_mixedT = Buf()
    xg = sb("xg_s", [128, 512], F32); b_xg = Buf()
    tmpb = xg; b_tmpb = b_xg
    gates = sb("gates_s", [128, 768], BF16); b_gates = Buf()
    gatesC = [sb(f"gatesC{k}_s", [128, 256], BF16) for k in range(2)]; b_gatesC = [Buf(), Buf()]
    vln = sb("vln_s", [128, 256], BF16); b_vln = Buf()
    qblk = {"B": sb("qblkB_s", [128, 2, 256], BF16), "M": sb("qblkM_s", [128, 2, 256], BF16)}
    b_qblk = {"B": Buf(), "M": Buf()}
    qblkC = [sb(f"qblkC{k}_s", [128, 2, 256], BF16) for k in range(2)]; b_qblkC = [Buf(), Buf()]
    iqblk = sb("iqblk_s", [128, 8, 128], BF16); b_iqblk = Buf()
    iw = sb("iw_s", [128, 8], F32); b_iw = Buf()
    PTA = [sb("PTA0_s", [128, 512], BF16)]; b_PTA = [Buf()]
    PTC = [sb(f"PTC{k}_s", [128, 512], BF16) for k in range(2)]; b_PTC = [Buf(), Buf()]
    stA = sb("stA_s", [128, 32], F32); b_stA = Buf()
    stB = sb("stB_s", [128, 32], F32); b_stB = Buf()
    recs = {m: sb(f"rec{m}_s", [128, 4], F32) for m in "BMC"}; b_recs = {m: Buf() for m in "BMC"}
    cst = sb("cst_s", [128, 4], F32); b_cst = Buf()
    bis = sb("bis_s", [128, 32], F32); b_bis = Buf()
    rk = sb("rk_s", [128, NBIS + 2], F32); b_rk = Buf()

    NA, NS, NC = 2, 2, 2
    bigA = [ps(f"bigA{k}", [128, 512], F32) for k in range(NA)]; b_bigA = [Buf(psum=True) for _ in range(NA)]
    bigS = [ps(f"bigS{k}", [128, 512], F32) for k in range(NS)]; b_bigS = [Buf(psum=True) for _ in range(NS)]
    bigC = [ps(f"bigC{k}", [128, 512], F32) for k in range(NC)]; b_bigC = [Buf(psum=True) for _ in range(NC)]
    _accS = ps("accS", [128, 512], F32); _baccS = Buf(psum=True)
    _accC = ps("accC", [128, 512], F32); _baccC = Buf(psum=True)
    acc = {"B": _accS, "M": _accS, "C": _accC}; b_acc = {"B": _baccS, "M": _baccS, "C": _baccC}
    rr = {"A": 0, "S": 0, "C": 0, "PTA": 0, "PTC": 0, "rtmp": 0, "cvt": 0, "stage": 0}

    def nxt(key, n):
        k = rr[key]
        rr[key] = (k + 1) % n
        return k

    def bankA():
        k = nxt("A", NA)
        return bigA[k], b_bigA[k]

    def bankS():
        k = nxt("S", NS)
        return bigS[k], b_bigS[k]

    def bankC():
        k = nxt("C", NC)
        return bigC[k], b_bigC[k]

    d_stage = [S.dma_sem("st0"), S.dma_sem("st1")]
    d_x = [S.dma_sem("x0"), S.dma_sem("x1")]
    d_o = [S.dma_sem("o0"), S.dma_sem("o1")]
    d_misc = {}

    def dsem(name):
        if name not in d_misc:
            d_misc[name] = S.dma_sem(name)
        return d_misc[name]

    V, A, G, T = nc.vector, nc.scalar, nc.gpsimd, nc.tensor
    ENG = {"dve": V, "act": A, "pool": G}

    cur_q = [None]

    def free_elems(ap):
        n = 1
        for d in tuple(ap.shape)[1:]:
            n *= int(d)
        return n

    def issue(eng, fn, reads, writes, dur, ds=None, holder=None):
        if cur_q[0] is not None:
            cur_q[0].append((eng, fn, list(reads), list(writes), dur, ds, holder))
            return None
        if ds is not None:
            tok = S.dma(eng, ds, fn, reads=reads, writes=writes)
            if holder is not None:
                holder[0][holder[1]] = tok
            return tok
        return S.op(eng, fn, reads=reads, writes=writes)

    def op(eng, name, reads, writes, *args, **kw):
        f = getattr(ENG[eng], name)
        o = kw.get("out", args[0] if args else None)
        n = free_elems(o) if o is not None else 1
        if eng == "dve":
            dur = 0.12 + n / 960.0 + (0.08 if "accum_out" in kw else 0.0)
        elif eng == "act":
            dur = 0.22 + n / 1200.0 + (0.1 if "accum_out" in kw else 0.0)
        else:
            dur = 0.3 + n / 480.0
        issue(eng, lambda: f(*args, **kw), reads, writes, dur)

    def mm(out, lhsT, rhs, start, reads, writes):
        dur = 0.1 + free_elems(rhs) / 1500.0
        issue("pe", lambda: T.matmul(out, lhsT, rhs, start=start, stop=True, skip_group_check=True), reads, writes, dur)

    def transpose(out, in_, reads, writes):
        idn = ident4[:, 0:128]
        issue("pe", lambda: T.transpose(out, in_, idn), reads + [b_id], writes, 0.2)

    def cast_copy(k, out, in_, reads, writes, scale=None):
        e = k % 3
        if scale is not None:
            if e == 1:
                op("act", "mul", reads, writes, out=out, in_=in_, mul=scale)
            else:
                op("dve" if e == 0 else "pool", "tensor_scalar", reads, writes, out=out, in0=in_, scalar1=scale, scalar2=None, op0=ALU.mult)
            return
        if e == 0:
            op("dve", "tensor_copy", reads, writes, out=out, in_=in_)
        elif e == 1:
            op("act", "copy", reads, writes, out=out, in_=in_)
        else:
            op("pool", "tensor_copy", reads, writes, out=out, in_=in_)

    def dma(ds, out, in_, reads, writes, holder=None):
        return issue("sp", lambda: nc.sync.dma_start(out=out, in_=in_), reads, writes, 3.0, ds=ds, holder=holder)

    def load_direct(name, dst, bdst, src):
        dma(dsem(name), dst, src, [], [bdst])

    def load_cvt(dst, bdst, src, L, post=None, scale=None):
        c0 = 0
        while c0 < L:
            n = min(2048, L - c0)
            h = nxt("stage", 2)
            stg = score[:, h * 2048:h * 2048 + n]
            dma(d_stage[h], stg, src[:, c0:c0 + n], [], [b_sc[h]])
            if post is None:
                cast_copy(nxt("cvt", 3), dst[:, c0:c0 + n], stg, [b_sc[h]], [bdst], scale=scale)
            else:
                post(dst[:, c0:c0 + n], stg, c0, n, h)
            c0 += n

    load_cvt(ident4, b_id, ident4_d, 512)
    load_cvt(Emat, b_E, E_d, 256)
    load_cvt(sel, b_sel, sel_d, 7 * 128)
    load_direct("pow2", pow2[:], b_pow2, pow2_d)
    op("dve", "memset", [], [b_cst], cst[:, 0:1], -0.5)
    load_direct("tmpb", tmpb[:], b_tmpb, cC_d)
    for d, (dstt, bd) in enumerate(((biasC0, b_biasC0), (biasC1, b_biasC1))):
        def post(dst, stg, c0, n, h, bd=bd):
            op("dve", "tensor_tensor", [b_sc[h], b_tmpb], [bd], out=dst, in0=stg, in1=tmpb[:, c0:c0 + n], op=ALU.subtract)
        load_cvt(dstt, bd, biasC_d[d], 512, post=post)
    load_cvt(maskB4, b_maskB4, maskB_d[1], 512)
    for m in "BM":
        op("pool", "memset", [], [b_qblk[m]], qblk[m][:], 0.0)
    for k in range(2):
        op("pool", "memset", [], [b_qblkC[k]], qblkC[k][:], 0.0)
    op("pool", "memset", [], [b_iqblk], iqblk[:], 0.0)
    op("pool", "memset", [], b_vC, vC[:, :, :, 64:65], 1.0)
    op("pool", "memset", [], b_vB, vB[:, :, :, 64:65], 1.0)
    op("pool", "memset", [], [b_vM], vM[:, :, :, 64:65], 1.0)

    def layer_norm(X, bX, width, gt, bg_, bt, bb_, stt, bst, out=None, bout=None):
        nch = (width + 511) // 512
        cw = width // nch
        for c in range(nch):
            op("dve", "bn_stats", [bX], [bst], out=stt[:, 8 + 6 * c:14 + 6 * c], in_=X[:, c * cw:(c + 1) * cw])
        op("dve", "bn_aggr", [bst], [bst], out=stt[:, 0:2], in_=stt[:, 8:8 + 6 * nch])
        op("dve", "tensor_scalar", [bst], [bst], out=stt[:, 2:3], in0=stt[:, 1:2], scalar1=LN_EPS, scalar2=None, op0=ALU.add)
        op("pool", "tensor_tensor", [bst, b_cst], [bst], out=stt[:, 3:4], in0=stt[:, 2:3], in1=cst[:, 0:1], op=ALU.pow)
        op("dve", "tensor_scalar", [bst], [bst], out=stt[:, 4:5], in0=stt[:, 0:1], scalar1=stt[:, 3:4], scalar2=-1.0, op0=ALU.mult, op1=ALU.mult)
        op("act", "activation", [bX, bst], [bX], out=X, in_=X, func=AF.Identity, bias=stt[:, 4:5], scale=stt[:, 3:4])
        op("dve", "tensor_tensor", [bX, bg_], [bX], out=X, in0=X, in1=gt, op=ALU.mult)
        if out is None:
            op("pool", "tensor_tensor", [bX, bb_], [bX], out=X, in0=X, in1=bt, op=ALU.add)
        else:
            op("pool", "tensor_tensor", [bX, bb_], [bout], out=out, in0=X, in1=bt, op=ALU.add)

    def attention(mixer, q, bq, keys, mixed_t, bmixed, mixed_off, gate_ap, bgate, bank_fn, PTs, bPTs, ptkey):
        oacc = acc[mixer]
        bacc = b_acc[mixer]
        nk = len(keys)
        pend = []
        npt = len(PTs)

        def stage1(jj):
            (k0, k1, rk_, vfn, rv_, extra) = keys[jj]
            ST, bST = bank_fn()
            mm(ST[:, 0:256], k0, q[:, 0, :], True, rk_ + [bq], [bST])
            mm(ST[:, 256:512], k1, q[:, 1, :], False, rk_ + [bq], [bST])
            for (l_, r_, rd_) in extra:
                mm(ST[:, :], l_, r_, False, rd_, [bST])
            pk = nxt(ptkey, npt)
            op("act", "activation", [bST], [bPTs[pk]], out=PTs[pk][:], in_=ST[:, :], func=AF.Exp)
            pend.append((jj, pk))

        def stage2():
            jj, pk = pend.pop(0)
            (k0, k1, rk_, vfn, rv_, extra) = keys[jj]
            for h in range(4):
                mm(oacc[:, h * 65:(h + 1) * 65], PTs[pk][:, h * 128:(h + 1) * 128], vfn(h), (jj == 0 and h == 0), [bPTs[pk]] + rv_, [bacc])

        look = min(2, npt)
        for jj in range(nk):
            stage1(jj)
            if len(pend) >= look:
                stage2()
            yield
        while pend:
            stage2()
        ov = oacc[:, 0:260].rearrange("p (h d) -> p h d", d=65)
        rec = recs[mixer]
        op("dve", "reciprocal", [bacc], [b_recs[mixer]], out=rec[:, 0:4], in_=ov[:, :, 64])
        for h in range(4):
            op("dve", "scalar_tensor_tensor", [bacc, b_recs[mixer], bgate], [bmixed],
               out=mixed_t[:, mixed_off + h * 64:mixed_off + (h + 1) * 64], in0=oacc[:, h * 65:h * 65 + 64],
               scalar=rec[:, h:h + 1], in1=gate_ap[:, h * 64:(h + 1) * 64], op0=ALU.mult, op1=ALU.mult)
        yield

    load_direct("lng", lng[:], b_lng, lnin_d[0])
    load_direct("lnb", lnb[:], b_lnb, lnin_d[1])
    last_store = [None, None]
    for i in range(nblk):
        s_ = i % 2
        X = xin[s_]
        dma(d_x[s_], X[:], x_d[i * 128:(i + 1) * 128, :], [], [b_x[s_]])
        layer_norm(X[:], b_x[s_], 1024, lng[:], b_lng, lnb[:], b_lnb, stB, b_stB)
        last_store[s_] = dma(d_o[s_], out_d[i * 128:(i + 1) * 128, :], X[:], [b_x[s_]], [b_o[i]])

    def setup_layer(l):
        load_cvt(wtm, b_wtm, wtm_d[l], 8 * NTM)
        load_cvt(wfm, b_wfm, wfm_d[l], 8 * NFM)
        load_cvt(wout, b_wout, wout_d[l], 8 * 1024, scale=0.5)
        load_cvt(brow, b_brow, brow_d[l], 512)
        load_cvt(wsT, b_wsT, wsT_d[l], 512)
        op("pool", "memset", [], [b_wsT], wsT[64:128, :].rearrange("p (g i) -> p g i", g=4)[:, :, 0:64], 0.0)
        load_cvt(absT, b_absT, absT_d[l], 128)
        load_direct("bfm", bfm[:], b_bfm, bfm_d[l])
        op("dve", "tensor_scalar", [b_bfm], [b_bfm8], out=bfm8[:], in0=bfm[:], scalar1=0.125, scalar2=None, op0=ALU.mult)
        load_direct("lng", lng[:], b_lng, lng_d[l, 0])
        load_direct("lnb", lnb[:], b_lnb, lng_d[l, 1])
        load_direct("alng", alng[:], b_alng, alng_d[l, 0])
        load_direct("alnb", alnb[:], b_alnb, alng_d[l, 1])
        load_direct("tmpb", tmpb[:], b_tmpb, cB_d[l])
        for m_, (dstt, bd) in enumerate(((biasB0, b_biasB0), (biasB1, b_biasB1))):
            def post(dst, stg, c0, n, h, bd=bd):
                op("dve", "tensor_tensor", [b_sc[h], b_tmpb], [bd], out=dst, in0=stg, in1=tmpb[:, c0:c0 + n], op=ALU.subtract)
            load_cvt(dstt, bd, biasB_d[l, m_], 512, post=post)
        load_direct("tmpb", tmpb[:], b_tmpb, maskB_d[0])
        op("dve", "tensor_tensor", [b_biasB0, b_tmpb], [b_biasB0], out=biasB0[:], in0=biasB0[:], in1=tmpb[:], op=ALU.add)
        for mt in range(2):
            X = xin[mt]
            dma(d_x[mt], X[:], mem_d[mt * 128:(mt + 1) * 128, :], [], [b_x[mt]])
            op("act", "copy", [b_x[mt]], [b_mixed[0]], out=mixed[0][:], in_=X[:])
            tb, btb = bankA()
            tv = tb[:, :].bitcast(BF16)
            for kt in range(8):
                transpose(tv[:, kt * 128:(kt + 1) * 128], mixed[0][:, kt * 128:(kt + 1) * 128], [b_mixed[0]], [btb])
            op("dve", "tensor_copy", [btb], b_rt, out=memT[:, :, mt * 128:(mt + 1) * 128], in_=tv.rearrange("p (a b) -> p a b", b=128))
        load_cvt(wmb, b_wmb, wmk_d[l], 8 * 256)
        for t in range(2):
            bk, bbk = bankA()
            for kt in range(8):
                mm(bk[:, 0:256], wmb[:, kt * 256 + t * 128:kt * 256 + (t + 1) * 128], memT[:, kt, :], kt == 0, [b_wmb] + b_rt, [bbk])
            op("dve", "tensor_copy", [bbk], [b_kmT], out=kmT[:, t, :], in_=bk[:, 0:256])
        load_cvt(wmb, b_wmb, wmv_d[l], 8 * 256)
        for mt in range(2):
            bk, bbk = bankA()
            for kt in range(8):
                mm(bk[:, 0:256], memT[:, kt, mt * 128:(mt + 1) * 128], wmb[:, kt * 256:(kt + 1) * 256], kt == 0, [b_wmb] + b_rt, [bbk])
            op("dve", "tensor_copy", [bbk], [b_vM], out=vM[:, mt, :, 0:64], in_=bk[:, 0:256].rearrange("p (h d) -> p h d", d=64))

    lo, hi, full = slice(0, 64), slice(64, 128), slice(0, 128)

    def ev(out, in0, prt, ct, scale, reads, writes):
        if ct < 10:
            if scale is None:
                op("act", "activation", reads, writes, out=out, in_=in0, func=AF.Identity, bias=bfm[prt, ct:ct + 1], scale=1.0)
            else:
                op("act", "activation", reads + [b_bfm8], writes, out=out, in_=in0, func=AF.Identity, bias=bfm8[prt, ct:ct + 1], scale=scale)
            return
        if scale is None:
            op("dve", "tensor_scalar", reads, writes, out=out, in0=in0, scalar1=bfm[prt, ct:ct + 1], scalar2=None, op0=ALU.add)
        else:
            op("dve", "tensor_scalar", reads, writes, out=out, in0=in0, scalar1=bfm[prt, ct:ct + 1], scalar2=scale, op0=ALU.add, op1=ALU.mult)

    def fm_group(i, cts, bank_fn):
        p_ = i % 2
        sl = i % 5
        bk, bbk = bank_fn()
        first = True
        for ci, ct in enumerate(cts):
            for kt in range(8):
                mm(bk[:, ci * 128:(ci + 1) * 128], wfm[:, kt * NFM + ct * 128:kt * NFM + (ct + 1) * 128], xnT[:, kt, :], first, [b_wfm, b_xnT], [bbk])
                first = False
        for ci, ct in enumerate(cts):
            pst = bk[:, ci * 128:(ci + 1) * 128]
            rd = [bbk, b_bfm]
            if ct in (0, 1, 8, 9):
                m = "B" if ct < 2 else "M"
                t = ct % 2
                ev(qblk[m][lo, t, 0:128], pst[lo, :], lo, ct, 0.125, rd, [b_qblk[m]])
                ev(qblk[m][hi, t, 128:256], pst[hi, :], hi, ct, 0.125, rd, [b_qblk[m]])
            elif ct in (4, 5):
                t = ct % 2
                ev(qblkC[p_][lo, t, 0:128], pst[lo, :], lo, ct, 0.125, rd, [b_qblkC[p_]])
                ev(qblkC[p_][hi, t, 128:256], pst[hi, :], hi, ct, 0.125, rd, [b_qblkC[p_]])
            elif ct in (2, 3):
                ev(kBT[:, ct - 2, sl * 128:(sl + 1) * 128], pst[:, :], full, ct, None, rd, [b_kB[sl]])
            elif ct in (6, 7):
                ev(kCT[:, ct - 6, i * 128:(i + 1) * 128], pst[:, :], full, ct, None, rd, [b_kC[i]])
            elif ct in (10, 11, 12, 13):
                t = ct - 10
                ev(iqblk[lo, 2 * t, :], pst[lo, :], lo, ct, None, rd, [b_iqblk])
                ev(iqblk[hi, 2 * t + 1, :], pst[hi, :], hi, ct, None, rd, [b_iqblk])
            else:
                ev(ikT[:, i * 128:(i + 1) * 128], pst[:, :], full, ct, None, rd, [b_ik[i]])

    def tm_tile(r, c0, n, bank_fn):
        bk, bbk = bank_fn()
        for kt in range(8):
            mm(bk[:, 0:n], xnT[:, kt, :], wtm[:, kt * NTM + c0:kt * NTM + c0 + n], kt == 0, [b_xnT, b_wtm], [bbk])
        mm(bk[:, 0:n], sel[:, r * 128:(r + 1) * 128], brow[:, 0:n], False, [b_sel, b_brow], [bbk])
        return bk, bbk

    def phaseMain(i):
        s_ = i % 2
        p_ = i % 2
        X = xin[s_]
        N_i = (i + 1) * 128
        mx = mixed[p_]
        bmx = b_mixed[p_]
        dma(d_x[s_], X[:], out_d[i * 128:(i + 1) * 128, :], [b_o[i]], [b_x[s_]])
        op("act", "copy", [b_x[s_]], [bmx], out=mx[:], in_=X[:])
        tb, btb = bankA()
        tv = tb[:, :].bitcast(BF16)
        for kt in range(8):
            transpose(tv[:, kt * 128:(kt + 1) * 128], mx[:, kt * 128:(kt + 1) * 128], [bmx], [btb])
        op("dve", "tensor_copy", [btb], [b_xnT], out=xnT[:].rearrange("p a b -> p (a b)"), in_=tv)
        yield "XNT"
        fm_group(i, [10, 11, 12, 13], bankA)
        yield 0.2
        fm_group(i, [14], bankA)
        bk, bbk = tm_tile(4, 2048, 8, bankA)
        op("dve", "tensor_copy", [bbk], [b_iw], out=iw[:], in_=bk[:, 0:8])
        yield 0.5
        ntile = (N_i + 511) // 512
        for c in range(ntile):
            c0 = c * 512
            n = min(512, N_i - c0)
            half = [b_sc[0]] if c0 + n <= 2048 else [b_sc[1]]
            rik = [b_ik[jj] for jj in range(c0 // 128, (c0 + n) // 128)]
            for h in range(8):
                bk, bbk = bankA()
                mm(bk[:, 0:n], iqblk[:, h, :], ikT[:, c0:c0 + n], True, [b_iqblk] + rik, [bbk])
                if h == 0:
                    op("dve", "tensor_scalar", [bbk, b_iw], half, out=score[:, c0:c0 + n], in0=bk[:, 0:n], scalar1=0.0, scalar2=iw[:, 0:1], op0=ALU.max, op1=ALU.mult)
                else:
                    r_ = nxt("rtmp", 2)
                    op("act", "activation", [bbk], [b_rtmp[r_]], out=rtmp[r_][:, 0:n], in_=bk[:, 0:n], func=AF.Relu)
                    op("dve", "scalar_tensor_tensor", [b_rtmp[r_], b_iw] + half, half, out=score[:, c0:c0 + n], in0=rtmp[r_][:, 0:n], scalar=iw[:, h:h + 1],
                       in1=score[:, c0:c0 + n], op0=ALU.mult, op1=ALU.add)
                yield W_IDX * n / 512.0
        SC = b_sc if N_i > 2048 else [b_sc[0]]
        lasthalf = [b_sc[1]] if N_i > 2048 else [b_sc[0]]
        op("pool", "memset", [], lasthalf, score[0:64, N_i - 64:N_i], -1e30)
        if i >= 2:
            op("dve", "tensor_reduce", SC, [b_bis], out=bis[:, 0:1], in_=score[:, 0:N_i], axis=AX.X, op=ALU.max)
            op("dve", "tensor_reduce", SC, [b_bis], out=bis[:, 1:2], in_=score[:, 0:N_i - 64], axis=AX.X, op=ALU.min)
            op("dve", "tensor_tensor", [b_bis], [b_bis], out=bis[:, 2:3], in0=bis[:, 0:1], in1=bis[:, 1:2], op=ALU.subtract)
            op("dve", "tensor_scalar", [b_bis, b_pow2], [b_rk], out=rk[:, :], in0=pow2[:, :], scalar1=bis[:, 2:3], scalar2=None, op0=ALU.mult)
            op("dve", "tensor_tensor", [b_bis, b_rk], [b_bis], out=bis[:, 8:9], in0=bis[:, 1:2], in1=rk[:, 1:2], op=ALU.add)
            yield 1.0 + N_i / 500.0
            nd = max(64, int(round(N_i * DVE_FRAC / 64)) * 64, N_i - 2048)
            na = N_i - nd
            SCd = [b_sc[0]] if nd <= 2048 else b_sc
            SCa = [b_sc[1]] if nd >= 2048 else (b_sc if N_i > 2048 else [b_sc[0]])
            for k in range(1, NBIS + 1):
                mid = bis[:, 8 + (k - 1) % 2:9 + (k - 1) % 2]
                midn = bis[:, 8 + k % 2:9 + k % 2]
                op("dve", "tensor_scalar", SCd + [b_bis], [b_rt[0], b_cntD], out=junk[:, 0:nd], in0=score[:, 0:nd], scalar1=mid, scalar2=None,
                   op0=ALU.is_ge, op1=ALU.add, accum_out=cntD[:, 0:1])
                op("act", "activation", SCa + [b_bis], [b_rt[1], b_cntA], out=junk[:, 2048:2048 + na], in_=score[:, nd:N_i], func=AF.Sign, bias=mid, scale=-1.0,
                   accum_out=cntA[:, 0:1])
                op(SMALL_ENG, "tensor_scalar", [b_cntD, b_cntA], [b_bis], out=bis[:, 4:5], in0=cntD[:, 0:1], scalar1=2.0, scalar2=cntA[:, 0:1], op0=ALU.mult, op1=ALU.subtract)
                op(SMALL_ENG, "tensor_scalar", [b_bis, b_rk], [b_bis], out=bis[:, 5:6], in0=bis[:, 4:5], scalar1=float(512 - na), scalar2=rk[:, k:k + 1], op0=ALU.is_ge, op1=ALU.mult)
                kk = k + 1 if k < NBIS else k
                op(SMALL_ENG, "tensor_scalar", [b_bis, b_rk], [b_bis], out=midn, in0=bis[:, 5:6], scalar1=rk[:, kk:kk + 1], scalar2=mid, op0=ALU.subtract, op1=ALU.add)
                yield 1.2 + N_i / 1100.0
            thr = bis[:, 8 + NBIS % 2:9 + NBIS % 2]
        else:
            op("dve", "memset", [], [b_bis], bis[:, 12:13], -1e29)
            thr = bis[:, 12:13]
        yield "B_DONE"
        c0 = 0
        while c0 < N_i:
            n = min(2048, N_i - c0)
            op("dve", "tensor_scalar", [b_sc[c0 // 2048], b_bis], [b_mb], out=mb[:, c0:c0 + n], in0=score[:, c0:c0 + n], scalar1=thr, scalar2=NEG, op0=ALU.is_lt, op1=ALU.mult)
            c0 += n
        yield 0.0

    def main_weight(i):
        N_i = (i + 1) * 128
        w = 1.2 + W_IDX * 8 * N_i / 512.0
        if i >= 2:
            w += 1.0 + N_i / 500.0 + NBIS * (1.2 + N_i / 1100.0)
        return w

    def phaseSide(i):
        s_ = i % 2
        p_ = i % 2
        sl = i % 5
        mx = mixed[p_]
        bmx = b_mixed[p_]
        bk, bbk = tm_tile(0, 0, 512, bankS)
        op("act", "copy", [bbk], [b_xg], out=xg[:], in_=bk[:, :])
        op("pool", "tensor_tensor", [b_xg], [b_gs], out=gsc0[:], in0=xg[:], in1=xg[:], op=ALU.mult)
        op("dve", "scalar_tensor_tensor", [b_gs, b_xg], [b_gs], out=gsc0[:], in0=gsc0[:], scalar=0.044715, in1=xg[:], op0=ALU.mult, op1=ALU.mult)
        op("pool", "tensor_tensor", [b_gs, b_xg], [b_gs], out=gsc0[:], in0=gsc0[:], in1=xg[:], op=ALU.add)
        op("act", "activation", [b_gs], [b_gs], out=gsc0[:], in_=gsc0[:], func=AF.Tanh, scale=0.7978845608028654)
        op("dve", "scalar_tensor_tensor", [b_gs, b_xg], [b_xg], out=xg[:], in0=gsc0[:], scalar=1.0, in1=xg[:], op0=ALU.add, op1=ALU.mult)
        yield
        bk, bbk = tm_tile(1, 512, 512, bankS)
        op("act", "activation", [bbk], [b_gs], out=gsc0[:], in_=bk[:, :], func=AF.Tanh, scale=0.5)
        op("dve", "scalar_tensor_tensor", [b_gs, bbk], [b_gates], out=gates[:, 0:512], in0=gsc0[:], scalar=1.0, in1=bk[:, :], op0=ALU.add, op1=ALU.mult)
        yield
        bk, bbk = tm_tile(2, 1024, 512, bankS)
        op("act", "activation", [bbk], [b_gs], out=gsc0[:], in_=bk[:, :], func=AF.Tanh, scale=0.5)
        op("dve", "scalar_tensor_tensor", [b_gs, bbk], [b_gatesC[p_]], out=gatesC[p_][:, :], in0=gsc0[:, 0:256], scalar=1.0, in1=bk[:, 0:256], op0=ALU.add, op1=ALU.mult)
        op("dve", "scalar_tensor_tensor", [b_gs, bbk], [b_gates], out=gates[:, 512:768], in0=gsc0[:, 256:512], scalar=1.0, in1=bk[:, 256:512], op0=ALU.add, op1=ALU.mult)
        yield
        bk, bbk = tm_tile(3, 1536, 512, bankS)
        op("act", "copy", [bbk], [b_vB[sl]], out=vB[:, sl, :, 0:64], in_=bk[:, 0:256].rearrange("p (h d) -> p h d", d=64))
        op("act", "copy", [bbk], [b_vC[i]], out=vC[:, i, :, 0:64], in_=bk[:, 256:512].rearrange("p (h d) -> p h d", d=64))
        yield
        for cts in ([0, 1, 2, 3], [4, 5, 6, 7], [8, 9]):
            fm_group(i, cts, bankS)
            yield
        op("act", "mul", [b_xg], [b_xg], out=xg[:, 256:512], in_=xg[:, 256:512], mul=0.5)
        layer_norm(xg[:, 256:512], b_xg, 256, alng[:], b_alng, alnb[:], b_alnb, stA, b_stA, out=vln[:], bout=b_vln)
        bk, bbk = bankS()
        for g in range(4):
            mm(bk[:, g * 64:(g + 1) * 64], wsT[:, g * 128:(g + 1) * 128], vln[:, g * 64:(g + 1) * 64], g == 0, [b_wsT, b_vln], [bbk])
        mm(bk[:, 0:256], absT[:, :], Emat[:, :], False, [b_absT, b_E], [bbk])
        op("dve", "scalar_tensor_tensor", [bbk, b_xg], [b_xg], out=xg[:, 256:512], in0=bk[:, 0:256], scalar=0.5, in1=xg[:, 0:256], op0=ALU.mult, op1=ALU.mult)
        op("pool", "tensor_tensor", [b_xg, b_gates], [bmx], out=mx[:, 0:256], in0=xg[:, 256:512], in1=gates[:, 0:256], op=ALU.mult)
        yield
        keys = []
        for j in range(max(0, i - 4), i + 1):
            sj = j % 5
            m_ = i - j
            extra = []
            if m_ == 0:
                extra.append((ident4[:, 0:128], biasB0[:, :], [b_id, b_biasB0]))
            elif m_ == 1:
                extra.append((ident4[:, 0:128], biasB1[:, :], [b_id, b_biasB1]))
            elif m_ == 4:
                extra.append((ident4[:, 0:128], maskB4[:, :], [b_id, b_maskB4]))
            keys.append((kBT[:, 0, sj * 128:(sj + 1) * 128], kBT[:, 1, sj * 128:(sj + 1) * 128], [b_kB[sj]],
                         (lambda h, sj=sj: vB[:, sj, h, :]), [b_vB[sj]], extra))
        yield from attention("B", qblk["B"], b_qblk["B"], keys, mx, bmx, 256, gates[:, 256:512], b_gates, bankS, PTA, b_PTA, "PTA")
        keys = []
        for mt in range(2):
            keys.append((kmT[:, 0, mt * 128:(mt + 1) * 128], kmT[:, 1, mt * 128:(mt + 1) * 128], [b_kmT],
                         (lambda h, mt=mt: vM[:, mt, h, :]), [b_vM], []))
        yield from attention("M", qblk["M"], b_qblk["M"], keys, mx, bmx, 768, gates[:, 512:768], b_gates, bankS, PTA, b_PTA, "PTA")

    def phaseB(i, l):
        s_ = i % 2
        p_ = i % 2
        X = xin[s_]
        mx = mixed[p_]
        bmx = b_mixed[p_]
        keys = []
        for j in range(0, i + 1):
            extra = [(mb[:, j * 128:(j + 1) * 128], ident4[:, :], [b_mb, b_id])]
            d_ = i - j
            if d_ == 0:
                extra.append((ident4[:, 0:128], biasC0[:, :], [b_id, b_biasC0]))
            elif d_ == 1:
                extra.append((ident4[:, 0:128], biasC1[:, :], [b_id, b_biasC1]))
            keys.append((kCT[:, 0, j * 128:(j + 1) * 128], kCT[:, 1, j * 128:(j + 1) * 128], [b_kC[j]],
                         (lambda h, j=j: vC[:, j, h, :]), [b_vC[j]], extra))
        yield from attention("C", qblkC[p_], b_qblkC[p_], keys, mx, bmx, 512, gatesC[p_][:, :], b_gatesC[p_], bankC, PTC, b_PTC, "PTC")
        tb, btb = bankC()
        tv = tb[:, :].bitcast(BF16)
        for kt in range(8):
            transpose(tv[:, kt * 128:(kt + 1) * 128], mx[:, kt * 128:(kt + 1) * 128], [bmx], [btb])
        op("dve", "tensor_copy", [btb], [b_mixedT], out=mixedT[:].rearrange("p a b -> p (a b)"), in_=tv)
        yield
        for c in range(2):
            bk, bbk = bankC()
            for kt in range(8):
                mm(bk[:, :], mixedT[:, kt, :], wout[:, kt * 1024 + c * 512:kt * 1024 + (c + 1) * 512], kt == 0, [b_mixedT, b_wout], [bbk])
            mm(bk[:, :], sel[:, (5 + c) * 128:(6 + c) * 128], brow[:, :], False, [b_sel, b_brow], [bbk])
            op("dve", "scalar_tensor_tensor", [b_x[s_], bbk], [b_x[s_]], out=X[:, c * 512:(c + 1) * 512], in0=X[:, c * 512:(c + 1) * 512], scalar=ALPHA, in1=bk[:, :],
               op0=ALU.mult, op1=ALU.add)
            yield
        layer_norm(X[:], b_x[s_], 1024, lng[:], b_lng, lnb[:], b_lnb, stB, b_stB)
        dma(d_o[s_], out_d[i * 128:(i + 1) * 128, :], X[:], [b_x[s_]], [b_o[i]], holder=(last_store, s_))
        yield

    def drain(g):
        for _ in g:
            pass

    class Stream:
        def __init__(self, gen, prio):
            self.gen = gen
            self.q = []
            self.done = False
            self.prio = prio
            self.active = True

    eng_free = {e: 0.0 for e in ENGS}
    tok_done = {}

    def fill(st):
        while not st.q and not st.done:
            cur_q[0] = st.q
            try:
                y = next(st.gen)
                if isinstance(y, str):
                    st.q.append(y)
            except StopIteration:
                st.done = True
            finally:
                cur_q[0] = None

    def ready_time(eng, reads, writes):
        t = 0.0
        for bf in reads:
            if bf.writer is not None:
                t = max(t, tok_done.get((id(bf.writer[0]), bf.writer[1]), 0.0) + (0.0 if bf.writer[2] == eng else 0.12))
            if bf.psum:
                for tk in bf.readers.values():
                    if tk[2] != eng:
                        t = max(t, tok_done.get((id(tk[0]), tk[1]), 0.0) + 0.12)
        for bf in writes:
            if bf.writer is not None:
                t = max(t, tok_done.get((id(bf.writer[0]), bf.writer[1]), 0.0) + (0.0 if bf.writer[2] == eng else 0.12))
            for tk in bf.readers.values():
                if tk[2] != eng:
                    t = max(t, tok_done.get((id(tk[0]), tk[1]), 0.0) + 0.12)
        return t

    def commit(item, start):
        eng, fn, reads, writes, dur, ds, holder = item
        if ds is not None:
            tok = S.dma(eng, ds, fn, reads=reads, writes=writes)
            if holder is not None:
                holder[0][holder[1]] = tok
            eng_free[eng] = start + 0.1
        else:
            tok = S.op(eng, fn, reads=reads, writes=writes)
            eng_free[eng] = start + dur
        tok_done[(id(tok[0]), tok[1])] = start + dur

    def run_round(gmain, gside, gB):
        main = Stream(gmain, 0.0)
        side = Stream(gside, 0.0) if gside is not None else None
        stB = Stream(gB, 0.0) if gB is not None else None
        if side is not None:
            side.active = False
        streams = [x for x in (main, stB, side) if x is not None]
        while True:
            best = None
            alive = False
            for st in streams:
                fill(st)
                if not st.q:
                    continue
                alive = True
                if not st.active:
                    continue
                item = st.q[0]
                if isinstance(item, str):
                    if item == "XNT":
                        st.q.pop(0)
                        if side is not None:
                            side.active = True
                        best = "again"
                        break
                    if item == "B_DONE":
                        if stB is None or (stB.done and not stB.q):
                            st.q.pop(0)
                            best = "again"
                            break
                        continue
                    st.q.pop(0)
                    best = "again"
                    break
                eng = item[0]
                start = max(eng_free[eng], ready_time(eng, item[2], item[3]))
                key = start - st.prio
                if best is None or key < best[0]:
                    best = (key, start, st)
            if best == "again":
                continue
            if best is None:
                if not alive:
                    break
                raise RuntimeError("scheduler stuck")
            _, start, st = best
            commit(st.q.pop(0), start)

    for l in range(n_layers):
        setup_layer(l)
        if not pipeline:
            for i in range(nblk):
                drain(phaseMain(i))
                drain(phaseSide(i))
                drain(phaseB(i, l))
        else:
            run_round(phaseMain(0), phaseSide(0), None)
            for i in range(nblk):
                if i + 1 < nblk:
                    run_round(phaseMain(i + 1), phaseSide(i + 1), phaseB(i, l))
                else:
                    drain(phaseB(i, l))

    S.wait_tokens("sp", [t for t in last_store if t is not None])
    S.emit()
    es.close()
    return nc


def _t5_bucket_idx(rel):
    nb = 16
    max_exact = 8
    ret = np.where(rel > 0, nb, 0)
    n = np.abs(rel)
    nf = np.maximum(n, 1).astype(np.float32)
    large = max_exact + (np.log(nf / np.float32(max_exact)) / np.float32(np.log(128 / max_exact)) * (nb - max_exact)).astype(np.int32)
    large = np.minimum(large, nb - 1)
    return ret + np.where(n < max_exact, n, large)


def _kt_layout(w):
    C = w.shape[1]
    return np.ascontiguousarray(w.reshape(8, 128, C).transpose(1, 0, 2).reshape(128, 8 * C))


def prepare(inputs):
    f = np.float32
    cum = np.cumsum((0,) + SPLITS)
    rng = {n: (int(cum[k]), int(cum[k + 1])) for k, n in enumerate(NAMES)}
    tm_cols = np.concatenate([np.arange(*rng[n]) for n in TM_ORDER])
    fm_cols = np.concatenate([np.arange(*rng[n]) for n in FM_ORDER])
    w_in = np.asarray(inputs["w_in"], f)
    b_in = np.asarray(inputs["b_in"], f)
    sh = {}
    sh["wtm"] = np.stack([_kt_layout(w_in[l][:, tm_cols]) for l in range(DEPTH)])
    sh["wfm"] = np.stack([_kt_layout(w_in[l][:, fm_cols]) for l in range(DEPTH)])
    sh["wout"] = np.stack([_kt_layout(np.asarray(inputs["w_out"], f)[l]) for l in range(DEPTH)])
    wm = np.asarray(inputs["w_mem_kv"], f)
    sh["wmk"] = np.stack([_kt_layout(wm[l][:, 0:256]) for l in range(DEPTH)])
    sh["wmv"] = np.stack([_kt_layout(wm[l][:, 256:512]) for l in range(DEPTH)])
    brow = np.zeros((DEPTH, 128, 512), f)
    btm = b_in[:, tm_cols]
    for r in range(4):
        brow[:, r, :] = btm[:, r * 512:(r + 1) * 512]
    brow[:, 4, 0:8] = btm[:, 2048:2056]
    b_out = np.asarray(inputs["b_out"], f)
    brow[:, 5, :] = b_out[:, 0:512]
    brow[:, 6, :] = b_out[:, 512:1024]
    sh["brow"] = brow
    sel = np.zeros((128, 7, 128), f)
    for r in range(7):
        sel[r, r, :] = 1.0
    sh["sel"] = sel.reshape(128, 7 * 128)
    sh["bfm"] = np.ascontiguousarray(b_in[:, fm_cols].reshape(DEPTH, 15, 128).transpose(0, 2, 1))
    sh["lnin"] = np.stack([np.broadcast_to(np.asarray(inputs["ln_in_g"], f), (128, 1024)),
                           np.broadcast_to(np.asarray(inputs["ln_in_b"], f), (128, 1024))]).copy()
    sh["lng"] = np.stack([np.stack([np.broadcast_to(np.asarray(inputs["ln_g"], f)[l], (128, 1024)),
                                    np.broadcast_to(np.asarray(inputs["ln_b"], f)[l], (128, 1024))]) for l in range(DEPTH)]).copy()
    sh["alng"] = np.stack([np.stack([np.broadcast_to(np.asarray(inputs["a_ln_g"], f)[l], (128, 256)),
                                     np.broadcast_to(np.asarray(inputs["a_ln_b"], f)[l], (128, 256))]) for l in range(DEPTH)]).copy()
    a_ws = np.asarray(inputs["a_ws"], f)
    sh["wsT"] = np.ascontiguousarray(a_ws.transpose(0, 3, 1, 2).reshape(DEPTH, 128, 512))
    ab = np.zeros((DEPTH, 128, 128), f)
    ab[:, 0:4, :] = np.asarray(inputs["a_bs"], f)
    sh["absT"] = ab
    b_rel = np.asarray(inputs["b_rel"], f)
    a = np.arange(128)[:, None]
    b = np.arange(128)[None, :]
    bB = np.zeros((DEPTH, 2, 128, 4, 128), f)
    for m in range(2):
        idx = np.clip(128 * m + b - a, -128, 128) + 128
        for l in range(DEPTH):
            bB[l, m] = b_rel[l][:, idx].transpose(1, 0, 2)
    sh["biasB"] = bB.reshape(DEPTH, 2, 128, 512)
    sh["cB"] = np.ascontiguousarray(np.broadcast_to(b_rel[:, None, :, 256, None], (DEPTH, 128, 4, 128)).reshape(DEPTH, 128, 512))
    t5 = np.asarray(inputs["t5_table"], f)
    bC = np.zeros((2, 128, 4, 128), f)
    for d in range(2):
        relkq = -128 * d + a - b
        bi = _t5_bucket_idx(relkq)
        bC[d] = t5[bi].transpose(0, 2, 1)
    sh["biasC"] = bC.reshape(2, 128, 512)
    sh["cC"] = np.ascontiguousarray(np.broadcast_to(t5[15][None, :, None], (128, 4, 128)).reshape(128, 512))
    mB = np.zeros((2, 128, 4, 128), f)
    mB[0][64:128, :, 0:64] = NEG
    mB[1][0:64, :, 64:128] = NEG
    sh["maskB"] = mB.reshape(2, 128, 512)
    E = np.zeros((128, 256), f)
    for g in range(4):
        E[g, g * 64:(g + 1) * 64] = 1.0
    sh["Emat"] = E
    sh["ident4"] = np.tile(np.eye(128, dtype=f), (1, 4))
    sh["pow2"] = np.broadcast_to((2.0 ** -np.arange(NBIS + 2)).astype(f)[None, :], (128, NBIS + 2)).copy()
    return sh


_NC_CACHE = {}


def kernel(**inputs):
    sh = prepare(inputs)
    x = np.asarray(inputs["x"], np.float32)
    mem = np.asarray(inputs["mem"], np.float32)
    if "nc" not in _NC_CACHE:
        _NC_CACHE["nc"] = build_program()
    nc = _NC_CACHE["nc"]
    in_maps = []
    for c in range(BATCH):
        m = dict(sh)
        m["x"] = np.ascontiguousarray(x[c])
        m["mem"] = np.ascontiguousarray(mem[c])
        in_maps.append(m)
    res = run_bass_kernel_spmd(nc, in_maps, core_ids=list(range(BATCH)))
    return np.stack([np.asarray(r["out"]) for r in res.results]).astype(np.float32)
```

```python
import numpy as np
import concourse.bass as bass
import concourse.mybir as mybir
from concourse.bass_utils import run_bass_kernel_spmd
from contextlib import ExitStack

F32 = mybir.dt.float32
BF16 = mybir.dt.bfloat16
U8 = mybir.dt.uint8
ALU = mybir.AluOpType
AF = mybir.ActivationFunctionType
AX = mybir.AxisListType

D_MODEL = 1024
BATCH = 8
SEQ = 4096
DEPTH = 2
NBLK = SEQ // 128
GW = 256
SPLITS = (GW, GW, GW, GW, GW, GW, GW, GW, GW, GW, GW, 512, 64, 8, GW, GW)
NAMES = ("a_u", "a_v", "a_g", "bq", "bk", "bv", "bg", "cq", "ck", "cv", "cg", "iq", "ik", "iw", "mq", "mg")
TM_ORDER = ("a_u", "a_v", "a_g", "bg", "cg", "mg", "bv", "cv", "iw")
FM_ORDER = ("bq", "bk", "cq", "ck", "mq", "iq", "ik", "ik")
NTM = 2056
NFM = 1920
ALPHA = (2 * DEPTH) ** 0.25
LN_EPS = 1e-5
NBIS = 12
DVE_FRAC = 0.35
W_IDX = 0.12
SMALL_ENG = "pool"
NEG = -30000.0

ENGS = ("pe", "act", "dve", "pool", "sp")
STRICT_WAR = False


class Buf:
    __slots__ = ("name", "writer", "readers", "psum")

    def __init__(self, name="", psum=False):
        self.name = name
        self.writer = None
        self.readers = {}
        self.psum = psum


class DmaSem:
    def __init__(self, sem):
        self.sem = sem
        self.count = 0


class Sched:
    EPOCH = 12000

    def __init__(self, nc, es):
        self.nc = nc
        self.es = es
        self.prog = {e: [] for e in ENGS}
        self.cnt = {e: 0 for e in ENGS}
        self.nsem = 0
        self.cursem = {e: self._newsem(e) for e in ENGS}
        self.waited = {e: {} for e in ENGS}
        self.semobj = {}

    def _newsem(self, name):
        self.nsem += 1
        return self.es.enter_context(self.nc.semaphore(f"s_{name}_{self.nsem}"))

    def dma_sem(self, name):
        return DmaSem(self._newsem("d" + name))

    def _deps(self, eng, reads, writes):
        need = {}

        def add(tok, kind):
            if tok is None:
                return
            sem, val, teng = tok
            if teng == eng:
                if eng == "pe" or (kind == "war" and not STRICT_WAR):
                    return
            k = id(sem)
            self.semobj[k] = sem
            if need.get(k, 0) < val:
                need[k] = val

        for b in reads:
            add(b.writer, "raw")
            if b.psum:
                for t in b.readers.values():
                    if t[2] != eng:
                        add(t, "rar")
        for b in writes:
            add(b.writer, "waw")
            for t in b.readers.values():
                add(t, "war")
        waits = []
        w = self.waited[eng]
        for k, val in need.items():
            if w.get(k, 0) < val:
                w[k] = val
                waits.append((self.semobj[k], val))
        return waits

    def _mark(self, tok, reads, writes):
        k = id(tok[0])
        for b in reads:
            b.readers[k] = tok
        for b in writes:
            b.writer = tok
            b.readers = {}

    def op(self, eng, fn, reads=(), writes=()):
        waits = self._deps(eng, reads, writes)
        if self.cnt[eng] >= self.EPOCH:
            self.cursem[eng] = self._newsem(eng)
            self.cnt[eng] = 0
        self.cnt[eng] += 1
        tok = (self.cursem[eng], self.cnt[eng], eng)
        self.prog[eng].append((waits, fn, (self.cursem[eng], 1)))
        self._mark(tok, reads, writes)
        return tok

    def dma(self, eng, ds, fn, reads=(), writes=()):
        waits = self._deps(eng, reads, writes)
        ds.count += 16
        tok = (ds.sem, ds.count, "dma")
        self.prog[eng].append((waits, fn, (ds.sem, 16)))
        self._mark(tok, reads, writes)
        return tok

    def wait_tokens(self, eng, toks):
        self.prog[eng].append(([(t[0], t[1]) for t in toks], None, None))

    def emit(self):
        nc = self.nc
        engobj = {"pe": nc.tensor, "act": nc.scalar, "dve": nc.vector, "pool": nc.gpsimd, "sp": nc.sync}

        def replay(e):
            eo = engobj[e]
            for waits, fn, inc in self.prog[e]:
                for sem, val in waits:
                    eo.wait_ge(sem, val)
                if fn is not None:
                    fn().then_inc(inc[0], inc[1])

        with nc.Block() as block:
            @block.tensor
            def _(x):
                replay("pe")

            @block.scalar
            def _(x):
                replay("act")

            @block.vector
            def _(x):
                replay("dve")

            @block.gpsimd
            def _(x):
                replay("pool")

            @block.sync
            def _(x):
                replay("sp")


def build_program(n_layers=DEPTH, nblk=NBLK, pipeline=True):
    nc = bass.Bass("TRN2", target_bir_lowering=False)
    es = ExitStack()
    S = Sched(nc, es)

    def din(name, shape):
        return nc.dram_tensor(name, list(shape), F32, kind="ExternalInput").ap()

    x_d = din("x", [SEQ, D_MODEL])
    mem_d = din("mem", [256, D_MODEL])
    lnin_d = din("lnin", [2, 128, 1024])
    wtm_d = din("wtm", [DEPTH, 128, 8 * NTM])
    wfm_d = din("wfm", [DEPTH, 128, 8 * NFM])
    wout_d = din("wout", [DEPTH, 128, 8 * 1024])
    wmk_d = din("wmk", [DEPTH, 128, 8 * 256])
    wmv_d = din("wmv", [DEPTH, 128, 8 * 256])
    brow_d = din("brow", [DEPTH, 128, 512])
    sel_d = din("sel", [128, 7 * 128])
    bfm_d = din("bfm", [DEPTH, 128, 15])
    lng_d = din("lng", [DEPTH, 2, 128, 1024])
    alng_d = din("alng", [DEPTH, 2, 128, 256])
    wsT_d = din("wsT", [DEPTH, 128, 512])
    absT_d = din("absT", [DEPTH, 128, 128])
    biasB_d = din("biasB", [DEPTH, 2, 128, 512])
    cB_d = din("cB", [DEPTH, 128, 512])
    biasC_d = din("biasC", [2, 128, 512])
    cC_d = din("cC", [128, 512])
    maskB_d = din("maskB", [2, 128, 512])
    E_d = din("Emat", [128, 256])
    ident4_d = din("ident4", [128, 512])
    pow2_d = din("pow2", [128, NBIS + 2])
    out_d = nc.dram_tensor("out", [SEQ, D_MODEL], F32, kind="ExternalOutput").ap()
    b_o = [Buf() for _ in range(NBLK)]

    def sb(name, shape, dt):
        return es.enter_context(nc.sbuf_tensor(name, list(shape), dt))

    def ps(name, shape, dt):
        return es.enter_context(nc.psum_tensor(name, list(shape), dt))

    wtm = sb("wtm_s", [128, 8 * NTM], BF16); b_wtm = Buf()
    wfm = sb("wfm_s", [128, 8 * NFM], BF16); b_wfm = Buf()
    wout = sb("wout_s", [128, 8 * 1024], BF16); b_wout = Buf()
    brow = sb("brow_s", [128, 512], BF16); b_brow = Buf()
    sel = sb("sel_s", [128, 7 * 128], BF16); b_sel = Buf()
    bfm = sb("bfm_s", [128, 15], F32); b_bfm = Buf()
    bfm8 = sb("bfm8_s", [128, 15], F32); b_bfm8 = Buf()
    lng = sb("lng_s", [128, 1024], F32); b_lng = Buf()
    lnb = sb("lnb_s", [128, 1024], F32); b_lnb = Buf()
    alng = sb("alng_s", [128, 256], F32); b_alng = Buf()
    alnb = sb("alnb_s", [128, 256], F32); b_alnb = Buf()
    wsT = sb("wsT_s", [128, 512], BF16); b_wsT = Buf()
    absT = sb("absT_s", [128, 128], BF16); b_absT = Buf()
    Emat = sb("E_s", [128, 256], BF16); b_E = Buf()
    ident4 = sb("ident4_s", [128, 512], BF16); b_id = Buf()
    pow2 = sb("pow2_s", [128, NBIS + 2], F32); b_pow2 = Buf()
    biasB0 = sb("biasB0_s", [128, 512], BF16); biasB1 = sb("biasB1_s", [128, 512], BF16)
    maskB4 = sb("maskB4_s", [128, 512], BF16)
    biasC0 = sb("biasC0_s", [128, 512], BF16); biasC1 = sb("biasC1_s", [128, 512], BF16)
    b_biasB0, b_biasB1, b_maskB4, b_biasC0, b_biasC1 = Buf(), Buf(), Buf(), Buf(), Buf()

    kCT = sb("kCT_s", [128, 2, SEQ], BF16); b_kC = [Buf() for _ in range(NBLK)]
    vC = sb("vC_s", [128, NBLK, 4, 65], BF16); b_vC = [Buf() for _ in range(NBLK)]
    ikT = sb("ikT_s", [128, SEQ], BF16); b_ik = [Buf() for _ in range(NBLK)]
    kBT = sb("kBT_s", [128, 2, 5 * 128], BF16); b_kB = [Buf() for _ in range(5)]
    vB = sb("vB_s", [128, 5, 4, 65], BF16); b_vB = [Buf() for _ in range(5)]
    kmT = sb("kmT_s", [128, 2, 256], BF16); b_kmT = Buf()
    vM = sb("vM_s", [128, 2, 4, 65], BF16); b_vM = Buf()

    score = sb("score_s", [128, SEQ], F32); b_sc = [Buf(), Buf()]
    mb = sb("mb_s", [128, SEQ], BF16); b_mb = Buf()
    wmb = mb[:, 0:2048]; b_wmb = b_mb
    gsc0 = sb("gscr_s", [128, 512], F32); b_gs = Buf()
    rt = sb("rt_s", [128, 1024], F32); b_rt = [Buf(), Buf()]
    rtmp = [rt[:, 0:512], rt[:, 512:1024]]; b_rtmp = b_rt
    junk = rt[:, :].bitcast(U8)
    memT = rt[:, :].bitcast(BF16).rearrange("p (a b) -> p a b", b=256)
    cntD = sb("cntD_s", [128, 2], F32); b_cntD = Buf()
    cntA = sb("cntA_s", [128, 2], F32); b_cntA = Buf()
    xin = [sb(f"xin{k}_s", [128, 1024], F32) for k in range(2)]; b_x = [Buf(), Buf()]
    mixed = [sb(f"mixed{k}_s", [128, 1024], BF16) for k in range(2)]; b_mixed = [Buf(), Buf()]
    xnT = sb("xnT_s", [128, 8, 128], BF16); b_xnT = Buf()
    mixedT = sb("mixedT_s", [128, 8, 128], BF16); b_mixedT = Buf()
    xg = sb("xg_s", [128, 512], F32); b_xg = Buf()
    tmpb = xg; b_tmpb = b_xg
    gates = sb("gates_s", [128, 768], BF16); b_gates = Buf()
    gatesC = [sb(f"gatesC{k}_s", [128, 256], BF16) for k in range(2)]; b_gatesC = [Buf(), Buf()]
    vln = sb("vln_s", [128, 256], BF16); b_vln = Buf()
    qblk = {"B": sb("qblkB_s", [128, 2, 256], BF16), "M": sb("qblkM_s", [128, 2, 256], BF16)}
    b_qblk = {"B": Buf(), "M": Buf()}
    qblkC = [sb(f"qblkC{k}_s", [128, 2, 256], BF16) for k in range(2)]; b_qblkC = [Buf(), Buf()]
    iqblk = sb("iqblk_s", [128, 8, 128], BF16); b_iqblk = Buf()
    iw = sb("iw_s", [128, 8], F32); b_iw = Buf()
    PTA = [sb("PTA0_s", [128, 512], BF16)]; b_PTA = [Buf()]
    PTC = [sb(f"PTC{k}_s", [128, 512], BF16) for k in range(2)]; b_PTC = [Buf(), Buf()]
    stA = sb("stA_s", [128, 32], F32); b_stA = Buf()
    stB = sb("stB_s", [128, 32], F32); b_stB = Buf()
    recs = {m: sb(f"rec{m}_s", [128, 4], F32) for m in "BMC"}; b_recs = {m: Buf() for m in "BMC"}
    cst = sb("cst_s", [128, 4], F32); b_cst = Buf()
    bis = sb("bis_s", [128, 32], F32); b_bis = Buf()
    rk = sb("rk_s", [128, NBIS + 2], F32); b_rk = Buf()

    NA, NS, NC = 2, 2, 2
    bigA = [ps(f"bigA{k}", [128, 512], F32) for k in range(NA)]; b_bigA = [Buf(psum=True) for _ in range(NA)]
    bigS = [ps(f"bigS{k}", [128, 512], F32) for k in range(NS)]; b_bigS = [Buf(psum=True) for _ in range(NS)]
    bigC = [ps(f"bigC{k}", [128, 512], F32) for k in range(NC)]; b_bigC = [Buf(psum=True) for _ in range(NC)]
    _accS = ps("accS", [128, 512], F32); _baccS = Buf(psum=True)
    _accC = ps("accC", [128, 512], F32); _baccC = Buf(psum=True)
    acc = {"B": _accS, "M": _accS, "C": _accC}; b_acc = {"B": _baccS, "M": _baccS, "C": _baccC}
    rr = {"A": 0, "S": 0, "C": 0, "PTA": 0, "PTC": 0, "rtmp": 0, "cvt": 0, "stage": 0}

    def nxt(key, n):
        k = rr[key]
        rr[key] = (k + 1) % n
        return k

    def bankA():
        k = nxt("A", NA)
        return bigA[k], b_bigA[k]

    def bankS():
        k = nxt("S", NS)
        return bigS[k], b_bigS[k]

    def bankC():
        k = nxt("C", NC)
        return bigC[k], b_bigC[k]

    d_stage = [S.dma_sem("st0"), S.dma_sem("st1")]
    d_x = [S.dma_sem("x0"), S.dma_sem("x1")]
    d_o = [S.dma_sem("o0"), S.dma_sem("o1")]
    d_misc = {}

    def dsem(name):
        if name not in d_misc:
            d_misc[name] = S.dma_sem(name)
        return d_misc[name]

    V, A, G, T = nc.vector, nc.scalar, nc.gpsimd, nc.tensor
    ENG = {"dve": V, "act": A, "pool": G}

    cur_q = [None]

    def free_elems(ap):
        n = 1
        for d in tuple(ap.shape)[1:]:
            n *= int(d)
        return n

    def issue(eng, fn, reads, writes, dur, ds=None, holder=None):
        if cur_q[0] is not None:
            cur_q[0].append((eng, fn, list(reads), list(writes), dur, ds, holder))
            return None
        if ds is not None:
            tok = S.dma(eng, ds, fn, reads=reads, writes=writes)
            if holder is not None:
                holder[0][holder[1]] = tok
            return tok
        return S.op(eng, fn, reads=reads, writes=writes)

    def op(eng, name, reads, writes, *args, **kw):
        f = getattr(ENG[eng], name)
        o = kw.get("out", args[0] if args else None)
        n = free_elems(o) if o is not None else 1
        if eng == "dve":
            dur = 0.12 + n / 960.0 + (0.08 if "accum_out" in kw else 0.0)
        elif eng == "act":
            dur = 0.22 + n / 1200.0 + (0.1 if "accum_out" in kw else 0.0)
        else:
            dur = 0.3 + n / 480.0
        issue(eng, lambda: f(*args, **kw), reads, writes, dur)

    def mm(out, lhsT, rhs, start, reads, writes):
        dur = 0.1 + free_elems(rhs) / 1500.0
        issue("pe", lambda: T.matmul(out, lhsT, rhs, start=start, stop=True, skip_group_check=True), reads, writes, dur)

    def transpose(out, in_, reads, writes):
        idn = ident4[:, 0:128]
        issue("pe", lambda: T.transpose(out, in_, idn), reads + [b_id], writes, 0.2)

    def cast_copy(k, out, in_, reads, writes, scale=None):
        e = k % 3
        if scale is not None:
            if e == 1:
                op("act", "mul", reads, writes, out=out, in_=in_, mul=scale)
            else:
                op("dve" if e == 0 else "pool", "tensor_scalar", reads, writes, out=out, in0=in_, scalar1=scale, scalar2=None, op0=ALU.mult)
            return
        if e == 0:
            op("dve", "tensor_copy", reads, writes, out=out, in_=in_)
        elif e == 1:
            op("act", "copy", reads, writes, out=out, in_=in_)
        else:
            op("pool", "tensor_copy", reads, writes, out=out, in_=in_)

    def dma(ds, out, in_, reads, writes, holder=None):
        return issue("sp", lambda: nc.sync.dma_start(out=out, in_=in_), reads, writes, 3.0, ds=ds, holder=holder)

    def load_direct(name, dst, bdst, src):
        dma(dsem(name), dst, src, [], [bdst])

    def load_cvt(dst, bdst, src, L, post=None, scale=None):
        c0 = 0
        while c0 < L:
            n = min(2048, L - c0)
            h = nxt("stage", 2)
            stg = score[:, h * 2048:h * 2048 + n]
            dma(d_stage[h], stg, src[:, c0:c0 + n], [], [b_sc[h]])
            if post is None:
                cast_copy(nxt("cvt", 3), dst[:, c0:c0 + n], stg, [b_sc[h]], [bdst], scale=scale)
            else:
                post(dst[:, c0:c0 + n], stg, c0, n, h)
            c0 += n

    load_cvt(ident4, b_id, ident4_d, 512)
    load_cvt(Emat, b_E, E_d, 256)
    load_cvt(sel, b_sel, sel_d, 7 * 128)
    load_direct("pow2", pow2[:], b_pow2, pow2_d)
    op("dve", "memset", [], [b_cst], cst[:, 0:1], -0.5)
    load_direct("tmpb", tmpb[:], b_tmpb, cC_d)
    for d, (dstt, bd) in enumerate(((biasC0, b_biasC0), (biasC1, b_biasC1))):
        def post(dst, stg, c0, n, h, bd=bd):
            op("dve", "tensor_tensor", [b_sc[h], b_tmpb], [bd], out=dst, in0=stg, in1=tmpb[:, c0:c0 + n], op=ALU.subtract)
        load_cvt(dstt, bd, biasC_d[d], 512, post=post)
    load_cvt(maskB4, b_maskB4, maskB_d[1], 512)
    for m in "BM":
        op("pool", "memset", [], [b_qblk[m]], qblk[m][:], 0.0)
    for k in range(2):
        op("pool", "memset", [], [b_qblkC[k]], qblkC[k][:], 0.0)
    op("pool", "memset", [], [b_iqblk], iqblk[:], 0.0)
    op("pool", "memset", [], b_vC, vC[:, :, :, 64:65], 1.0)
    op("pool", "memset", [], b_vB, vB[:, :, :, 64:65], 1.0)
    op("pool", "memset", [], [b_vM], vM[:, :, :, 64:65], 1.0)

    def layer_norm(X, bX, width, gt, bg_, bt, bb_, stt, bst, out=None, bout=None):
        nch = (width + 511) // 512
        cw = width // nch
        for c in range(nch):
            op("dve", "bn_stats", [bX], [bst], out=stt[:, 8 + 6 * c:14 + 6 * c], in_=X[:, c * cw:(c + 1) * cw])
        op("dve", "bn_aggr", [bst], [bst], out=stt[:, 0:2], in_=stt[:, 8:8 + 6 * nch])
        op("dve", "tensor_scalar", [bst], [bst], out=stt[:, 2:3], in0=stt[:, 1:2], scalar1=LN_EPS, scalar2=None, op0=ALU.add)
        op("pool", "tensor_tensor", [bst, b_cst], [bst], out=stt[:, 3:4], in0=stt[:, 2:3], in1=cst[:, 0:1], op=ALU.pow)
        op("dve", "tensor_scalar", [bst], [bst], out=stt[:, 4:5], in0=stt[:, 0:1], scalar1=stt[:, 3:4], scalar2=-1.0, op0=ALU.mult, op1=ALU.mult)
        op("act", "activation", [bX, bst], [bX], out=X, in_=X, func=AF.Identity, bias=stt[:, 4:5], scale=stt[:, 3:4])
        op("dve", "tensor_tensor", [bX, bg_], [bX], out=X, in0=X, in1=gt, op=ALU.mult)
        if out is None:
            op("pool", "tensor_tensor", [bX, bb_], [bX], out=X, in0=X, in1=bt, op=ALU.add)
        else:
            op("pool", "tensor_tensor", [bX, bb_], [bout], out=out, in0=X, in1=bt, op=ALU.add)

    def attention(mixer, q, bq, keys, mixed_t, bmixed, mixed_off, gate_ap, bgate, bank_fn, PTs, bPTs, ptkey):
        oacc = acc[mixer]
        bacc = b_acc[mixer]
        nk = len(keys)
        pend = []
        npt = len(PTs)

        def stage1(jj):
            (k0, k1, rk_, vfn, rv_, extra) = keys[jj]
            ST, bST = bank_fn()
            mm(ST[:, 0:256], k0, q[:, 0, :], True, rk_ + [bq], [bST])
            mm(ST[:, 256:512], k1, q[:, 1, :], False, rk_ + [bq], [bST])
            for (l_, r_, rd_) in extra:
                mm(ST[:, :], l_, r_, False, rd_, [bST])
            pk = nxt(ptkey, npt)
            op("act", "activation", [bST], [bPTs[pk]], out=PTs[pk][:], in_=ST[:, :], func=AF.Exp)
            pend.append((jj, pk))

        def stage2():
            jj, pk = pend.pop(0)
            (k0, k1, rk_, vfn, rv_, extra) = keys[jj]
            for h in range(4):
                mm(oacc[:, h * 65:(h + 1) * 65], PTs[pk][:, h * 128:(h + 1) * 128], vfn(h), (jj == 0 and h == 0), [bPTs[pk]] + rv_, [bacc])

        look = min(2, npt)
        for jj in range(nk):
            stage1(jj)
            if len(pend) >= look:
                stage2()
            yield
        while pend:
            stage2()
        ov = oacc[:, 0:260].rearrange("p (h d) -> p h d", d=65)
        rec = recs[mixer]
        op("dve", "reciprocal", [bacc], [b_recs[mixer]], out=rec[:, 0:4], in_=ov[:, :, 64])
        for h in range(4):
            op("dve", "scalar_tensor_tensor", [bacc, b_recs[mixer], bgate], [bmixed],
               out=mixed_t[:, mixed_off + h * 64:mixed_off + (h + 1) * 64], in0=oacc[:, h * 65:h * 65 + 64],
               scalar=rec[:, h:h + 1], in1=gate_ap[:, h * 64:(h + 1) * 64], op0=ALU.mult, op1=ALU.mult)
        yield

    last_store = [None, None]

    def prepass():
        load_direct("lng", lng[:], b_lng, lnin_d[0])
        load_direct("lnb", lnb[:], b_lnb, lnin_d[1])
        for i in range(nblk):
            s_ = i % 2
            X = xin[s_]
            dma(d_x[s_], X[:], x_d[i * 128:(i + 1) * 128, :], [], [b_x[s_]])
            layer_norm(X[:], b_x[s_], 1024, lng[:], b_lng, lnb[:], b_lnb, stB, b_stB)
            dma(d_o[s_], out_d[i * 128:(i + 1) * 128, :], X[:], [b_x[s_]], [b_o[i]], holder=(last_store, s_))
            yield

    def setup_A(l):
        load_cvt(wtm, b_wtm, wtm_d[l], 8 * NTM)
        load_cvt(wfm, b_wfm, wfm_d[l], 8 * NFM)
        load_cvt(wout, b_wout, wout_d[l], 8 * 1024, scale=0.5)
        load_cvt(brow, b_brow, brow_d[l], 512)
        load_cvt(wsT, b_wsT, wsT_d[l], 512)
        op("pool", "memset", [], [b_wsT], wsT[64:128, :].rearrange("p (g i) -> p g i", g=4)[:, :, 0:64], 0.0)
        load_cvt(absT, b_absT, absT_d[l], 128)
        load_direct("bfm", bfm[:], b_bfm, bfm_d[l])
        op("dve", "tensor_scalar", [b_bfm], [b_bfm8], out=bfm8[:], in0=bfm[:], scalar1=0.125, scalar2=None, op0=ALU.mult)
        load_direct("alng", alng[:], b_alng, alng_d[l, 0])
        load_direct("alnb", alnb[:], b_alnb, alng_d[l, 1])
        load_direct("tmpb", tmpb[:], b_tmpb, cB_d[l])
        for m_, (dstt, bd) in enumerate(((biasB0, b_biasB0), (biasB1, b_biasB1))):
            def post(dst, stg, c0, n, h, bd=bd):
                op("dve", "tensor_tensor", [b_sc[h], b_tmpb], [bd], out=dst, in0=stg, in1=tmpb[:, c0:c0 + n], op=ALU.subtract)
            load_cvt(dstt, bd, biasB_d[l, m_], 512, post=post)
        load_direct("tmpb", tmpb[:], b_tmpb, maskB_d[0])
        op("dve", "tensor_tensor", [b_biasB0, b_tmpb], [b_biasB0], out=biasB0[:], in0=biasB0[:], in1=tmpb[:], op=ALU.add)

    def setup_B(l):
        load_direct("lng", lng[:], b_lng, lng_d[l, 0])
        load_direct("lnb", lnb[:], b_lnb, lng_d[l, 1])
        for mt in range(2):
            X = xin[mt]
            dma(d_x[mt], X[:], mem_d[mt * 128:(mt + 1) * 128, :], [], [b_x[mt]])
            op("act", "copy", [b_x[mt]], [b_mixed[0]], out=mixed[0][:], in_=X[:])
            tb, btb = bankA()
            tv = tb[:, :].bitcast(BF16)
            for kt in range(8):
                transpose(tv[:, kt * 128:(kt + 1) * 128], mixed[0][:, kt * 128:(kt + 1) * 128], [b_mixed[0]], [btb])
            op("dve", "tensor_copy", [btb], b_rt, out=memT[:, :, mt * 128:(mt + 1) * 128], in_=tv.rearrange("p (a b) -> p a b", b=128))
        load_cvt(wmb, b_wmb, wmk_d[l], 8 * 256)
        for t in range(2):
            bk, bbk = bankA()
            for kt in range(8):
                mm(bk[:, 0:256], wmb[:, kt * 256 + t * 128:kt * 256 + (t + 1) * 128], memT[:, kt, :], kt == 0, [b_wmb] + b_rt, [bbk])
            op("dve", "tensor_copy", [bbk], [b_kmT], out=kmT[:, t, :], in_=bk[:, 0:256])
        load_cvt(wmb, b_wmb, wmv_d[l], 8 * 256)
        for mt in range(2):
            bk, bbk = bankA()
            for kt in range(8):
                mm(bk[:, 0:256], memT[:, kt, mt * 128:(mt + 1) * 128], wmb[:, kt * 256:(kt + 1) * 256], kt == 0, [b_wmb] + b_rt, [bbk])
            op("dve", "tensor_copy", [bbk], [b_vM], out=vM[:, mt, :, 0:64], in_=bk[:, 0:256].rearrange("p (h d) -> p h d", d=64))

    lo, hi, full = slice(0, 64), slice(64, 128), slice(0, 128)

    def ev(out, in0, prt, ct, scale, reads, writes):
        if ct < 10:
            if scale is None:
                op("act", "activation", reads, writes, out=out, in_=in0, func=AF.Identity, bias=bfm[prt, ct:ct + 1], scale=1.0)
            else:
                op("act", "activation", reads + [b_bfm8], writes, out=out, in_=in0, func=AF.Identity, bias=bfm8[prt, ct:ct + 1], scale=scale)
            return
        if scale is None:
            op("dve", "tensor_scalar", reads, writes, out=out, in0=in0, scalar1=bfm[prt, ct:ct + 1], scalar2=None, op0=ALU.add)
        else:
            op("dve", "tensor_scalar", reads, writes, out=out, in0=in0, scalar1=bfm[prt, ct:ct + 1], scalar2=scale, op0=ALU.add, op1=ALU.mult)

    def fm_group(i, cts, bank_fn):
        p_ = i % 2
        sl = i % 5
        bk, bbk = bank_fn()
        first = True
        for ci, ct in enumerate(cts):
            for kt in range(8):
                mm(bk[:, ci * 128:(ci + 1) * 128], wfm[:, kt * NFM + ct * 128:kt * NFM + (ct + 1) * 128], xnT[:, kt, :], first, [b_wfm, b_xnT], [bbk])
                first = False
        for ci, ct in enumerate(cts):
            pst = bk[:, ci * 128:(ci + 1) * 128]
            rd = [bbk, b_bfm]
            if ct in (0, 1, 8, 9):
                m = "B" if ct < 2 else "M"
                t = ct % 2
                ev(qblk[m][lo, t, 0:128], pst[lo, :], lo, ct, 0.125, rd, [b_qblk[m]])
                ev(qblk[m][hi, t, 128:256], pst[hi, :], hi, ct, 0.125, rd, [b_qblk[m]])
            elif ct in (4, 5):
                t = ct % 2
                ev(qblkC[p_][lo, t, 0:128], pst[lo, :], lo, ct, 0.125, rd, [b_qblkC[p_]])
                ev(qblkC[p_][hi, t, 128:256], pst[hi, :], hi, ct, 0.125, rd, [b_qblkC[p_]])
            elif ct in (2, 3):
                ev(kBT[:, ct - 2, sl * 128:(sl + 1) * 128], pst[:, :], full, ct, None, rd, [b_kB[sl]])
            elif ct in (6, 7):
                ev(kCT[:, ct - 6, i * 128:(i + 1) * 128], pst[:, :], full, ct, None, rd, [b_kC[i]])
            elif ct in (10, 11, 12, 13):
                t = ct - 10
                ev(iqblk[lo, 2 * t, :], pst[lo, :], lo, ct, None, rd, [b_iqblk])
                ev(iqblk[hi, 2 * t + 1, :], pst[hi, :], hi, ct, None, rd, [b_iqblk])
            else:
                ev(ikT[:, i * 128:(i + 1) * 128], pst[:, :], full, ct, None, rd, [b_ik[i]])

    def tm_tile(r, c0, n, bank_fn):
        bk, bbk = bank_fn()
        for kt in range(8):
            mm(bk[:, 0:n], xnT[:, kt, :], wtm[:, kt * NTM + c0:kt * NTM + c0 + n], kt == 0, [b_xnT, b_wtm], [bbk])
        mm(bk[:, 0:n], sel[:, r * 128:(r + 1) * 128], brow[:, 0:n], False, [b_sel, b_brow], [bbk])
        return bk, bbk

    def phaseMain(i):
        s_ = i % 2
        p_ = i % 2
        X = xin[s_]
        N_i = (i + 1) * 128
        mx = mixed[p_]
        bmx = b_mixed[p_]
        dma(d_x[s_], X[:], out_d[i * 128:(i + 1) * 128, :], [b_o[i]], [b_x[s_]])
        op("act", "copy", [b_x[s_]], [bmx], out=mx[:], in_=X[:])
        tb, btb = bankA()
        tv = tb[:, :].bitcast(BF16)
        for kt in range(8):
            transpose(tv[:, kt * 128:(kt + 1) * 128], mx[:, kt * 128:(kt + 1) * 128], [bmx], [btb])
        op("dve", "tensor_copy", [btb], [b_xnT], out=xnT[:].rearrange("p a b -> p (a b)"), in_=tv)
        yield "XNT"
        fm_group(i, [10, 11, 12, 13], bankA)
        yield 0.2
        fm_group(i, [14], bankA)
        bk, bbk = tm_tile(4, 2048, 8, bankA)
        op("dve", "tensor_copy", [bbk], [b_iw], out=iw[:], in_=bk[:, 0:8])
        yield 0.5
        ntile = (N_i + 511) // 512
        for c in range(ntile):
            c0 = c * 512
            n = min(512, N_i - c0)
            half = [b_sc[0]] if c0 + n <= 2048 else [b_sc[1]]
            rik = [b_ik[jj] for jj in range(c0 // 128, (c0 + n) // 128)]
            for h in range(8):
                bk, bbk = bankA()
                mm(bk[:, 0:n], iqblk[:, h, :], ikT[:, c0:c0 + n], True, [b_iqblk] + rik, [bbk])
                if h == 0:
                    op("dve", "tensor_scalar", [bbk, b_iw], half, out=score[:, c0:c0 + n], in0=bk[:, 0:n], scalar1=0.0, scalar2=iw[:, 0:1], op0=ALU.max, op1=ALU.mult)
                else:
                    r_ = nxt("rtmp", 2)
                    op("act", "activation", [bbk], [b_rtmp[r_]], out=rtmp[r_][:, 0:n], in_=bk[:, 0:n], func=AF.Relu)
                    op("dve", "scalar_tensor_tensor", [b_rtmp[r_], b_iw] + half, half, out=score[:, c0:c0 + n], in0=rtmp[r_][:, 0:n], scalar=iw[:, h:h + 1],
                       in1=score[:, c0:c0 + n], op0=ALU.mult, op1=ALU.add)
                yield W_IDX * n / 512.0
        SC = b_sc if N_i > 2048 else [b_sc[0]]
        lasthalf = [b_sc[1]] if N_i > 2048 else [b_sc[0]]
        op("pool", "memset", [], lasthalf, score[0:64, N_i - 64:N_i], -1e30)
        if i >= 2:
            op("dve", "tensor_reduce", SC, [b_bis], out=bis[:, 0:1], in_=score[:, 0:N_i], axis=AX.X, op=ALU.max)
            op("dve", "tensor_reduce", SC, [b_bis], out=bis[:, 1:2], in_=score[:, 0:N_i - 64], axis=AX.X, op=ALU.min)
            op("dve", "tensor_tensor", [b_bis], [b_bis], out=bis[:, 2:3], in0=bis[:, 0:1], in1=bis[:, 1:2], op=ALU.subtract)
            op("dve", "tensor_scalar", [b_bis, b_pow2], [b_rk], out=rk[:, :], in0=pow2[:, :], scalar1=bis[:, 2:3], scalar2=None, op0=ALU.mult)
            op("dve", "tensor_tensor", [b_bis, b_rk], [b_bis], out=bis[:, 8:9], in0=bis[:, 1:2], in1=rk[:, 1:2], op=ALU.add)
            yield 1.0 + N_i / 500.0
            nd = max(64, int(round(N_i * DVE_FRAC / 64)) * 64, N_i - 2048)
            na = N_i - nd
            SCd = [b_sc[0]] if nd <= 2048 else b_sc
            SCa = [b_sc[1]] if nd >= 2048 else (b_sc if N_i > 2048 else [b_sc[0]])
            for k in range(1, NBIS + 1):
                mid = bis[:, 8 + (k - 1) % 2:9 + (k - 1) % 2]
                midn = bis[:, 8 + k % 2:9 + k % 2]
                op("dve", "tensor_scalar", SCd + [b_bis], [b_rt[0], b_cntD], out=junk[:, 0:nd], in0=score[:, 0:nd], scalar1=mid, scalar2=None,
                   op0=ALU.is_ge, op1=ALU.add, accum_out=cntD[:, 0:1])
                op("act", "activation", SCa + [b_bis], [b_rt[1], b_cntA], out=junk[:, 2048:2048 + na], in_=score[:, nd:N_i], func=AF.Sign, bias=mid, scale=-1.0,
                   accum_out=cntA[:, 0:1])
                op(SMALL_ENG, "tensor_scalar", [b_cntD, b_cntA], [b_bis], out=bis[:, 4:5], in0=cntD[:, 0:1], scalar1=2.0, scalar2=cntA[:, 0:1], op0=ALU.mult, op1=ALU.subtract)
                op(SMALL_ENG, "tensor_scalar", [b_bis, b_rk], [b_bis], out=bis[:, 5:6], in0=bis[:, 4:5], scalar1=float(512 - na), scalar2=rk[:, k:k + 1], op0=ALU.is_ge, op1=ALU.mult)
                kk = k + 1 if k < NBIS else k
                op(SMALL_ENG, "tensor_scalar", [b_bis, b_rk], [b_bis], out=midn, in0=bis[:, 5:6], scalar1=rk[:, kk:kk + 1], scalar2=mid, op0=ALU.subtract, op1=ALU.add)
                yield 1.2 + N_i / 1100.0
            thr = bis[:, 8 + NBIS % 2:9 + NBIS % 2]
        else:
            op("dve", "memset", [], [b_bis], bis[:, 12:13], -1e29)
            thr = bis[:, 12:13]
        yield "B_DONE"
        c0 = 0
        while c0 < N_i:
            n = min(2048, N_i - c0)
            op("dve", "tensor_scalar", [b_sc[c0 // 2048], b_bis], [b_mb], out=mb[:, c0:c0 + n], in0=score[:, c0:c0 + n], scalar1=thr, scalar2=NEG, op0=ALU.is_lt, op1=ALU.mult)
            c0 += n
        yield 0.0

    def main_weight(i):
        N_i = (i + 1) * 128
        w = 1.2 + W_IDX * 8 * N_i / 512.0
        if i >= 2:
            w += 1.0 + N_i / 500.0 + NBIS * (1.2 + N_i / 1100.0)
        return w

    def phaseSide(i):
        s_ = i % 2
        p_ = i % 2
        sl = i % 5
        mx = mixed[p_]
        bmx = b_mixed[p_]
        bk, bbk = tm_tile(0, 0, 512, bankS)
        op("act", "copy", [bbk], [b_xg], out=xg[:], in_=bk[:, :])
        op("pool", "tensor_tensor", [b_xg], [b_gs], out=gsc0[:], in0=xg[:], in1=xg[:], op=ALU.mult)
        op("dve", "scalar_tensor_tensor", [b_gs, b_xg], [b_gs], out=gsc0[:], in0=gsc0[:], scalar=0.044715, in1=xg[:], op0=ALU.mult, op1=ALU.mult)
        op("pool", "tensor_tensor", [b_gs, b_xg], [b_gs], out=gsc0[:], in0=gsc0[:], in1=xg[:], op=ALU.add)
        op("act", "activation", [b_gs], [b_gs], out=gsc0[:], in_=gsc0[:], func=AF.Tanh, scale=0.7978845608028654)
        op("dve", "scalar_tensor_tensor", [b_gs, b_xg], [b_xg], out=xg[:], in0=gsc0[:], scalar=1.0, in1=xg[:], op0=ALU.add, op1=ALU.mult)
        yield
        bk, bbk = tm_tile(1, 512, 512, bankS)
        op("act", "activation", [bbk], [b_gs], out=gsc0[:], in_=bk[:, :], func=AF.Tanh, scale=0.5)
        op("dve", "scalar_tensor_tensor", [b_gs, bbk], [b_gates], out=gates[:, 0:512], in0=gsc0[:], scalar=1.0, in1=bk[:, :], op0=ALU.add, op1=ALU.mult)
        yield
        bk, bbk = tm_tile(2, 1024, 512, bankS)
        op("act", "activation", [bbk], [b_gs], out=gsc0[:], in_=bk[:, :], func=AF.Tanh, scale=0.5)
        op("dve", "scalar_tensor_tensor", [b_gs, bbk], [b_gatesC[p_]], out=gatesC[p_][:, :], in0=gsc0[:, 0:256], scalar=1.0, in1=bk[:, 0:256], op0=ALU.add, op1=ALU.mult)
        op("dve", "scalar_tensor_tensor", [b_gs, bbk], [b_gates], out=gates[:, 512:768], in0=gsc0[:, 256:512], scalar=1.0, in1=bk[:, 256:512], op0=ALU.add, op1=ALU.mult)
        yield
        bk, bbk = tm_tile(3, 1536, 512, bankS)
        op("act", "copy", [bbk], [b_vB[sl]], out=vB[:, sl, :, 0:64], in_=bk[:, 0:256].rearrange("p (h d) -> p h d", d=64))
        op("act", "copy", [bbk], [b_vC[i]], out=vC[:, i, :, 0:64], in_=bk[:, 256:512].rearrange("p (h d) -> p h d", d=64))
        yield
        for cts in ([0, 1, 2, 3], [4, 5, 6, 7], [8, 9]):
            fm_group(i, cts, bankS)
            yield
        op("act", "mul", [b_xg], [b_xg], out=xg[:, 256:512], in_=xg[:, 256:512], mul=0.5)
        layer_norm(xg[:, 256:512], b_xg, 256, alng[:], b_alng, alnb[:], b_alnb, stA, b_stA, out=vln[:], bout=b_vln)
        bk, bbk = bankS()
        for g in range(4):
            mm(bk[:, g * 64:(g + 1) * 64], wsT[:, g * 128:(g + 1) * 128], vln[:, g * 64:(g + 1) * 64], g == 0, [b_wsT, b_vln], [bbk])
        mm(bk[:, 0:256], absT[:, :], Emat[:, :], False, [b_absT, b_E], [bbk])
        op("dve", "scalar_tensor_tensor", [bbk, b_xg], [b_xg], out=xg[:, 256:512], in0=bk[:, 0:256], scalar=0.5, in1=xg[:, 0:256], op0=ALU.mult, op1=ALU.mult)
        op("pool", "tensor_tensor", [b_xg, b_gates], [bmx], out=mx[:, 0:256], in0=xg[:, 256:512], in1=gates[:, 0:256], op=ALU.mult)
        yield
        keys = []
        for j in range(max(0, i - 4), i + 1):
            sj = j % 5
            m_ = i - j
            extra = []
            if m_ == 0:
                extra.append((ident4[:, 0:128], biasB0[:, :], [b_id, b_biasB0]))
            elif m_ == 1:
                extra.append((ident4[:, 0:128], biasB1[:, :], [b_id, b_biasB1]))
            elif m_ == 4:
                extra.append((ident4[:, 0:128], maskB4[:, :], [b_id, b_maskB4]))
            keys.append((kBT[:, 0, sj * 128:(sj + 1) * 128], kBT[:, 1, sj * 128:(sj + 1) * 128], [b_kB[sj]],
                         (lambda h, sj=sj: vB[:, sj, h, :]), [b_vB[sj]], extra))
        yield from attention("B", qblk["B"], b_qblk["B"], keys, mx, bmx, 256, gates[:, 256:512], b_gates, bankS, PTA, b_PTA, "PTA")
        keys = []
        for mt in range(2):
            keys.append((kmT[:, 0, mt * 128:(mt + 1) * 128], kmT[:, 1, mt * 128:(mt + 1) * 128], [b_kmT],
                         (lambda h, mt=mt: vM[:, mt, h, :]), [b_vM], []))
        yield from attention("M", qblk["M"], b_qblk["M"], keys, mx, bmx, 768, gates[:, 512:768], b_gates, bankS, PTA, b_PTA, "PTA")

    def phaseB(i, l):
        s_ = i % 2
        p_ = i % 2
        X = xin[s_]
        mx = mixed[p_]
        bmx = b_mixed[p_]
        keys = []
        for j in range(0, i + 1):
            extra = [(mb[:, j * 128:(j + 1) * 128], ident4[:, :], [b_mb, b_id])]
            d_ = i - j
            if d_ == 0:
                extra.append((ident4[:, 0:128], biasC0[:, :], [b_id, b_biasC0]))
            elif d_ == 1:
                extra.append((ident4[:, 0:128], biasC1[:, :], [b_id, b_biasC1]))
            keys.append((kCT[:, 0, j * 128:(j + 1) * 128], kCT[:, 1, j * 128:(j + 1) * 128], [b_kC[j]],
                         (lambda h, j=j: vC[:, j, h, :]), [b_vC[j]], extra))
        yield from attention("C", qblkC[p_], b_qblkC[p_], keys, mx, bmx, 512, gatesC[p_][:, :], b_gatesC[p_], bankC, PTC, b_PTC, "PTC")
        tb, btb = bankC()
        tv = tb[:, :].bitcast(BF16)
        for kt in range(8):
            transpose(tv[:, kt * 128:(kt + 1) * 128], mx[:, kt * 128:(kt + 1) * 128], [bmx], [btb])
        op("dve", "tensor_copy", [btb], [b_mixedT], out=mixedT[:].rearrange("p a b -> p (a b)"), in_=tv)
        yield
        for c in range(2):
            bk, bbk = bankC()
            for kt in range(8):
                mm(bk[:, :], mixedT[:, kt, :], wout[:, kt * 1024 + c * 512:kt * 1024 + (c + 1) * 512], kt == 0, [b_mixedT, b_wout], [bbk])
            mm(bk[:, :], sel[:, (5 + c) * 128:(6 + c) * 128], brow[:, :], False, [b_sel, b_brow], [bbk])
            op("dve", "scalar_tensor_tensor", [b_x[s_], bbk], [b_x[s_]], out=X[:, c * 512:(c + 1) * 512], in0=X[:, c * 512:(c + 1) * 512], scalar=ALPHA, in1=bk[:, :],
               op0=ALU.mult, op1=ALU.add)
            yield
        layer_norm(X[:], b_x[s_], 1024, lng[:], b_lng, lnb[:], b_lnb, stB, b_stB)
        dma(d_o[s_], out_d[i * 128:(i + 1) * 128, :], X[:], [b_x[s_]], [b_o[i]], holder=(last_store, s_))
        yield

    def drain(g):
        for _ in g:
            pass

    class Stream:
        def __init__(self, gen, prio):
            self.gen = gen
            self.q = []
            self.done = False
            self.prio = prio
            self.active = True

    eng_free = {e: 0.0 for e in ENGS}
    tok_done = {}

    def fill(st):
        while not st.q and not st.done:
            cur_q[0] = st.q
            try:
                y = next(st.gen)
                if isinstance(y, str):
                    st.q.append(y)
            except StopIteration:
                st.done = True
            finally:
                cur_q[0] = None

    def ready_time(eng, reads, writes):
        t = 0.0
        for bf in reads:
            if bf.writer is not None:
                t = max(t, tok_done.get((id(bf.writer[0]), bf.writer[1]), 0.0) + (0.0 if bf.writer[2] == eng else 0.12))
            if bf.psum:
                for tk in bf.readers.values():
                    if tk[2] != eng:
                        t = max(t, tok_done.get((id(tk[0]), tk[1]), 0.0) + 0.12)
        for bf in writes:
            if bf.writer is not None:
                t = max(t, tok_done.get((id(bf.writer[0]), bf.writer[1]), 0.0) + (0.0 if bf.writer[2] == eng else 0.12))
            for tk in bf.readers.values():
                if tk[2] != eng:
                    t = max(t, tok_done.get((id(tk[0]), tk[1]), 0.0) + 0.12)
        return t

    def commit(item, start):
        eng, fn, reads, writes, dur, ds, holder = item
        if ds is not None:
            tok = S.dma(eng, ds, fn, reads=reads, writes=writes)
            if holder is not None:
                holder[0][holder[1]] = tok
            eng_free[eng] = start + 0.1
        else:
            tok = S.op(eng, fn, reads=reads, writes=writes)
            eng_free[eng] = start + dur
        tok_done[(id(tok[0]), tok[1])] = start + dur

    def run_round(gmain, gside, gB):
        main = Stream(gmain, 0.0)
        side = Stream(gside, 0.0) if gside is not None else None
        stB = Stream(gB, 0.0) if gB is not None else None
        if side is not None:
            side.active = False
        streams = [x for x in (main, stB, side) if x is not None]
        while True:
            best = None
            alive = False
            for st in streams:
                fill(st)
                if not st.q:
                    continue
                alive = True
                if not st.active:
                    continue
                item = st.q[0]
                if isinstance(item, str):
                    if item == "XNT":
                        st.q.pop(0)
                        if side is not None:
                            side.active = True
                        best = "again"
                        break
                    if item == "B_DONE":
                        if stB is None or (stB.done and not stB.q):
                            st.q.pop(0)
                            best = "again"
                            break
                        continue
                    st.q.pop(0)
                    best = "again"
                    break
                eng = item[0]
                start = max(eng_free[eng], ready_time(eng, item[2], item[3]))
                key = start - st.prio
                if best is None or key < best[0]:
                    best = (key, start, st)
            if best == "again":
                continue
            if best is None:
                if not alive:
                    break
                raise RuntimeError("scheduler stuck")
            _, start, st = best
            commit(st.q.pop(0), start)

    def run_streams(gens):
        sts = [Stream(g, 0.0) for g in gens]
        while True:
            best = None
            for st in sts:
                fill(st)
                if not st.q:
                    continue
                item = st.q[0]
                if isinstance(item, str):
                    st.q.pop(0)
                    best = "again"
                    break
                start = max(eng_free[item[0]], ready_time(item[0], item[2], item[3]))
                if best is None or start < best[0]:
                    best = (start, st)
            if best == "again":
                continue
            if best is None:
                break
            commit(best[1].q.pop(0), best[0])

    def once(f, *a):
        f(*a)
        yield

    for l in range(n_layers):
        if l == 0:
            run_streams([prepass(), once(setup_A, 0)])
        else:
            setup_A(l)
        setup_B(l)
        if not pipeline:
            for i in range(nblk):
                drain(phaseMain(i))
                drain(phaseSide(i))
                drain(phaseB(i, l))
        else:
            run_round(phaseMain(0), phaseSide(0), None)
            for i in range(nblk):
                if i + 1 < nblk:
                    run_round(phaseMain(i + 1), phaseSide(i + 1), phaseB(i, l))
                else:
                    drain(phaseB(i, l))

    S.wait_tokens("sp", [t for t in last_store if t is not None])
    S.emit()
    es.close()
    return nc


def _t5_bucket_idx(rel):
    nb = 16
    max_exact = 8
    ret = np.where(rel > 0, nb, 0)
    n = np.abs(rel)
    nf = np.maximum(n, 1).astype(np.float32)
    large = max_exact + (np.log(nf / np.float32(max_exact)) / np.float32(np.log(128 / max_exact)) * (nb - max_exact)).astype(np.int32)
    large = np.minimum(large, nb - 1)
    return ret + np.where(n < max_exact, n, large)


def _kt_layout(w):
    C = w.shape[1]
    return np.ascontiguousarray(w.reshape(8, 128, C).transpose(1, 0, 2).reshape(128, 8 * C))


def prepare(inputs):
    f = np.float32
    cum = np.cumsum((0,) + SPLITS)
    rng = {n: (int(cum[k]), int(cum[k + 1])) for k, n in enumerate(NAMES)}
    tm_cols = np.concatenate([np.arange(*rng[n]) for n in TM_ORDER])
    fm_cols = np.concatenate([np.arange(*rng[n]) for n in FM_ORDER])
    w_in = np.asarray(inputs["w_in"], f)
    b_in = np.asarray(inputs["b_in"], f)
    sh = {}
    sh["wtm"] = np.stack([_kt_layout(w_in[l][:, tm_cols]) for l in range(DEPTH)])
    sh["wfm"] = np.stack([_kt_layout(w_in[l][:, fm_cols]) for l in range(DEPTH)])
    sh["wout"] = np.stack([_kt_layout(np.asarray(inputs["w_out"], f)[l]) for l in range(DEPTH)])
    wm = np.asarray(inputs["w_mem_kv"], f)
    sh["wmk"] = np.stack([_kt_layout(wm[l][:, 0:256]) for l in range(DEPTH)])
    sh["wmv"] = np.stack([_kt_layout(wm[l][:, 256:512]) for l in range(DEPTH)])
    brow = np.zeros((DEPTH, 128, 512), f)
    btm = b_in[:, tm_cols]
    for r in range(4):
        brow[:, r, :] = btm[:, r * 512:(r + 1) * 512]
    brow[:, 4, 0:8] = btm[:, 2048:2056]
    b_out = np.asarray(inputs["b_out"], f)
    brow[:, 5, :] = b_out[:, 0:512]
    brow[:, 6, :] = b_out[:, 512:1024]
    sh["brow"] = brow
    sel = np.zeros((128, 7, 128), f)
    for r in range(7):
        sel[r, r, :] = 1.0
    sh["sel"] = sel.reshape(128, 7 * 128)
    sh["bfm"] = np.ascontiguousarray(b_in[:, fm_cols].reshape(DEPTH, 15, 128).transpose(0, 2, 1))
    sh["lnin"] = np.stack([np.broadcast_to(np.asarray(inputs["ln_in_g"], f), (128, 1024)),
                           np.broadcast_to(np.asarray(inputs["ln_in_b"], f), (128, 1024))]).copy()
    sh["lng"] = np.stack([np.stack([np.broadcast_to(np.asarray(inputs["ln_g"], f)[l], (128, 1024)),
                                    np.broadcast_to(np.asarray(inputs["ln_b"], f)[l], (128, 1024))]) for l in range(DEPTH)]).copy()
    sh["alng"] = np.stack([np.stack([np.broadcast_to(np.asarray(inputs["a_ln_g"], f)[l], (128, 256)),
                                     np.broadcast_to(np.asarray(inputs["a_ln_b"], f)[l], (128, 256))]) for l in range(DEPTH)]).copy()
    a_ws = np.asarray(inputs["a_ws"], f)
    sh["wsT"] = np.ascontiguousarray(a_ws.transpose(0, 3, 1, 2).reshape(DEPTH, 128, 512))
    ab = np.zeros((DEPTH, 128, 128), f)
    ab[:, 0:4, :] = np.asarray(inputs["a_bs"], f)
    sh["absT"] = ab
    b_rel = np.asarray(inputs["b_rel"], f)
    a = np.arange(128)[:, None]
    b = np.arange(128)[None, :]
    bB = np.zeros((DEPTH, 2, 128, 4, 128), f)
    for m in range(2):
        idx = np.clip(128 * m + b - a, -128, 128) + 128
        for l in range(DEPTH):
            bB[l, m] = b_rel[l][:, idx].transpose(1, 0, 2)
    sh["biasB"] = bB.reshape(DEPTH, 2, 128, 512)
    sh["cB"] = np.ascontiguousarray(np.broadcast_to(b_rel[:, None, :, 256, None], (DEPTH, 128, 4, 128)).reshape(DEPTH, 128, 512))
    t5 = np.asarray(inputs["t5_table"], f)
    bC = np.zeros((2, 128, 4, 128), f)
    for d in range(2):
        relkq = -128 * d + a - b
        bi = _t5_bucket_idx(relkq)
        bC[d] = t5[bi].transpose(0, 2, 1)
    sh["biasC"] = bC.reshape(2, 128, 512)
    sh["cC"] = np.ascontiguousarray(np.broadcast_to(t5[15][None, :, None], (128, 4, 128)).reshape(128, 512))
    mB = np.zeros((2, 128, 4, 128), f)
    mB[0][64:128, :, 0:64] = NEG
    mB[1][0:64, :, 64:128] = NEG
    sh["maskB"] = mB.reshape(2, 128, 512)
    E = np.zeros((128, 256), f)
    for g in range(4):
        E[g, g * 64:(g + 1) * 64] = 1.0
    sh["Emat"] = E
    sh["ident4"] = np.tile(np.eye(128, dtype=f), (1, 4))
    sh["pow2"] = np.broadcast_to((2.0 ** -np.arange(NBIS + 2)).astype(f)[None, :], (128, NBIS + 2)).copy()
    return sh


_NC_CACHE = {}


def kernel(**inputs):
    sh = prepare(inputs)
    x = np.asarray(inputs["x"], np.float32)
    mem = np.asarray(inputs["mem"], np.float32)
    if "nc" not in _NC_CACHE:
        _NC_CACHE["nc"] = build_program()
    nc = _NC_CACHE["nc"]
    in_maps = []
    for c in range(BATCH):
        m = dict(sh)
        m["x"] = np.ascontiguousarray(x[c])
        m["mem"] = np.ascontiguousarray(mem[c])
        in_maps.append(m)
    res = run_bass_kernel_spmd(nc, in_maps, core_ids=list(range(BATCH)))
    return np.stack([np.asarray(r["out"]) for r in res.results]).astype(np.float32)
```

```python
import numpy as np
import concourse.bass as bass
import concourse.mybir as mybir
from concourse.bass_utils import run_bass_kernel_spmd
from contextlib import ExitStack

F32 = mybir.dt.float32
BF16 = mybir.dt.bfloat16
U8 = mybir.dt.uint8
ALU = mybir.AluOpType
AF = mybir.ActivationFunctionType
AX = mybir.AxisListType

D_MODEL = 1024
BATCH = 8
SEQ = 4096
DEPTH = 2
NBLK = SEQ // 128
GW = 256
SPLITS = (GW, GW, GW, GW, GW, GW, GW, GW, GW, GW, GW, 512, 64, 8, GW, GW)
NAMES = ("a_u", "a_v", "a_g", "bq", "bk", "bv", "bg", "cq", "ck", "cv", "cg", "iq", "ik", "iw", "mq", "mg")
TM_ORDER = ("a_u", "a_v", "a_g", "bg", "cg", "mg", "bv", "cv", "iw")
FM_ORDER = ("bq", "bk", "cq", "ck", "mq", "iq", "ik", "ik")
NTM = 2056
NFM = 1920
ALPHA = (2 * DEPTH) ** 0.25
LN_EPS = 1e-5
NBIS = 12
SCHED_EPS = 0.3
DVE_FRAC = 0.35
W_IDX = 0.12
SMALL_ENG = "pool"
NEG = -30000.0

ENGS = ("pe", "act", "dve", "pool", "sp")
STRICT_WAR = False


class Buf:
    __slots__ = ("name", "writer", "readers", "psum")

    def __init__(self, name="", psum=False):
        self.name = name
        self.writer = None
        self.readers = {}
        self.psum = psum


class DmaSem:
    def __init__(self, sem):
        self.sem = sem
        self.count = 0


class Sched:
    EPOCH = 12000

    def __init__(self, nc, es):
        self.nc = nc
        self.es = es
        self.prog = {e: [] for e in ENGS}
        self.cnt = {e: 0 for e in ENGS}
        self.nsem = 0
        self.cursem = {e: self._newsem(e) for e in ENGS}
        self.waited = {e: {} for e in ENGS}
        self.semobj = {}

    def _newsem(self, name):
        self.nsem += 1
        return self.es.enter_context(self.nc.semaphore(f"s_{name}_{self.nsem}"))

    def dma_sem(self, name):
        return DmaSem(self._newsem("d" + name))

    def _deps(self, eng, reads, writes):
        need = {}

        def add(tok, kind):
            if tok is None:
                return
            sem, val, teng = tok
            if teng == eng:
                if eng == "pe" or (kind == "war" and not STRICT_WAR):
                    return
            k = id(sem)
            self.semobj[k] = sem
            if need.get(k, 0) < val:
                need[k] = val

        for b in reads:
            add(b.writer, "raw")
            if b.psum:
                for t in b.readers.values():
                    if t[2] != eng:
                        add(t, "rar")
        for b in writes:
            add(b.writer, "waw")
            for t in b.readers.values():
                add(t, "war")
        waits = []
        w = self.waited[eng]
        for k, val in need.items():
            if w.get(k, 0) < val:
                w[k] = val
                waits.append((self.semobj[k], val))
        return waits

    def _mark(self, tok, reads, writes):
        k = id(tok[0])
        for b in reads:
            b.readers[k] = tok
        for b in writes:
            b.writer = tok
            b.readers = {}

    def op(self, eng, fn, reads=(), writes=()):
        waits = self._deps(eng, reads, writes)
        if self.cnt[eng] >= self.EPOCH:
            self.cursem[eng] = self._newsem(eng)
            self.cnt[eng] = 0
        self.cnt[eng] += 1
        tok = (self.cursem[eng], self.cnt[eng], eng)
        self.prog[eng].append((waits, fn, (self.cursem[eng], 1)))
        self._mark(tok, reads, writes)
        return tok

    def dma(self, eng, ds, fn, reads=(), writes=()):
        waits = self._deps(eng, reads, writes)
        ds.count += 16
        tok = (ds.sem, ds.count, "dma")
        self.prog[eng].append((waits, fn, (ds.sem, 16)))
        self._mark(tok, reads, writes)
        return tok

    def wait_tokens(self, eng, toks):
        self.prog[eng].append(([(t[0], t[1]) for t in toks], None, None))

    def emit(self):
        nc = self.nc
        engobj = {"pe": nc.tensor, "act": nc.scalar, "dve": nc.vector, "pool": nc.gpsimd, "sp": nc.sync}

        def replay(e):
            eo = engobj[e]
            for waits, fn, inc in self.prog[e]:
                for sem, val in waits:
                    eo.wait_ge(sem, val)
                if fn is not None:
                    fn().then_inc(inc[0], inc[1])

        with nc.Block() as block:
            @block.tensor
            def _(x):
                replay("pe")

            @block.scalar
            def _(x):
                replay("act")

            @block.vector
            def _(x):
                replay("dve")

            @block.gpsimd
            def _(x):
                replay("pool")

            @block.sync
            def _(x):
                replay("sp")


def build_program(n_layers=DEPTH, nblk=NBLK, pipeline=True):
    nc = bass.Bass("TRN2", target_bir_lowering=False)
    es = ExitStack()
    S = Sched(nc, es)

    def din(name, shape):
        return nc.dram_tensor(name, list(shape), F32, kind="ExternalInput").ap()

    x_d = din("x", [SEQ, D_MODEL])
    mem_d = din("mem", [256, D_MODEL])
    lnin_d = din("lnin", [2, 128, 1024])
    wtm_d = din("wtm", [DEPTH, 128, 8 * NTM])
    wfm_d = din("wfm", [DEPTH, 128, 8 * NFM])
    wout_d = din("wout", [DEPTH, 128, 8 * 1024])
    wmk_d = din("wmk", [DEPTH, 128, 8 * 256])
    wmv_d = din("wmv", [DEPTH, 128, 8 * 256])
    brow_d = din("brow", [DEPTH, 128, 512])
    sel_d = din("sel", [128, 7 * 128])
    bfm_d = din("bfm", [DEPTH, 128, 15])
    lng_d = din("lng", [DEPTH, 2, 128, 1024])
    alng_d = din("alng", [DEPTH, 2, 128, 256])
    wsT_d = din("wsT", [DEPTH, 128, 512])
    absT_d = din("absT", [DEPTH, 128, 128])
    biasB_d = din("biasB", [DEPTH, 2, 128, 512])
    cB_d = din("cB", [DEPTH, 128, 512])
    biasC_d = din("biasC", [2, 128, 512])
    cC_d = din("cC", [128, 512])
    maskB_d = din("maskB", [2, 128, 512])
    E_d = din("Emat", [128, 256])
    ident4_d = din("ident4", [128, 512])
    pow2_d = din("pow2", [128, NBIS + 2])
    out_d = nc.dram_tensor("out", [SEQ, D_MODEL], F32, kind="ExternalOutput").ap()
    b_o = [Buf() for _ in range(NBLK)]

    def sb(name, shape, dt):
        return es.enter_context(nc.sbuf_tensor(name, list(shape), dt))

    def ps(name, shape, dt):
        return es.enter_context(nc.psum_tensor(name, list(shape), dt))

    wtm = sb("wtm_s", [128, 8 * NTM], BF16); b_wtm = Buf()
    wfm = sb("wfm_s", [128, 8 * NFM], BF16); b_wfm = Buf()
    wout = sb("wout_s", [128, 8 * 1024], BF16); b_wout = Buf()
    brow = sb("brow_s", [128, 512], BF16); b_brow = Buf()
    sel = sb("sel_s", [128, 7 * 128], BF16); b_sel = Buf()
    bfm = sb("bfm_s", [128, 15], F32); b_bfm = Buf()
    bfm8 = sb("bfm8_s", [128, 15], F32); b_bfm8 = Buf()
    lng = sb("lng_s", [128, 1024], F32); b_lng = Buf()
    lnb = sb("lnb_s", [128, 1024], F32); b_lnb = Buf()
    alng = sb("alng_s", [128, 256], F32); b_alng = Buf()
    alnb = sb("alnb_s", [128, 256], F32); b_alnb = Buf()
    wsT = sb("wsT_s", [128, 512], BF16); b_wsT = Buf()
    absT = sb("absT_s", [128, 128], BF16); b_absT = Buf()
    Emat = sb("E_s", [128, 256], BF16); b_E = Buf()
    ident4 = sb("ident4_s", [128, 512], BF16); b_id = Buf()
    pow2 = sb("pow2_s", [128, NBIS + 2], F32); b_pow2 = Buf()
    biasB0 = sb("biasB0_s", [128, 512], BF16); biasB1 = sb("biasB1_s", [128, 512], BF16)
    maskB4 = sb("maskB4_s", [128, 512], BF16)
    biasC0 = sb("biasC0_s", [128, 512], BF16); biasC1 = sb("biasC1_s", [128, 512], BF16)
    b_biasB0, b_biasB1, b_maskB4, b_biasC0, b_biasC1 = Buf(), Buf(), Buf(), Buf(), Buf()

    kCT = sb("kCT_s", [128, 2, SEQ], BF16); b_kC = [Buf() for _ in range(NBLK)]
    vC = sb("vC_s", [128, NBLK, 4, 65], BF16); b_vC = [Buf() for _ in range(NBLK)]
    ikT = sb("ikT_s", [128, SEQ], BF16); b_ik = [Buf() for _ in range(NBLK)]
    kBT = sb("kBT_s", [128, 2, 5 * 128], BF16); b_kB = [Buf() for _ in range(5)]
    vB = sb("vB_s", [128, 5, 4, 65], BF16); b_vB = [Buf() for _ in range(5)]
    kmT = sb("kmT_s", [128, 2, 256], BF16); b_kmT = Buf()
    vM = sb("vM_s", [128, 2, 4, 65], BF16); b_vM = Buf()

    score = sb("score_s", [128, SEQ], F32); b_sc = [Buf(), Buf()]
    mb = sb("mb_s", [128, SEQ], BF16); b_mb = Buf()
    wmb = mb[:, 0:2048]; b_wmb = b_mb
    gsc0 = sb("gscr_s", [128, 512], F32); b_gs = Buf()
    rt = sb("rt_s", [128, 1024], F32); b_rt = [Buf(), Buf()]
    rtmp = [rt[:, 0:512], rt[:, 512:1024]]; b_rtmp = b_rt
    junk = rt[:, :].bitcast(U8)
    memT = rt[:, :].bitcast(BF16).rearrange("p (a b) -> p a b", b=256)
    cntD = sb("cntD_s", [128, 2], F32); b_cntD = Buf()
    cntA = sb("cntA_s", [128, 2], F32); b_cntA = Buf()
    xin = [sb(f"xin{k}_s", [128, 1024], F32) for k in range(2)]; b_x = [Buf(), Buf()]
    mixed = [sb(f"mixed{k}_s", [128, 1024], BF16) for k in range(2)]; b_mixed = [Buf(), Buf()]
    xnT = sb("xnT_s", [128, 8, 128], BF16); b_xnT = Buf()
    mixedT = sb("mixedT_s", [128, 8, 128], BF16); b_mixedT = Buf()
    xg = sb("xg_s", [128, 512], F32); b_xg = Buf()
    tmpb = xg; b_tmpb = b_xg
    gates = sb("gates_s", [128, 768], BF16); b_gates = Buf()
    gatesC = [sb(f"gatesC{k}_s", [128, 256], BF16) for k in range(2)]; b_gatesC = [Buf(), Buf()]
    vln = sb("vln_s", [128, 256], BF16); b_vln = Buf()
    qblk = {"B": sb("qblkB_s", [128, 2, 256], BF16), "M": sb("qblkM_s", [128, 2, 256], BF16)}
    b_qblk = {"B": Buf(), "M": Buf()}
    qblkC = [sb(f"qblkC{k}_s", [128, 2, 256], BF16) for k in range(2)]; b_qblkC = [Buf(), Buf()]
    iqblk = sb("iqblk_s", [128, 8, 128], BF16); b_iqblk = Buf()
    iw = sb("iw_s", [128, 8], F32); b_iw = Buf()
    PTA = [sb("PTA0_s", [128, 512], BF16)]; b_PTA = [Buf()]
    PTC = [sb(f"PTC{k}_s", [128, 512], BF16) for k in range(2)]; b_PTC = [Buf(), Buf()]
    stA = sb("stA_s", [128, 32], F32); b_stA = Buf()
    stB = sb("stB_s", [128, 32], F32); b_stB = Buf()
    recs = {m: sb(f"rec{m}_s", [128, 4], F32) for m in "BMC"}; b_recs = {m: Buf() for m in "BMC"}
    cst = sb("cst_s", [128, 4], F32); b_cst = Buf()
    bis = sb("bis_s", [128, 32], F32); b_bis = Buf()
    rk = sb("rk_s", [128, NBIS + 2], F32); b_rk = Buf()

    NA, NS, NC = 2, 2, 2
    bigA = [ps(f"bigA{k}", [128, 512], F32) for k in range(NA)]; b_bigA = [Buf(psum=True) for _ in range(NA)]
    bigS = [ps(f"bigS{k}", [128, 512], F32) for k in range(NS)]; b_bigS = [Buf(psum=True) for _ in range(NS)]
    bigC = [ps(f"bigC{k}", [128, 512], F32) for k in range(NC)]; b_bigC = [Buf(psum=True) for _ in range(NC)]
    _accS = ps("accS", [128, 512], F32); _baccS = Buf(psum=True)
    _accC = ps("accC", [128, 512], F32); _baccC = Buf(psum=True)
    acc = {"B": _accS, "M": _accS, "C": _accC}; b_acc = {"B": _baccS, "M": _baccS, "C": _baccC}
    rr = {"A": 0, "S": 0, "C": 0, "PTA": 0, "PTC": 0, "rtmp": 0, "cvt": 0, "stage": 0}

    def nxt(key, n):
        k = rr[key]
        rr[key] = (k + 1) % n
        return k

    def bankA():
        k = nxt("A", NA)
        return bigA[k], b_bigA[k]

    def bankS():
        k = nxt("S", NS)
        return bigS[k], b_bigS[k]

    def bankC():
        k = nxt("C", NC)
        return bigC[k], b_bigC[k]

    d_stage = [S.dma_sem("st0"), S.dma_sem("st1")]
    d_x = [S.dma_sem("x0"), S.dma_sem("x1")]
    d_o = [S.dma_sem("o0"), S.dma_sem("o1")]
    d_misc = {}

    def dsem(name):
        if name not in d_misc:
            d_misc[name] = S.dma_sem(name)
        return d_misc[name]

    V, A, G, T = nc.vector, nc.scalar, nc.gpsimd, nc.tensor
    ENG = {"dve": V, "act": A, "pool": G}

    cur_q = [None]

    def free_elems(ap):
        n = 1
        for d in tuple(ap.shape)[1:]:
            n *= int(d)
        return n

    def issue(eng, fn, reads, writes, dur, ds=None, holder=None):
        if cur_q[0] is not None:
            cur_q[0].append((eng, fn, list(reads), list(writes), dur, ds, holder))
            return None
        if ds is not None:
            tok = S.dma(eng, ds, fn, reads=reads, writes=writes)
            if holder is not None:
                holder[0][holder[1]] = tok
            return tok
        return S.op(eng, fn, reads=reads, writes=writes)

    def op(eng, name, reads, writes, *args, **kw):
        f = getattr(ENG[eng], name)
        o = kw.get("out", args[0] if args else None)
        n = free_elems(o) if o is not None else 1
        if eng == "dve":
            dur = 0.12 + n / 960.0 + (0.08 if "accum_out" in kw else 0.0)
        elif eng == "act":
            dur = 0.22 + n / 1200.0 + (0.1 if "accum_out" in kw else 0.0)
        else:
            dur = 0.3 + n / 480.0
        issue(eng, lambda: f(*args, **kw), reads, writes, dur)

    def mm(out, lhsT, rhs, start, reads, writes):
        dur = 0.1 + free_elems(rhs) / 1500.0
        issue("pe", lambda: T.matmul(out, lhsT, rhs, start=start, stop=True, skip_group_check=True), reads, writes, dur)

    def transpose(out, in_, reads, writes):
        idn = ident4[:, 0:128]
        issue("pe", lambda: T.transpose(out, in_, idn), reads + [b_id], writes, 0.2)

    def cast_copy(k, out, in_, reads, writes, scale=None):
        e = k % 3
        if scale is not None:
            if e == 1:
                op("act", "mul", reads, writes, out=out, in_=in_, mul=scale)
            else:
                op("dve" if e == 0 else "pool", "tensor_scalar", reads, writes, out=out, in0=in_, scalar1=scale, scalar2=None, op0=ALU.mult)
            return
        if e == 0:
            op("dve", "tensor_copy", reads, writes, out=out, in_=in_)
        elif e == 1:
            op("act", "copy", reads, writes, out=out, in_=in_)
        else:
            op("pool", "tensor_copy", reads, writes, out=out, in_=in_)

    def dma(ds, out, in_, reads, writes, holder=None):
        return issue("sp", lambda: nc.sync.dma_start(out=out, in_=in_), reads, writes, 3.0, ds=ds, holder=holder)

    def load_direct(name, dst, bdst, src):
        dma(dsem(name), dst, src, [], [bdst])

    def load_cvt(dst, bdst, src, L, post=None, scale=None):
        c0 = 0
        while c0 < L:
            n = min(2048, L - c0)
            h = nxt("stage", 2)
            stg = score[:, h * 2048:h * 2048 + n]
            dma(d_stage[h], stg, src[:, c0:c0 + n], [], [b_sc[h]])
            if post is None:
                cast_copy(nxt("cvt", 3), dst[:, c0:c0 + n], stg, [b_sc[h]], [bdst], scale=scale)
            else:
                post(dst[:, c0:c0 + n], stg, c0, n, h)
            c0 += n

    load_cvt(ident4, b_id, ident4_d, 512)
    load_cvt(Emat, b_E, E_d, 256)
    load_cvt(sel, b_sel, sel_d, 7 * 128)
    load_direct("pow2", pow2[:], b_pow2, pow2_d)
    op("dve", "memset", [], [b_cst], cst[:, 0:1], -0.5)
    load_direct("tmpb", tmpb[:], b_tmpb, cC_d)
    for d, (dstt, bd) in enumerate(((biasC0, b_biasC0), (biasC1, b_biasC1))):
        def post(dst, stg, c0, n, h, bd=bd):
            op("dve", "tensor_tensor", [b_sc[h], b_tmpb], [bd], out=dst, in0=stg, in1=tmpb[:, c0:c0 + n], op=ALU.subtract)
        load_cvt(dstt, bd, biasC_d[d], 512, post=post)
    load_cvt(maskB4, b_maskB4, maskB_d[1], 512)
    for m in "BM":
        op("pool", "memset", [], [b_qblk[m]], qblk[m][:], 0.0)
    for k in range(2):
        op("pool", "memset", [], [b_qblkC[k]], qblkC[k][:], 0.0)
    op("pool", "memset", [], [b_iqblk], iqblk[:], 0.0)
    op("pool", "memset", [], b_vC, vC[:, :, :, 64:65], 1.0)
    op("pool", "memset", [], b_vB, vB[:, :, :, 64:65], 1.0)
    op("pool", "memset", [], [b_vM], vM[:, :, :, 64:65], 1.0)

    def layer_norm(X, bX, width, gt, bg_, bt, bb_, stt, bst, out=None, bout=None):
        nch = (width + 511) // 512
        cw = width // nch
        for c in range(nch):
            op("dve", "bn_stats", [bX], [bst], out=stt[:, 8 + 6 * c:14 + 6 * c], in_=X[:, c * cw:(c + 1) * cw])
        op("dve", "bn_aggr", [bst], [bst], out=stt[:, 0:2], in_=stt[:, 8:8 + 6 * nch])
        op("dve", "tensor_scalar", [bst], [bst], out=stt[:, 2:3], in0=stt[:, 1:2], scalar1=LN_EPS, scalar2=None, op0=ALU.add)
        op("pool", "tensor_tensor", [bst, b_cst], [bst], out=stt[:, 3:4], in0=stt[:, 2:3], in1=cst[:, 0:1], op=ALU.pow)
        op("dve", "tensor_scalar", [bst], [bst], out=stt[:, 4:5], in0=stt[:, 0:1], scalar1=stt[:, 3:4], scalar2=-1.0, op0=ALU.mult, op1=ALU.mult)
        op("act", "activation", [bX, bst], [bX], out=X, in_=X, func=AF.Identity, bias=stt[:, 4:5], scale=stt[:, 3:4])
        op("dve", "tensor_tensor", [bX, bg_], [bX], out=X, in0=X, in1=gt, op=ALU.mult)
        if out is None:
            op("pool", "tensor_tensor", [bX, bb_], [bX], out=X, in0=X, in1=bt, op=ALU.add)
        else:
            op("pool", "tensor_tensor", [bX, bb_], [bout], out=out, in0=X, in1=bt, op=ALU.add)

    def attention(mixer, q, bq, keys, mixed_t, bmixed, mixed_off, gate_ap, bgate, bank_fn, PTs, bPTs, ptkey):
        oacc = acc[mixer]
        bacc = b_acc[mixer]
        nk = len(keys)
        pend = []
        npt = len(PTs)

        def stage1(jj):
            (k0, k1, rk_, vfn, rv_, extra) = keys[jj]
            ST, bST = bank_fn()
            mm(ST[:, 0:256], k0, q[:, 0, :], True, rk_ + [bq], [bST])
            mm(ST[:, 256:512], k1, q[:, 1, :], False, rk_ + [bq], [bST])
            for (l_, r_, rd_) in extra:
                mm(ST[:, :], l_, r_, False, rd_, [bST])
            pk = nxt(ptkey, npt)
            op("act", "activation", [bST], [bPTs[pk]], out=PTs[pk][:], in_=ST[:, :], func=AF.Exp)
            pend.append((jj, pk))

        def stage2():
            jj, pk = pend.pop(0)
            (k0, k1, rk_, vfn, rv_, extra) = keys[jj]
            for h in range(4):
                mm(oacc[:, h * 65:(h + 1) * 65], PTs[pk][:, h * 128:(h + 1) * 128], vfn(h), (jj == 0 and h == 0), [bPTs[pk]] + rv_, [bacc])

        look = min(2, npt)
        for jj in range(nk):
            stage1(jj)
            if len(pend) >= look:
                stage2()
            yield
        while pend:
            stage2()
        ov = oacc[:, 0:260].rearrange("p (h d) -> p h d", d=65)
        rec = recs[mixer]
        op("dve", "reciprocal", [bacc], [b_recs[mixer]], out=rec[:, 0:4], in_=ov[:, :, 64])
        for h in range(4):
            op("dve", "scalar_tensor_tensor", [bacc, b_recs[mixer], bgate], [bmixed],
               out=mixed_t[:, mixed_off + h * 64:mixed_off + (h + 1) * 64], in0=oacc[:, h * 65:h * 65 + 64],
               scalar=rec[:, h:h + 1], in1=gate_ap[:, h * 64:(h + 1) * 64], op0=ALU.mult, op1=ALU.mult)
        yield

    last_store = [None, None]

    def prepass():
        load_direct("lng", lng[:], b_lng, lnin_d[0])
        load_direct("lnb", lnb[:], b_lnb, lnin_d[1])
        for i in range(nblk):
            s_ = i % 2
            X = xin[s_]
            dma(d_x[s_], X[:], x_d[i * 128:(i + 1) * 128, :], [], [b_x[s_]])
            layer_norm(X[:], b_x[s_], 1024, lng[:], b_lng, lnb[:], b_lnb, stB, b_stB)
            dma(d_o[s_], out_d[i * 128:(i + 1) * 128, :], X[:], [b_x[s_]], [b_o[i]], holder=(last_store, s_))
            yield

    def setup_A(l):
        load_cvt(wtm, b_wtm, wtm_d[l], 8 * NTM)
        load_cvt(wfm, b_wfm, wfm_d[l], 8 * NFM)
        load_cvt(wout, b_wout, wout_d[l], 8 * 1024, scale=0.5)
        load_cvt(brow, b_brow, brow_d[l], 512)
        load_cvt(wsT, b_wsT, wsT_d[l], 512)
        op("pool", "memset", [], [b_wsT], wsT[64:128, :].rearrange("p (g i) -> p g i", g=4)[:, :, 0:64], 0.0)
        load_cvt(absT, b_absT, absT_d[l], 128)
        load_direct("bfm", bfm[:], b_bfm, bfm_d[l])
        op("dve", "tensor_scalar", [b_bfm], [b_bfm8], out=bfm8[:], in0=bfm[:], scalar1=0.125, scalar2=None, op0=ALU.mult)
        load_direct("alng", alng[:], b_alng, alng_d[l, 0])
        load_direct("alnb", alnb[:], b_alnb, alng_d[l, 1])
        load_direct("tmpb", tmpb[:], b_tmpb, cB_d[l])
        for m_, (dstt, bd) in enumerate(((biasB0, b_biasB0), (biasB1, b_biasB1))):
            def post(dst, stg, c0, n, h, bd=bd):
                op("dve", "tensor_tensor", [b_sc[h], b_tmpb], [bd], out=dst, in0=stg, in1=tmpb[:, c0:c0 + n], op=ALU.subtract)
            load_cvt(dstt, bd, biasB_d[l, m_], 512, post=post)
        load_direct("tmpb", tmpb[:], b_tmpb, maskB_d[0])
        op("dve", "tensor_tensor", [b_biasB0, b_tmpb], [b_biasB0], out=biasB0[:], in0=biasB0[:], in1=tmpb[:], op=ALU.add)

    def setup_B(l):
        load_direct("lng", lng[:], b_lng, lng_d[l, 0])
        load_direct("lnb", lnb[:], b_lnb, lng_d[l, 1])
        for mt in range(2):
            X = xin[mt]
            dma(d_x[mt], X[:], mem_d[mt * 128:(mt + 1) * 128, :], [], [b_x[mt]])
            op("act", "copy", [b_x[mt]], [b_mixed[0]], out=mixed[0][:], in_=X[:])
            tb, btb = bankA()
            tv = tb[:, :].bitcast(BF16)
            for kt in range(8):
                transpose(tv[:, kt * 128:(kt + 1) * 128], mixed[0][:, kt * 128:(kt + 1) * 128], [b_mixed[0]], [btb])
            op("dve", "tensor_copy", [btb], b_rt, out=memT[:, :, mt * 128:(mt + 1) * 128], in_=tv.rearrange("p (a b) -> p a b", b=128))
        load_cvt(wmb, b_wmb, wmk_d[l], 8 * 256)
        for t in range(2):
            bk, bbk = bankA()
            for kt in range(8):
                mm(bk[:, 0:256], wmb[:, kt * 256 + t * 128:kt * 256 + (t + 1) * 128], memT[:, kt, :], kt == 0, [b_wmb] + b_rt, [bbk])
            op("dve", "tensor_copy", [bbk], [b_kmT], out=kmT[:, t, :], in_=bk[:, 0:256])
        load_cvt(wmb, b_wmb, wmv_d[l], 8 * 256)
        for mt in range(2):
            bk, bbk = bankA()
            for kt in range(8):
                mm(bk[:, 0:256], memT[:, kt, mt * 128:(mt + 1) * 128], wmb[:, kt * 256:(kt + 1) * 256], kt == 0, [b_wmb] + b_rt, [bbk])
            op("dve", "tensor_copy", [bbk], [b_vM], out=vM[:, mt, :, 0:64], in_=bk[:, 0:256].rearrange("p (h d) -> p h d", d=64))

    lo, hi, full = slice(0, 64), slice(64, 128), slice(0, 128)

    def ev(out, in0, prt, ct, scale, reads, writes):
        if ct < 10:
            if scale is None:
                op("act", "activation", reads, writes, out=out, in_=in0, func=AF.Identity, bias=bfm[prt, ct:ct + 1], scale=1.0)
            else:
                op("act", "activation", reads + [b_bfm8], writes, out=out, in_=in0, func=AF.Identity, bias=bfm8[prt, ct:ct + 1], scale=scale)
            return
        if scale is None:
            op("dve", "tensor_scalar", reads, writes, out=out, in0=in0, scalar1=bfm[prt, ct:ct + 1], scalar2=None, op0=ALU.add)
        else:
            op("dve", "tensor_scalar", reads, writes, out=out, in0=in0, scalar1=bfm[prt, ct:ct + 1], scalar2=scale, op0=ALU.add, op1=ALU.mult)

    def fm_group(i, cts, bank_fn):
        p_ = i % 2
        sl = i % 5
        bk, bbk = bank_fn()
        first = True
        for ci, ct in enumerate(cts):
            for kt in range(8):
                mm(bk[:, ci * 128:(ci + 1) * 128], wfm[:, kt * NFM + ct * 128:kt * NFM + (ct + 1) * 128], xnT[:, kt, :], first, [b_wfm, b_xnT], [bbk])
                first = False
        for ci, ct in enumerate(cts):
            pst = bk[:, ci * 128:(ci + 1) * 128]
            rd = [bbk, b_bfm]
            if ct in (0, 1, 8, 9):
                m = "B" if ct < 2 else "M"
                t = ct % 2
                ev(qblk[m][lo, t, 0:128], pst[lo, :], lo, ct, 0.125, rd, [b_qblk[m]])
                ev(qblk[m][hi, t, 128:256], pst[hi, :], hi, ct, 0.125, rd, [b_qblk[m]])
            elif ct in (4, 5):
                t = ct % 2
                ev(qblkC[p_][lo, t, 0:128], pst[lo, :], lo, ct, 0.125, rd, [b_qblkC[p_]])
                ev(qblkC[p_][hi, t, 128:256], pst[hi, :], hi, ct, 0.125, rd, [b_qblkC[p_]])
            elif ct in (2, 3):
                ev(kBT[:, ct - 2, sl * 128:(sl + 1) * 128], pst[:, :], full, ct, None, rd, [b_kB[sl]])
            elif ct in (6, 7):
                ev(kCT[:, ct - 6, i * 128:(i + 1) * 128], pst[:, :], full, ct, None, rd, [b_kC[i]])
            elif ct in (10, 11, 12, 13):
                t = ct - 10
                ev(iqblk[lo, 2 * t, :], pst[lo, :], lo, ct, None, rd, [b_iqblk])
                ev(iqblk[hi, 2 * t + 1, :], pst[hi, :], hi, ct, None, rd, [b_iqblk])
            else:
                ev(ikT[:, i * 128:(i + 1) * 128], pst[:, :], full, ct, None, rd, [b_ik[i]])

    def tm_tile(r, c0, n, bank_fn):
        bk, bbk = bank_fn()
        for kt in range(8):
            mm(bk[:, 0:n], xnT[:, kt, :], wtm[:, kt * NTM + c0:kt * NTM + c0 + n], kt == 0, [b_xnT, b_wtm], [bbk])
        mm(bk[:, 0:n], sel[:, r * 128:(r + 1) * 128], brow[:, 0:n], False, [b_sel, b_brow], [bbk])
        return bk, bbk

    def phaseMain(i):
        s_ = i % 2
        p_ = i % 2
        X = xin[s_]
        N_i = (i + 1) * 128
        mx = mixed[p_]
        bmx = b_mixed[p_]
        dma(d_x[s_], X[:], out_d[i * 128:(i + 1) * 128, :], [b_o[i]], [b_x[s_]])
        op("act", "copy", [b_x[s_]], [bmx], out=mx[:], in_=X[:])
        tb, btb = bankA()
        tv = tb[:, :].bitcast(BF16)
        for kt in range(8):
            transpose(tv[:, kt * 128:(kt + 1) * 128], mx[:, kt * 128:(kt + 1) * 128], [bmx], [btb])
        op("dve", "tensor_copy", [btb], [b_xnT], out=xnT[:].rearrange("p a b -> p (a b)"), in_=tv)
        yield "XNT"
        fm_group(i, [10, 11, 12, 13], bankA)
        yield 0.2
        fm_group(i, [14], bankA)
        bk, bbk = tm_tile(4, 2048, 8, bankA)
        op("dve", "tensor_copy", [bbk], [b_iw], out=iw[:], in_=bk[:, 0:8])
        yield 0.5
        ntile = (N_i + 511) // 512
        for c in range(ntile):
            c0 = c * 512
            n = min(512, N_i - c0)
            half = [b_sc[0]] if c0 + n <= 2048 else [b_sc[1]]
            rik = [b_ik[jj] for jj in range(c0 // 128, (c0 + n) // 128)]
            for h in range(8):
                bk, bbk = bankA()
                mm(bk[:, 0:n], iqblk[:, h, :], ikT[:, c0:c0 + n], True, [b_iqblk] + rik, [bbk])
                if h == 0:
                    op("dve", "tensor_scalar", [bbk, b_iw], half, out=score[:, c0:c0 + n], in0=bk[:, 0:n], scalar1=0.0, scalar2=iw[:, 0:1], op0=ALU.max, op1=ALU.mult)
                else:
                    r_ = nxt("rtmp", 2)
                    op("act", "activation", [bbk], [b_rtmp[r_]], out=rtmp[r_][:, 0:n], in_=bk[:, 0:n], func=AF.Relu)
                    op("dve", "scalar_tensor_tensor", [b_rtmp[r_], b_iw] + half, half, out=score[:, c0:c0 + n], in0=rtmp[r_][:, 0:n], scalar=iw[:, h:h + 1],
                       in1=score[:, c0:c0 + n], op0=ALU.mult, op1=ALU.add)
                yield W_IDX * n / 512.0
        SC = b_sc if N_i > 2048 else [b_sc[0]]
        lasthalf = [b_sc[1]] if N_i > 2048 else [b_sc[0]]
        op("pool", "memset", [], lasthalf, score[0:64, N_i - 64:N_i], -1e30)
        if i >= 2:
            op("dve", "tensor_reduce", SC, [b_bis], out=bis[:, 0:1], in_=score[:, 0:N_i], axis=AX.X, op=ALU.max)
            op("dve", "tensor_reduce", SC, [b_bis], out=bis[:, 1:2], in_=score[:, 0:N_i - 64], axis=AX.X, op=ALU.min)
            op("dve", "tensor_tensor", [b_bis], [b_bis], out=bis[:, 2:3], in0=bis[:, 0:1], in1=bis[:, 1:2], op=ALU.subtract)
            op("dve", "tensor_scalar", [b_bis, b_pow2], [b_rk], out=rk[:, :], in0=pow2[:, :], scalar1=bis[:, 2:3], scalar2=None, op0=ALU.mult)
            op("dve", "tensor_tensor", [b_bis, b_rk], [b_bis], out=bis[:, 8:9], in0=bis[:, 1:2], in1=rk[:, 1:2], op=ALU.add)
            yield 1.0 + N_i / 500.0
            nd = max(64, int(round(N_i * DVE_FRAC / 64)) * 64, N_i - 2048)
            na = N_i - nd
            SCd = [b_sc[0]] if nd <= 2048 else b_sc
            SCa = [b_sc[1]] if nd >= 2048 else (b_sc if N_i > 2048 else [b_sc[0]])
            for k in range(1, NBIS + 1):
                mid = bis[:, 8 + (k - 1) % 2:9 + (k - 1) % 2]
                midn = bis[:, 8 + k % 2:9 + k % 2]
                op("dve", "tensor_scalar", SCd + [b_bis], [b_rt[0], b_cntD], out=junk[:, 0:nd], in0=score[:, 0:nd], scalar1=mid, scalar2=None,
                   op0=ALU.is_ge, op1=ALU.add, accum_out=cntD[:, 0:1])
                op("act", "activation", SCa + [b_bis], [b_rt[1], b_cntA], out=junk[:, 2048:2048 + na], in_=score[:, nd:N_i], func=AF.Sign, bias=mid, scale=-1.0,
                   accum_out=cntA[:, 0:1])
                op(SMALL_ENG, "tensor_scalar", [b_cntD, b_cntA], [b_bis], out=bis[:, 4:5], in0=cntD[:, 0:1], scalar1=2.0, scalar2=cntA[:, 0:1], op0=ALU.mult, op1=ALU.subtract)
                op(SMALL_ENG, "tensor_scalar", [b_bis, b_rk], [b_bis], out=bis[:, 5:6], in0=bis[:, 4:5], scalar1=float(512 - na), scalar2=rk[:, k:k + 1], op0=ALU.is_ge, op1=ALU.mult)
                kk = k + 1 if k < NBIS else k
                op(SMALL_ENG, "tensor_scalar", [b_bis, b_rk], [b_bis], out=midn, in0=bis[:, 5:6], scalar1=rk[:, kk:kk + 1], scalar2=mid, op0=ALU.subtract, op1=ALU.add)
                yield 1.2 + N_i / 1100.0
            thr = bis[:, 8 + NBIS % 2:9 + NBIS % 2]
        else:
            op("dve", "memset", [], [b_bis], bis[:, 12:13], -1e29)
            thr = bis[:, 12:13]
        yield "B_DONE"
        c0 = 0
        while c0 < N_i:
            n = min(2048, N_i - c0)
            op("dve", "tensor_scalar", [b_sc[c0 // 2048], b_bis], [b_mb], out=mb[:, c0:c0 + n], in0=score[:, c0:c0 + n], scalar1=thr, scalar2=NEG, op0=ALU.is_lt, op1=ALU.mult)
            c0 += n
        yield 0.0

    def main_weight(i):
        N_i = (i + 1) * 128
        w = 1.2 + W_IDX * 8 * N_i / 512.0
        if i >= 2:
            w += 1.0 + N_i / 500.0 + NBIS * (1.2 + N_i / 1100.0)
        return w

    def phaseSide(i):
        s_ = i % 2
        p_ = i % 2
        sl = i % 5
        mx = mixed[p_]
        bmx = b_mixed[p_]
        bk, bbk = tm_tile(0, 0, 512, bankS)
        op("act", "copy", [bbk], [b_xg], out=xg[:], in_=bk[:, :])
        op("pool", "tensor_tensor", [b_xg], [b_gs], out=gsc0[:], in0=xg[:], in1=xg[:], op=ALU.mult)
        op("dve", "scalar_tensor_tensor", [b_gs, b_xg], [b_gs], out=gsc0[:], in0=gsc0[:], scalar=0.044715, in1=xg[:], op0=ALU.mult, op1=ALU.mult)
        op("pool", "tensor_tensor", [b_gs, b_xg], [b_gs], out=gsc0[:], in0=gsc0[:], in1=xg[:], op=ALU.add)
        op("act", "activation", [b_gs], [b_gs], out=gsc0[:], in_=gsc0[:], func=AF.Tanh, scale=0.7978845608028654)
        op("dve", "scalar_tensor_tensor", [b_gs, b_xg], [b_xg], out=xg[:], in0=gsc0[:], scalar=1.0, in1=xg[:], op0=ALU.add, op1=ALU.mult)
        yield
        bk, bbk = tm_tile(1, 512, 512, bankS)
        op("act", "activation", [bbk], [b_gs], out=gsc0[:], in_=bk[:, :], func=AF.Tanh, scale=0.5)
        op("dve", "scalar_tensor_tensor", [b_gs, bbk], [b_gates], out=gates[:, 0:512], in0=gsc0[:], scalar=1.0, in1=bk[:, :], op0=ALU.add, op1=ALU.mult)
        yield
        bk, bbk = tm_tile(2, 1024, 512, bankS)
        op("act", "activation", [bbk], [b_gs], out=gsc0[:], in_=bk[:, :], func=AF.Tanh, scale=0.5)
        op("dve", "scalar_tensor_tensor", [b_gs, bbk], [b_gatesC[p_]], out=gatesC[p_][:, :], in0=gsc0[:, 0:256], scalar=1.0, in1=bk[:, 0:256], op0=ALU.add, op1=ALU.mult)
        op("dve", "scalar_tensor_tensor", [b_gs, bbk], [b_gates], out=gates[:, 512:768], in0=gsc0[:, 256:512], scalar=1.0, in1=bk[:, 256:512], op0=ALU.add, op1=ALU.mult)
        yield
        bk, bbk = tm_tile(3, 1536, 512, bankS)
        op("act", "copy", [bbk], [b_vB[sl]], out=vB[:, sl, :, 0:64], in_=bk[:, 0:256].rearrange("p (h d) -> p h d", d=64))
        op("act", "copy", [bbk], [b_vC[i]], out=vC[:, i, :, 0:64], in_=bk[:, 256:512].rearrange("p (h d) -> p h d", d=64))
        yield
        for cts in ([0, 1, 2, 3], [4, 5, 6, 7], [8, 9]):
            fm_group(i, cts, bankS)
            yield
        op("act", "mul", [b_xg], [b_xg], out=xg[:, 256:512], in_=xg[:, 256:512], mul=0.5)
        layer_norm(xg[:, 256:512], b_xg, 256, alng[:], b_alng, alnb[:], b_alnb, stA, b_stA, out=vln[:], bout=b_vln)
        bk, bbk = bankS()
        for g in range(4):
            mm(bk[:, g * 64:(g + 1) * 64], wsT[:, g * 128:(g + 1) * 128], vln[:, g * 64:(g + 1) * 64], g == 0, [b_wsT, b_vln], [bbk])
        mm(bk[:, 0:256], absT[:, :], Emat[:, :], False, [b_absT, b_E], [bbk])
        op("dve", "scalar_tensor_tensor", [bbk, b_xg], [b_xg], out=xg[:, 256:512], in0=bk[:, 0:256], scalar=0.5, in1=xg[:, 0:256], op0=ALU.mult, op1=ALU.mult)
        op("pool", "tensor_tensor", [b_xg, b_gates], [bmx], out=mx[:, 0:256], in0=xg[:, 256:512], in1=gates[:, 0:256], op=ALU.mult)
        yield
        keys = []
        for j in range(max(0, i - 4), i + 1):
            sj = j % 5
            m_ = i - j
            extra = []
            if m_ == 0:
                extra.append((ident4[:, 0:128], biasB0[:, :], [b_id, b_biasB0]))
            elif m_ == 1:
                extra.append((ident4[:, 0:128], biasB1[:, :], [b_id, b_biasB1]))
            elif m_ == 4:
                extra.append((ident4[:, 0:128], maskB4[:, :], [b_id, b_maskB4]))
            keys.append((kBT[:, 0, sj * 128:(sj + 1) * 128], kBT[:, 1, sj * 128:(sj + 1) * 128], [b_kB[sj]],
                         (lambda h, sj=sj: vB[:, sj, h, :]), [b_vB[sj]], extra))
        yield from attention("B", qblk["B"], b_qblk["B"], keys, mx, bmx, 256, gates[:, 256:512], b_gates, bankS, PTA, b_PTA, "PTA")
        keys = []
        for mt in range(2):
            keys.append((kmT[:, 0, mt * 128:(mt + 1) * 128], kmT[:, 1, mt * 128:(mt + 1) * 128], [b_kmT],
                         (lambda h, mt=mt: vM[:, mt, h, :]), [b_vM], []))
        yield from attention("M", qblk["M"], b_qblk["M"], keys, mx, bmx, 768, gates[:, 512:768], b_gates, bankS, PTA, b_PTA, "PTA")

    def phaseB(i, l):
        s_ = i % 2
        p_ = i % 2
        X = xin[s_]
        mx = mixed[p_]
        bmx = b_mixed[p_]
        keys = []
        for j in range(0, i + 1):
            extra = [(mb[:, j * 128:(j + 1) * 128], ident4[:, :], [b_mb, b_id])]
            d_ = i - j
            if d_ == 0:
                extra.append((ident4[:, 0:128], biasC0[:, :], [b_id, b_biasC0]))
            elif d_ == 1:
                extra.append((ident4[:, 0:128], biasC1[:, :], [b_id, b_biasC1]))
            keys.append((kCT[:, 0, j * 128:(j + 1) * 128], kCT[:, 1, j * 128:(j + 1) * 128], [b_kC[j]],
                         (lambda h, j=j: vC[:, j, h, :]), [b_vC[j]], extra))
        yield from attention("C", qblkC[p_], b_qblkC[p_], keys, mx, bmx, 512, gatesC[p_][:, :], b_gatesC[p_], bankC, PTC, b_PTC, "PTC")
        tb, btb = bankC()
        tv = tb[:, :].bitcast(BF16)
        for kt in range(8):
            transpose(tv[:, kt * 128:(kt + 1) * 128], mx[:, kt * 128:(kt + 1) * 128], [bmx], [btb])
        op("dve", "tensor_copy", [btb], [b_mixedT], out=mixedT[:].rearrange("p a b -> p (a b)"), in_=tv)
        yield
        for c in range(2):
            bk, bbk = bankC()
            for kt in range(8):
                mm(bk[:, :], mixedT[:, kt, :], wout[:, kt * 1024 + c * 512:kt * 1024 + (c + 1) * 512], kt == 0, [b_mixedT, b_wout], [bbk])
            mm(bk[:, :], sel[:, (5 + c) * 128:(6 + c) * 128], brow[:, :], False, [b_sel, b_brow], [bbk])
            op("dve", "scalar_tensor_tensor", [b_x[s_], bbk], [b_x[s_]], out=X[:, c * 512:(c + 1) * 512], in0=X[:, c * 512:(c + 1) * 512], scalar=ALPHA, in1=bk[:, :],
               op0=ALU.mult, op1=ALU.add)
            yield
        layer_norm(X[:], b_x[s_], 1024, lng[:], b_lng, lnb[:], b_lnb, stB, b_stB)
        dma(d_o[s_], out_d[i * 128:(i + 1) * 128, :], X[:], [b_x[s_]], [b_o[i]], holder=(last_store, s_))
        yield

    def drain(g):
        for _ in g:
            pass

    class Stream:
        def __init__(self, gen, prio):
            self.gen = gen
            self.q = []
            self.done = False
            self.prio = prio
            self.active = True

    eng_free = {e: 0.0 for e in ENGS}
    tok_done = {}

    def fill(st):
        while not st.q and not st.done:
            cur_q[0] = st.q
            try:
                y = next(st.gen)
                if isinstance(y, str):
                    st.q.append(y)
            except StopIteration:
                st.done = True
            finally:
                cur_q[0] = None

    def remaining(st):
        while not st.done:
            cur_q[0] = st.q
            try:
                y = next(st.gen)
                if isinstance(y, str):
                    st.q.append(y)
            except StopIteration:
                st.done = True
            finally:
                cur_q[0] = None
        return sum(it[4] for it in st.q if not isinstance(it, str))

    def ready_time(eng, reads, writes):
        t = 0.0
        for bf in reads:
            if bf.writer is not None:
                t = max(t, tok_done.get((id(bf.writer[0]), bf.writer[1]), 0.0) + (0.0 if bf.writer[2] == eng else 0.12))
            if bf.psum:
                for tk in bf.readers.values():
                    if tk[2] != eng:
                        t = max(t, tok_done.get((id(tk[0]), tk[1]), 0.0) + 0.12)
        for bf in writes:
            if bf.writer is not None:
                t = max(t, tok_done.get((id(bf.writer[0]), bf.writer[1]), 0.0) + (0.0 if bf.writer[2] == eng else 0.12))
            for tk in bf.readers.values():
                if tk[2] != eng:
                    t = max(t, tok_done.get((id(tk[0]), tk[1]), 0.0) + 0.12)
        return t

    def commit(item, start):
        eng, fn, reads, writes, dur, ds, holder = item
        if ds is not None:
            tok = S.dma(eng, ds, fn, reads=reads, writes=writes)
            if holder is not None:
                holder[0][holder[1]] = tok
            eng_free[eng] = start + 0.1
        else:
            tok = S.op(eng, fn, reads=reads, writes=writes)
            eng_free[eng] = start + dur
        tok_done[(id(tok[0]), tok[1])] = start + dur

    def run_round(gmain, gside, gB):
        main = Stream(gmain, 0.0)
        side = Stream(gside, 0.0) if gside is not None else None
        stB = Stream(gB, 0.0) if gB is not None else None
        if side is not None:
            side.active = False
        streams = [x for x in (main, stB, side) if x is not None]
        while True:
            best = None
            alive = False
            cands = []
            for st in streams:
                fill(st)
                if not st.q:
                    continue
                alive = True
                if not st.active:
                    continue
                item = st.q[0]
                if isinstance(item, str):
                    if item == "XNT":
                        st.q.pop(0)
                        if side is not None:
                            side.active = True
                        best = "again"
                        break
                    if item == "B_DONE":
                        if stB is None or (stB.done and not stB.q):
                            st.q.pop(0)
                            best = "again"
                            break
                        continue
                    st.q.pop(0)
                    best = "again"
                    break
                eng = item[0]
                start = max(eng_free[eng], ready_time(eng, item[2], item[3]))
                cands.append((start, st))
            if best == "again":
                continue
            if cands:
                smin = min(c[0] for c in cands)
                c = max((c for c in cands if c[0] <= smin + SCHED_EPS), key=lambda c: (remaining(c[1]), -c[0]))
                best = (c[0], c[0], c[1])
            if best is None:
                if not alive:
                    break
                raise RuntimeError("scheduler stuck")
            _, start, st = best
            commit(st.q.pop(0), start)

    def run_streams(gens):
        sts = [Stream(g, 0.0) for g in gens]
        while True:
            best = None
            for st in sts:
                fill(st)
                if not st.q:
                    continue
                item = st.q[0]
                if isinstance(item, str):
                    st.q.pop(0)
                    best = "again"
                    break
                start = max(eng_free[item[0]], ready_time(item[0], item[2], item[3]))
                if best is None or start < best[0]:
                    best = (start, st)
            if best == "again":
                continue
            if best is None:
                break
            commit(best[1].q.pop(0), best[0])

    def once(f, *a):
        f(*a)
        yield

    for l in range(n_layers):
        if l == 0:
            run_streams([prepass(), once(setup_A, 0)])
        else:
            setup_A(l)
        setup_B(l)
        if not pipeline:
            for i in range(nblk):
                drain(phaseMain(i))
                drain(phaseSide(i))
                drain(phaseB(i, l))
        else:
            run_round(phaseMain(0), phaseSide(0), None)
            for i in range(nblk):
                if i + 1 < nblk:
                    run_round(phaseMain(i + 1), phaseSide(i + 1), phaseB(i, l))
                else:
                    drain(phaseB(i, l))

    S.wait_tokens("sp", [t for t in last_store if t is not None])
    S.emit()
    es.close()
    return nc


def _t5_bucket_idx(rel):
    nb = 16
    max_exact = 8
    ret = np.where(rel > 0, nb, 0)
    n = np.abs(rel)
    nf = np.maximum(n, 1).astype(np.float32)
    large = max_exact + (np.log(nf / np.float32(max_exact)) / np.float32(np.log(128 / max_exact)) * (nb - max_exact)).astype(np.int32)
    large = np.minimum(large, nb - 1)
    return ret + np.where(n < max_exact, n, large)


def _kt_layout(w):
    C = w.shape[1]
    return np.ascontiguousarray(w.reshape(8, 128, C).transpose(1, 0, 2).reshape(128, 8 * C))


def prepare(inputs):
    f = np.float32
    cum = np.cumsum((0,) + SPLITS)
    rng = {n: (int(cum[k]), int(cum[k + 1])) for k, n in enumerate(NAMES)}
    tm_cols = np.concatenate([np.arange(*rng[n]) for n in TM_ORDER])
    fm_cols = np.concatenate([np.arange(*rng[n]) for n in FM_ORDER])
    w_in = np.asarray(inputs["w_in"], f)
    b_in = np.asarray(inputs["b_in"], f)
    sh = {}
    sh["wtm"] = np.stack([_kt_layout(w_in[l][:, tm_cols]) for l in range(DEPTH)])
    sh["wfm"] = np.stack([_kt_layout(w_in[l][:, fm_cols]) for l in range(DEPTH)])
    sh["wout"] = np.stack([_kt_layout(np.asarray(inputs["w_out"], f)[l]) for l in range(DEPTH)])
    wm = np.asarray(inputs["w_mem_kv"], f)
    sh["wmk"] = np.stack([_kt_layout(wm[l][:, 0:256]) for l in range(DEPTH)])
    sh["wmv"] = np.stack([_kt_layout(wm[l][:, 256:512]) for l in range(DEPTH)])
    brow = np.zeros((DEPTH, 128, 512), f)
    btm = b_in[:, tm_cols]
    for r in range(4):
        brow[:, r, :] = btm[:, r * 512:(r + 1) * 512]
    brow[:, 4, 0:8] = btm[:, 2048:2056]
    b_out = np.asarray(inputs["b_out"], f)
    brow[:, 5, :] = b_out[:, 0:512]
    brow[:, 6, :] = b_out[:, 512:1024]
    sh["brow"] = brow
    sel = np.zeros((128, 7, 128), f)
    for r in range(7):
        sel[r, r, :] = 1.0
    sh["sel"] = sel.reshape(128, 7 * 128)
    sh["bfm"] = np.ascontiguousarray(b_in[:, fm_cols].reshape(DEPTH, 15, 128).transpose(0, 2, 1))
    sh["lnin"] = np.stack([np.broadcast_to(np.asarray(inputs["ln_in_g"], f), (128, 1024)),
                           np.broadcast_to(np.asarray(inputs["ln_in_b"], f), (128, 1024))]).copy()
    sh["lng"] = np.stack([np.stack([np.broadcast_to(np.asarray(inputs["ln_g"], f)[l], (128, 1024)),
                                    np.broadcast_to(np.asarray(inputs["ln_b"], f)[l], (128, 1024))]) for l in range(DEPTH)]).copy()
    sh["alng"] = np.stack([np.stack([np.broadcast_to(np.asarray(inputs["a_ln_g"], f)[l], (128, 256)),
                                     np.broadcast_to(np.asarray(inputs["a_ln_b"], f)[l], (128, 256))]) for l in range(DEPTH)]).copy()
    a_ws = np.asarray(inputs["a_ws"], f)
    sh["wsT"] = np.ascontiguousarray(a_ws.transpose(0, 3, 1, 2).reshape(DEPTH, 128, 512))
    ab = np.zeros((DEPTH, 128, 128), f)
    ab[:, 0:4, :] = np.asarray(inputs["a_bs"], f)
    sh["absT"] = ab
    b_rel = np.asarray(inputs["b_rel"], f)
    a = np.arange(128)[:, None]
    b = np.arange(128)[None, :]
    bB = np.zeros((DEPTH, 2, 128, 4, 128), f)
    for m in range(2):
        idx = np.clip(128 * m + b - a, -128, 128) + 128
        for l in range(DEPTH):
            bB[l, m] = b_rel[l][:, idx].transpose(1, 0, 2)
    sh["biasB"] = bB.reshape(DEPTH, 2, 128, 512)
    sh["cB"] = np.ascontiguousarray(np.broadcast_to(b_rel[:, None, :, 256, None], (DEPTH, 128, 4, 128)).reshape(DEPTH, 128, 512))
    t5 = np.asarray(inputs["t5_table"], f)
    bC = np.zeros((2, 128, 4, 128), f)
    for d in range(2):
        relkq = -128 * d + a - b
        bi = _t5_bucket_idx(relkq)
        bC[d] = t5[bi].transpose(0, 2, 1)
    sh["biasC"] = bC.reshape(2, 128, 512)
    sh["cC"] = np.ascontiguousarray(np.broadcast_to(t5[15][None, :, None], (128, 4, 128)).reshape(128, 512))
    mB = np.zeros((2, 128, 4, 128), f)
    mB[0][64:128, :, 0:64] = NEG
    mB[1][0:64, :, 64:128] = NEG
    sh["maskB"] = mB.reshape(2, 128, 512)
    E = np.zeros((128, 256), f)
    for g in range(4):
        E[g, g * 64:(g + 1) * 64] = 1.0
    sh["Emat"] = E
    sh["ident4"] = np.tile(np.eye(128, dtype=f), (1, 4))
    sh["pow2"] = np.broadcast_to((2.0 ** -np.arange(NBIS + 2)).astype(f)[None, :], (128, NBIS + 2)).copy()
    return sh


_NC_CACHE = {}


def kernel(**inputs):
    sh = prepare(inputs)
    x = np.asarray(inputs["x"], np.float32)
    mem = np.asarray(inputs["mem"], np.float32)
    if "nc" not in _NC_CACHE:
        _NC_CACHE["nc"] = build_program()
    nc = _NC_CACHE["nc"]
    in_maps = []
    for c in range(BATCH):
        m = dict(sh)
        m["x"] = np.ascontiguousarray(x[c])
        m["mem"] = np.ascontiguousarray(mem[c])
        in_maps.append(m)
    res = run_bass_kernel_spmd(nc, in_maps, core_ids=list(range(BATCH)))
    return np.stack([np.asarray(r["out"]) for r in res.results]).astype(np.float32)
```
